# Optimizing a Trainium2 kernel written in Bass

```python
import math
import jax, jax.numpy as jnp
from jax import lax
import numpy as np

D_MODEL = 1024
BATCH = 8
SEQ = 4096
DEPTH = 2

HEAD_DIM = 64
BLOCK = 128
EPS = 1e-6
NEG_INF = -1e30
SWA_Q_HEADS = 8
SWA_KV_HEADS = 2
SWA_WINDOW = 128
DIL_PATTERNS = ((128, 1), (512, 4), (2048, 16))
N_DIL = 3
DIL_HEADS = 4
SSM_GROUP = 16
SSM_GROUPS = 32
SSM_WIDTH = SSM_GROUP * SSM_GROUPS
SSM_STATE = 64
DT_MIN = 1e-3
DT_MAX = 1e-1
N_BRANCH = 3
FFN_DIM = 2816
CONV_WIDTH = 3

A_Q = SWA_Q_HEADS * HEAD_DIM
A_KV = SWA_KV_HEADS * HEAD_DIM
B_Q = N_DIL * DIL_HEADS * HEAD_DIM
B_KV = DIL_HEADS * HEAD_DIM
GATE_W = N_BRANCH * D_MODEL
IN_SPLITS = (A_Q, A_KV, A_KV, B_Q, B_KV, B_KV, SSM_WIDTH, GATE_W)
IN_WIDTH = A_Q + 2 * A_KV + B_Q + 2 * B_KV + SSM_WIDTH + GATE_W

kernel_name = "hybrid_swa_dilated_s5_gated_block"


def rmsnorm(x, g):
    xf = x.astype(jnp.float32)
    y = xf * lax.rsqrt(jnp.mean(xf * xf, axis=-1, keepdims=True) + EPS)
    return (y * g.astype(jnp.float32)).astype(x.dtype)


def banded_attention(q, k, v, max_offset, sink=None):
    n, L, g, r, dh = q.shape
    nb = -(-L // BLOCK)
    pad = nb * BLOCK - L
    q = jnp.pad(q, ((0, 0), (0, pad), (0, 0), (0, 0), (0, 0)))
    kv_pad = ((0, 0), (BLOCK, pad), (0, 0), (0, 0))
    k = jnp.pad(k, kv_pad).reshape(n, nb + 1, BLOCK, g, dh)
    v = jnp.pad(v, kv_pad).reshape(n, nb + 1, BLOCK, g, dh)
    kk = jnp.concatenate([k[:, :-1], k[:, 1:]], axis=2)
    vv = jnp.concatenate([v[:, :-1], v[:, 1:]], axis=2)
    qb = q.reshape(n, nb, BLOCK, g, r, dh)
    s = jnp.einsum("nbqgrd,nbkgd->nbgrqk", qb, kk).astype(jnp.float32) * (dh ** -0.5)
    qpos = BLOCK + jnp.arange(BLOCK)[:, None]
    kpos = jnp.arange(2 * BLOCK)[None, :]
    off = qpos - kpos
    band = (off >= 0) & (off <= max_offset)
    has_prev = (jnp.arange(nb) > 0)[:, None, None] | (kpos >= BLOCK)[None]
    mask = band[None] & has_prev
    s = jnp.where(mask[None, :, None, None], s, NEG_INF)
    m = jnp.max(s, axis=-1, keepdims=True)
    if sink is not None:
        sk = sink.astype(jnp.float32)[None, None, :, :, None, None]
        m = jnp.maximum(m, sk)
    p = jnp.exp(s - m)
    l = jnp.sum(p, axis=-1, keepdims=True)
    if sink is not None:
        l = l + jnp.exp(sk - m)
    o = jnp.einsum("nbgrqk,nbkgd->nbgrqd", p.astype(vv.dtype), vv).astype(jnp.float32) / l
    o = jnp.transpose(o, (0, 1, 4, 2, 3, 5)).reshape(n, nb * BLOCK, g, r, dh)[:, :L]
    lse = jnp.transpose((m + jnp.log(l))[..., 0], (0, 1, 4, 2, 3)).reshape(n, nb * BLOCK, g, r)[:, :L]
    return o.astype(q.dtype), lse


def to_sub(x, dil):
    b, s = x.shape[:2]
    rest = x.shape[2:]
    x = x.reshape((b, s // dil, dil) + rest)
    return jnp.moveaxis(x, 2, 1).reshape((b * dil, s // dil) + rest)


def from_sub(x, b, dil):
    n, L = x.shape[:2]
    rest = x.shape[2:]
    x = x.reshape((b, dil, L) + rest)
    return jnp.moveaxis(x, 1, 2).reshape((b, L * dil) + rest)


def dilated_attention(q, k, v):
    bsz, s = q.shape[:2]
    outs, lses = [], []
    for gi, (window, dil) in enumerate(DIL_PATTERNS):
        o, lse = banded_attention(to_sub(q[:, :, gi], dil)[:, :, :, None], to_sub(k, dil), to_sub(v, dil),
                                  window // dil)
        outs.append(from_sub(o[:, :, :, 0], bsz, dil))
        lses.append(from_sub(lse[..., 0], bsz, dil))
    wts = jax.nn.softmax(jnp.stack(lses), axis=0)
    y = jnp.sum(wts[..., None] * jnp.stack(outs).astype(jnp.float32), axis=0)
    return y.reshape(bsz, s, B_KV).astype(q.dtype)


def s5_mixer(u, lam_re, lam_im, log_dt, b_re, b_im, c_re, c_im, d_skip, w_glu, b_glu):
    f32 = jnp.float32
    bsz, s, _ = u.shape
    uf = u.astype(f32).reshape(bsz, s, SSM_GROUPS, SSM_GROUP)
    lr, li = lam_re.astype(f32), lam_im.astype(f32)
    dt = jnp.exp(log_dt.astype(f32))[:, None]
    mag = jnp.exp(lr * dt)
    ab_re, ab_im = mag * jnp.cos(li * dt), mag * jnp.sin(li * dt)
    nr, ni = ab_re - 1.0, ab_im
    den = lr * lr + li * li
    f_re = (nr * lr + ni * li) / den
    f_im = (ni * lr - nr * li) / den
    br, bi = b_re.astype(f32), b_im.astype(f32)
    bb_re = f_re[..., None] * br - f_im[..., None] * bi
    bb_im = f_re[..., None] * bi + f_im[..., None] * br
    bu_re = jnp.einsum("bsgh,gph->bsgp", uf, bb_re)
    bu_im = jnp.einsum("bsgh,gph->bsgp", uf, bb_im)
    a_re = jnp.broadcast_to(ab_re, bu_re.shape)
    a_im = jnp.broadcast_to(ab_im, bu_im.shape)

    def combine(e1, e2):
        a1r, a1i, b1r, b1i = e1
        a2r, a2i, b2r, b2i = e2
        return (a2r * a1r - a2i * a1i, a2r * a1i + a2i * a1r,
                a2r * b1r - a2i * b1i + b2r, a2r * b1i + a2i * b1r + b2i)

    _, _, xr, xi = lax.associative_scan(combine, (a_re, a_im, bu_re, bu_im), axis=1)
    y = (jnp.einsum("bsgp,ghp->bsgh", xr, c_re.astype(f32))
         - jnp.einsum("bsgp,ghp->bsgh", xi, c_im.astype(f32))
         + d_skip.astype(f32).reshape(SSM_GROUPS, SSM_GROUP) * uf)
    z = jax.nn.gelu(y.reshape(bsz, s, SSM_WIDTH))
    z = z * jax.nn.sigmoid(z @ w_glu.astype(f32) + b_glu.astype(f32))
    return z.astype(u.dtype)


def hybrid_mixer(h, w_in, attn_sinks, lam_re, lam_im, log_dt, b_re, b_im, c_re, c_im, d_skip, w_glu, b_glu,
                 w_branch_a, w_branch_b, w_branch_c, w_out):
    bsz, s, _ = h.shape
    proj = h @ w_in
    cuts = [int(c) for c in np.cumsum(IN_SPLITS)[:-1]]
    qa, ka, va, qd, kd, vd, u, g = jnp.split(proj, cuts, axis=-1)
    rep = SWA_Q_HEADS // SWA_KV_HEADS
    ya, _ = banded_attention(qa.reshape(bsz, s, SWA_KV_HEADS, rep, HEAD_DIM),
                             ka.reshape(bsz, s, SWA_KV_HEADS, HEAD_DIM),
                             va.reshape(bsz, s, SWA_KV_HEADS, HEAD_DIM),
                             SWA_WINDOW - 1, attn_sinks.reshape(SWA_KV_HEADS, rep))
    ya = ya.reshape(bsz, s, A_Q)
    yb = dilated_attention(qd.reshape(bsz, s, N_DIL, DIL_HEADS, HEAD_DIM),
                           kd.reshape(bsz, s, DIL_HEADS, HEAD_DIM),
                           vd.reshape(bsz, s, DIL_HEADS, HEAD_DIM))
    yc = s5_mixer(u, lam_re, lam_im, log_dt, b_re, b_im, c_re, c_im, d_skip, w_glu, b_glu)
    gates = jax.nn.sigmoid(g.astype(jnp.float32)).reshape(bsz, s, N_BRANCH, D_MODEL)
    merged = (gates[:, :, 0] * (ya @ w_branch_a).astype(jnp.float32)
              + gates[:, :, 1] * (yb @ w_branch_b).astype(jnp.float32)
              + gates[:, :, 2] * (yc @ w_branch_c).astype(jnp.float32))
    return merged.astype(h.dtype) @ w_out


def conv_ffn(h, w_up, conv_w, conv_b, w_down):
    up = h @ w_up
    up = lax.conv_general_dilated(up, conv_w[:, None, :], window_strides=(1,),
                                  padding=[(CONV_WIDTH - 1, 0)],
                                  dimension_numbers=("NWC", "WIO", "NWC"),
                                  feature_group_count=2 * FFN_DIM) + conv_b
    gate, val = jnp.split(up, 2, axis=-1)
    return (jax.nn.silu(gate) * val) @ w_down


def setup_inputs(seed: int = 0) -> dict:
    key = jax.random.key(seed)
    ks = jax.random.split(key, 26)
    f32 = jnp.float32
    L = DEPTH

    def nrm(k, shape, scale):
        return jax.random.normal(k, shape, f32) * scale

    lam_im = jnp.broadcast_to(jnp.pi * jnp.arange(SSM_STATE, dtype=f32), (L, SSM_GROUPS, SSM_STATE))
    return {
        "x": nrm(ks[0], (BATCH, SEQ, D_MODEL), 1.0),
        "norm_mix": 1.0 + nrm(ks[1], (L, D_MODEL), 0.02),
        "w_in": nrm(ks[2], (L, D_MODEL, IN_WIDTH), D_MODEL ** -0.5),
        "attn_sinks": nrm(ks[3], (L, SWA_Q_HEADS), 0.5),
        "ssm_lambda_re": -0.5 * jnp.exp(nrm(ks[4], (L, SSM_GROUPS, SSM_STATE), 0.05)),
        "ssm_lambda_im": lam_im + nrm(ks[5], (L, SSM_GROUPS, SSM_STATE), 0.01),
        "ssm_log_dt": jax.random.uniform(ks[6], (L, SSM_GROUPS), f32, math.log(DT_MIN), math.log(DT_MAX)),
        "ssm_b_re": nrm(ks[7], (L, SSM_GROUPS, SSM_STATE, SSM_GROUP), (2 * SSM_GROUP) ** -0.5),
        "ssm_b_im": nrm(ks[8], (L, SSM_GROUPS, SSM_STATE, SSM_GROUP), (2 * SSM_GROUP) ** -0.5),
        "ssm_c_re": nrm(ks[9], (L, SSM_GROUPS, SSM_GROUP, SSM_STATE), (2 * SSM_STATE) ** -0.5),
        "ssm_c_im": nrm(ks[10], (L, SSM_GROUPS, SSM_GROUP, SSM_STATE), (2 * SSM_STATE) ** -0.5),
        "ssm_d": nrm(ks[11], (L, SSM_WIDTH), 1.0),
        "w_glu": nrm(ks[12], (L, SSM_WIDTH, SSM_WIDTH), SSM_WIDTH ** -0.5),
        "b_glu": nrm(ks[13], (L, SSM_WIDTH), 0.02),
        "w_branch_a": nrm(ks[14], (L, A_Q, D_MODEL), A_Q ** -0.5),
        "w_branch_b": nrm(ks[15], (L, B_KV, D_MODEL), B_KV ** -0.5),
        "w_branch_c": nrm(ks[16], (L, SSM_WIDTH, D_MODEL), SSM_WIDTH ** -0.5),
        "w_out": nrm(ks[17], (L, D_MODEL, D_MODEL), D_MODEL ** -0.5),
        "norm_ffn": 1.0 + nrm(ks[18], (L, D_MODEL), 0.02),
        "w_up": nrm(ks[19], (L, D_MODEL, 2 * FFN_DIM), D_MODEL ** -0.5),
        "conv_w": nrm(ks[20], (L, CONV_WIDTH, 2 * FFN_DIM), CONV_WIDTH ** -0.5),
        "conv_b": nrm(ks[21], (L, 2 * FFN_DIM), 0.02),
        "w_down": nrm(ks[22], (L, FFN_DIM, D_MODEL), FFN_DIM ** -0.5),
        "norm_final": 1.0 + nrm(ks[23], (D_MODEL,), 0.02),
    }


def reference(x, norm_mix, w_in, attn_sinks, ssm_lambda_re, ssm_lambda_im, ssm_log_dt, ssm_b_re, ssm_b_im,
              ssm_c_re, ssm_c_im, ssm_d, w_glu, b_glu, w_branch_a, w_branch_b, w_branch_c, w_out,
              norm_ffn, w_up, conv_w, conv_b, w_down, norm_final):
    for l in range(DEPTH):
        h = rmsnorm(x, norm_mix[l])
        x = x + hybrid_mixer(h, w_in[l], attn_sinks[l], ssm_lambda_re[l], ssm_lambda_im[l], ssm_log_dt[l],
                             ssm_b_re[l], ssm_b_im[l], ssm_c_re[l], ssm_c_im[l], ssm_d[l], w_glu[l], b_glu[l],
                             w_branch_a[l], w_branch_b[l], w_branch_c[l], w_out[l]).astype(x.dtype)
        h = rmsnorm(x, norm_ffn[l])
        x = x + conv_ffn(h, w_up[l], conv_w[l], conv_b[l], w_down[l]).astype(x.dtype)
    return rmsnorm(x, norm_final)
```

```python
import numpy as np
from contextlib import ExitStack
import concourse.bass as bass
import concourse.mybir as mybir
from concourse.bass_utils import run_bass_kernel_spmd

F32 = mybir.dt.float32
BF16 = mybir.dt.bfloat16
AF = mybir.ActivationFunctionType
ALU = mybir.AluOpType

D = 1024
SEQ = 4096
DEPTH = 2
NCORES = 8
INW = 5632
FFN = 2816
EPS = 1e-6
TT = 512
NTT = SEQ // TT
KC = D // 128

DEBUG = {}
STOP_AFTER = None


class Buf:
    __slots__ = ("t", "w", "r", "name")

    def __init__(self, t, name=""):
        self.t = t
        self.w = None
        self.r = {}
        self.name = name

    def __getitem__(self, k):
        return self.t[k]


class Eng:
    def __init__(self, name, eng, sem):
        self.name = name
        self.eng = eng
        self.sem = sem
        self.known = {}


class DmaQ:
    def __init__(self, S, E, k, name):
        self.S = S
        self.E = E
        self.sems = [S.new_sem(f"dq_{name}{i}") for i in range(k)]
        self.n = 0


class Sched:
    def __init__(self, nc):
        self.nc = nc
        self.sems = []
        self.issued = []
        self.pe = Eng("pe", nc.tensor, self.new_sem("c_pe"))
        self.act = Eng("act", nc.scalar, self.new_sem("c_act"))
        self.dve = Eng("dve", nc.vector, self.new_sem("c_dve"))
        self.pool = Eng("pool", nc.gpsimd, self.new_sem("c_pool"))
        self.sp = Eng("sp", nc.sync, None)
        self.engs = [self.pe, self.act, self.dve, self.pool, self.sp]
        self.q_sp = DmaQ(self, self.sp, 8, "sp")
        self.q_pool = DmaQ(self, self.pool, 8, "pool")
        self.q_act = DmaQ(self, self.act, 6, "act")
        self.nops = 0

    def new_sem(self, name):
        self.sems.append(self.nc.alloc_semaphore(name))
        self.issued.append(0)
        return len(self.sems) - 1

    def wait(self, E, s, v):
        if v > 0 and E.known.get(s, 0) < v:
            E.eng.wait_ge(self.sems[s], v)
            E.known[s] = v

    def _sync(self, E, reads, writes, is_dma):
        need = {}

        def add(ev):
            if ev is not None and ev[1] > need.get(ev[0], 0):
                need[ev[0]] = ev[1]

        for b in reads:
            add(b.w)
        for b in writes:
            if b.w is not None and (is_dma or b.w[0] != E.sem):
                add(b.w)
            for s, v in b.r.items():
                if is_dma or s != E.sem:
                    add((s, v))
        for s, v in need.items():
            self.wait(E, s, v)

    def _mark(self, ev, reads, writes):
        s, v = ev
        for b in reads:
            if b.r.get(s, 0) < v:
                b.r[s] = v
        for b in writes:
            b.w = ev
            b.r = {}

    def op(self, E, reads, writes, emit):
        self._sync(E, reads, writes, False)
        ins = emit()
        self.issued[E.sem] += 1
        ins.then_inc(self.sems[E.sem], 1)
        self._mark((E.sem, self.issued[E.sem]), reads, writes)
        self.nops += 1

    def mm(self, reads, writes, emits):
        E = self.pe
        self._sync(E, reads, writes, False)
        ins = None
        for e in emits:
            ins = e()
            self.nops += 1
        self.issued[E.sem] += 1
        ins.then_inc(self.sems[E.sem], 1)
        self._mark((E.sem, self.issued[E.sem]), reads, writes)

    def dma(self, Q, reads, writes, emit):
        E = Q.E
        s = Q.sems[Q.n % len(Q.sems)]
        Q.n += 1
        self.wait(E, s, self.issued[s])
        self._sync(E, reads, writes, True)
        ins = emit(E.eng)
        self.issued[s] += 16
        ins.then_inc(self.sems[s], 16)
        self._mark((s, self.issued[s]), reads, writes)
        self.nops += 1

    def barrier(self):
        for E in self.engs:
            for s in range(len(self.sems)):
                self.wait(E, s, self.issued[s])


class Ctx:
    pass


def tt_sl(tt):
    return slice(tt * TT, (tt + 1) * TT)


def emit_rmsnorm_tile(K, es_bufs, xt, gain_ap_fn, out_fn):
    S, nc = K.S, K.nc
    sq, ps, rstd = es_bufs
    S.op(S.act, [xt], [sq], lambda: nc.scalar.activation(out=sq[:], in_=xt[:], func=AF.Square))
    S.mm([sq, K.ones_b], [ps],
         [(lambda c=c: nc.tensor.matmul(ps[:], K.ones_b[:], sq[:, c, :], start=(c == 0), stop=(c == KC - 1)))
          for c in range(KC)])
    S.op(S.act, [ps], [rstd], lambda: nc.scalar.activation(out=rstd[:], in_=ps[:], func=AF.Ln,
                                                             scale=1.0 / D, bias=K.eps_col[:, 0:1]))
    S.op(S.act, [rstd], [rstd], lambda: nc.scalar.activation(out=rstd[:], in_=rstd[:], func=AF.Exp, scale=-0.5))


def phase_p1(K, l, x_src):
    nc, S = K.nc, K.S
    with ExitStack() as es:
        sb, psum = mk_alloc(nc, es)

        hT = [sb(f"p1_hT{tt}", [128, KC, TT], BF16) for tt in range(NTT)]
        g = K.pp[l]
        xv = x_src.rearrange("(c p) t -> p c t", p=128)
        with ExitStack() as es2:
            sb2, psum2 = mk_alloc(nc, es2)
            xt = [sb2(f"p1_xt{i}", [128, KC, TT], F32) for i in range(4)]
            sqs = [sb2(f"p1_sq{i}", [128, KC, TT], BF16) for i in range(2)]
            rstd = [sb2(f"p1_rstd{i}", [128, TT], F32) for i in range(2)]
            ps_n = [psum2(f"p1_psn{i}") for i in range(2)]
            def load_x(tt):
                xb = xt[tt % 4]
                S.dma(S.q_sp, [], [xb], lambda e: e.dma_start(out=xb[:], in_=xv[:, :, tt_sl(tt)]))
            for tt in range(4):
                load_x(tt)
            for tt in range(NTT):
                xb = xt[tt % 4]
                rs = rstd[tt % 2]
                emit_rmsnorm_tile(K, (sqs[tt % 2], ps_n[tt % 2], rs), xb, None, None)
                for c in range(KC):
                    S.op(S.dve, [xb, rs, g], [hT[tt]],
                         lambda c=c, xb=xb, rs=rs, tt=tt: nc.vector.scalar_tensor_tensor(
                             out=hT[tt][:, c, :], in0=xb[:, c, :], scalar=g[:, PP_NMIX + c:PP_NMIX + c + 1],
                             in1=rs[:], op0=ALU.mult, op1=ALU.mult))
                if tt + 4 < NTT:
                    load_x(tt + 4)
            S.barrier()
        ps_m = [psum(f"p1_psm{i}") for i in range(4)]
        wbuf = [sb(f"p1_w{i}", [128, KC, 512], BF16) for i in range(2)]
        stage = [sb(f"p1_st{i}", [128, SEQ], BF16) for i in range(2)]
        vst = sb("p1_vst", [128, 32, 384], BF16)

        w_in = K.w_in[l]
        wv = w_in.rearrange("(c p) n -> p c n", p=128)
        wvb = wbuf[0]
        S.dma(S.q_pool, [], [wvb], lambda e: e.dma_start(out=wvb[:, :, 0:128], in_=wv[:, :, 640:768]))
        S.dma(S.q_pool, [], [wvb], lambda e: e.dma_start(out=wvb[:, :, 128:384], in_=wv[:, :, 1792:2048]))
        pi = 0
        for blk in range(32):
            tt, o = blk // 4, (blk % 4) * 128
            ps = ps_m[pi % 4]
            pi += 1
            S.mm([hT[tt], wvb], [ps],
                 [(lambda c=c, tt=tt, o=o, ps=ps: nc.tensor.matmul(ps[:, 0:384], hT[tt][:, c, o:o + 128], wvb[:, c, 0:384],
                                                                   start=(c == 0), stop=(c == KC - 1)))
                  for c in range(KC)])
            S.op(S.act, [ps], [vst], lambda blk=blk, ps=ps: nc.scalar.activation(out=vst[:, blk, :], in_=ps[:, 0:384], func=AF.Copy))
        S.dma(S.q_act, [vst], [], lambda e: e.dma_start(out=K.va_tok.rearrange("(b p) f -> p b f", p=128), in_=vst[:, :, 0:128]))
        S.dma(S.q_act, [vst], [], lambda e: e.dma_start(out=K.vb_tok.rearrange("(b p) f -> p b f", p=128), in_=vst[:, :, 128:384]))

        ci = 0
        for grp in range(INW // 512):
            wb = wbuf[(grp + 1) % 2]
            S.dma(S.q_pool, [], [wb], lambda e, wb=wb, grp=grp: e.dma_start(out=wb[:], in_=wv[:, :, grp * 512:(grp + 1) * 512]))
            for j in range(4):
                ch = grp * 4 + j
                if ch in (5, 14, 15):
                    continue
                st = stage[ci % 2]
                ci += 1
                func = AF.Sigmoid if ch >= 20 else AF.Copy
                for tt in range(NTT):
                    ps = ps_m[pi % 4]
                    pi += 1
                    S.mm([hT[tt], wb], [ps],
                         [(lambda c=c, tt=tt, j=j, ps=ps, wb=wb: nc.tensor.matmul(
                             ps[:], wb[:, c, j * 128:(j + 1) * 128], hT[tt][:, c, :], start=(c == 0), stop=(c == KC - 1)))
                          for c in range(KC)])
                    S.op(S.act, [ps], [st], lambda tt=tt, ps=ps, st=st, func=func: nc.scalar.activation(
                        out=st[:, tt_sl(tt)], in_=ps[:], func=func))
                S.dma(S.q_sp, [st], [], lambda e, st=st, ch=ch: e.dma_start(out=K.projT[ch * 128:(ch + 1) * 128, :], in_=st[:]))
    S.barrier()


NEG = -30000.0
PP_NMIX, PP_NFFN, PP_CW, PP_CB, PP_BGLU, PP_SSMD, PP_SINK, PP_LR, PP_LI, PP_LDT = 0, 8, 16, 148, 192, 196, 200, 208, 224, 240
PPW = 256


_UID = [0]


def mk_alloc(nc, es):
    _UID[0] += 1
    uid = _UID[0]

    def sb(name, shape, dt):
        return Buf(es.enter_context(nc.sbuf_tensor(f"{name}_u{uid}", shape, dt)), name)

    def psum(name):
        return Buf(es.enter_context(nc.psum_tensor(f"{name}_u{uid}", [128, 512], F32)), name)
    return sb, psum


def dump(K, slot, buf, ap, cols=512):
    if "dbg" not in DEBUG:
        return
    K.S.dma(K.S.q_pool, [buf], [], lambda e: e.dma_start(out=K.dbgbuf[slot, :, 0:cols], in_=ap))


def phase_attn_a(K, l):
    nc, S = K.nc, K.S
    pp = K.pp[l]
    with ExitStack() as es:
        sb, psum = mk_alloc(nc, es)
        KTs = [sb(f"a_KT{i}", [64, SEQ], BF16) for i in range(2)]
        QTs = [sb(f"a_QT{i}", [64, 4, SEQ], BF16) for i in range(2)]
        Vs = [sb(f"a_V{i}", [128, 32, 64], BF16) for i in range(2)]
        yst = sb("a_yst", [64, 4, SEQ], BF16)
        esink = sb("a_es", [128, 8], F32)
        sinkt = sb("a_sinkt", [64, 8, 128], F32)
        zeros = sb("a_zeros", [64, 128], F32)
        PTc = [sb(f"a_PTc{i}", [128, 512], BF16) for i in range(2)]
        PTp = [sb(f"a_PTp{i}", [128, 512], BF16) for i in range(2)]
        tmp = [sb(f"a_tmp{i}", [64, 512], F32) for i in range(2)]
        ps_c = [psum(f"a_psc{i}") for i in range(2)]
        ps_p = [psum(f"a_psp{i}") for i in range(2)]
        ps_o = [psum(f"a_pso{i}") for i in range(2)]
        ps_l = [psum(f"a_psl{i}") for i in range(2)]

        S.op(S.act, [pp], [esink], lambda: nc.scalar.activation(out=esink[:], in_=pp[:, PP_SINK:PP_SINK + 8], func=AF.Exp))
        S.op(S.dve, [], [zeros], lambda: nc.vector.memset(zeros[:], 0.0))
        for h8 in range(8):
            S.op(S.dve, [zeros, esink], [sinkt], lambda h8=h8: nc.vector.tensor_scalar(
                out=sinkt[:, h8, :], in0=zeros[:], scalar1=esink[0:64, h8:h8 + 1], scalar2=None, op0=ALU.add))

        it = 0
        pend = None
        for g in range(2):
            KT, QT, V = KTs[g], QTs[g], Vs[g]
            S.dma(S.q_sp, [], [KT], lambda e, g=g, KT=KT: e.dma_start(out=KT[:], in_=K.projT[512 + g * 64:512 + (g + 1) * 64, :]))
            S.dma(S.q_sp, [], [QT], lambda e, g=g, QT=QT: e.dma_start(
                out=QT[:], in_=K.projT[g * 256:(g + 1) * 256, :].rearrange("(h d) t -> d h t", d=64)))
            S.dma(S.q_sp, [], [V], lambda e, g=g, V=V: e.dma_start(
                out=V[:], in_=K.va_tok[:, g * 64:(g + 1) * 64].rearrange("(b p) f -> p b f", p=128)))
        for g in range(2):
            KT, QT, V = KTs[g], QTs[g], Vs[g]
            for b in range(32):
                i2 = it % 2
                it += 1
                q_ap = QT[:, :, b * 128:(b + 1) * 128]
                pc, pp_, po, pl = ps_c[i2], ps_p[i2], ps_o[i2], ps_l[i2]
                ptc, ptp, tm = PTc[i2], PTp[i2], tmp[i2]

                def s1(b=b, q_ap=q_ap, pc=pc, pp_=pp_, ptc=ptc, ptp=ptp, KT=KT, QT=QT):
                    S.mm([KT, QT], [pc], [
                        lambda: nc.tensor.matmul(pc[:].rearrange("p (h q) -> p h q", h=4), KT[:, b * 128:(b + 1) * 128], q_ap, start=True, stop=True)])
                    S.op(S.act, [pc], [ptc], lambda: nc.scalar.activation(out=ptc[:], in_=pc[:], func=AF.Exp, scale=0.125))
                    S.op(S.dve, [ptc, K.m01c4], [ptc], lambda: nc.vector.tensor_tensor(out=ptc[:], in0=ptc[:], in1=K.m01c4[:], op=ALU.mult))
                    if b > 0:
                        S.mm([KT, QT], [pp_], [
                            lambda: nc.tensor.matmul(pp_[:].rearrange("p (h q) -> p h q", h=4), KT[:, (b - 1) * 128:b * 128], q_ap, start=True, stop=True)])
                        S.op(S.act, [pp_], [ptp], lambda: nc.scalar.activation(out=ptp[:], in_=pp_[:], func=AF.Exp, scale=0.125))
                        S.op(S.dve, [ptp, K.m01pA4], [ptp], lambda: nc.vector.tensor_tensor(out=ptp[:], in0=ptp[:], in1=K.m01pA4[:], op=ALU.mult))

                def s2(b=b, g=g, po=po, pl=pl, ptc=ptc, ptp=ptp, tm=tm, V=V):
                    if b > 0:
                        S.mm([V, ptc, ptp], [po], [
                            lambda: nc.tensor.matmul(po[0:64, :], V[:, b - 1, :], ptp[:], start=True, stop=False),
                            lambda: nc.tensor.matmul(po[0:64, :], V[:, b, :], ptc[:], start=False, stop=True)])
                        S.mm([K.ones_b, ptc, ptp], [pl], [
                            lambda: nc.tensor.matmul(pl[0:64, :], K.ones_b[:, 0:64], ptp[:], start=True, stop=False),
                            lambda: nc.tensor.matmul(pl[0:64, :], K.ones_b[:, 0:64], ptc[:], start=False, stop=True)])
                    else:
                        S.mm([V, ptc], [po], [lambda: nc.tensor.matmul(po[0:64, :], V[:, b, :], ptc[:], start=True, stop=True)])
                        S.mm([K.ones_b, ptc], [pl], [lambda: nc.tensor.matmul(pl[0:64, :], K.ones_b[:, 0:64], ptc[:], start=True, stop=True)])
                    S.op(S.dve, [pl, sinkt], [tm], lambda: nc.vector.tensor_tensor(
                        out=tm[:], in0=pl[0:64, :], in1=sinkt[:, g * 4:(g + 1) * 4, :].rearrange("p h q -> p (h q)"), op=ALU.add))
                    S.op(S.act, [tm], [tm], lambda: nc.scalar.activation(out=tm[:], in_=tm[:], func=AF.Ln))
                    S.op(S.act, [tm], [tm], lambda: nc.scalar.activation(out=tm[:], in_=tm[:], func=AF.Exp, scale=-1.0))
                    S.op(S.dve, [po, tm], [yst], lambda: nc.vector.tensor_tensor(
                        out=yst[:, :, b * 128:(b + 1) * 128], in0=po[0:64, :].rearrange("p (h q) -> p h q", h=4),
                        in1=tm[:].rearrange("p (h q) -> p h q", h=4), op=ALU.mult))
                s1()
                if pend is not None:
                    pend()
                pend = s2
            pend()
            pend = None
            S.dma(S.q_sp, [yst], [], lambda e, g=g: e.dma_start(
                out=K.ymix[g * 256:(g + 1) * 256, :].rearrange("(h d) t -> d h t", d=64), in_=yst[:]))
    S.barrier()


def phase_attn_b(K, l):
    nc, S = K.nc, K.S
    dils = (1, 4, 16)
    with ExitStack() as es:
        sb, psum = mk_alloc(nc, es)
        KTs = [sb(f"b_KT{i}", [64, SEQ], BF16) for i in range(2)]
        QTs = [sb(f"b_QT{i}", [64, 3, SEQ], BF16) for i in range(2)]

        def load_kq(h):
            KT, QT = KTs[h % 2], QTs[h % 2]
            S.dma(S.q_sp, [], [KT], lambda e: e.dma_start(out=KT[:], in_=K.projT[1536 + h * 64:1536 + (h + 1) * 64, :]))
            S.dma(S.q_sp, [], [QT], lambda e: e.dma_start(
                out=QT[:], in_=K.projT[768:1536, :].rearrange("(gi hh d) t -> d gi hh t", gi=3, hh=4, d=64)[:, :, h, :]))
        Vd = [sb(f"b_V{gi}", [128, 32, 256], BF16) for gi in range(3)]
        Va = [sb(f"b_Va{gi}", [128, 32, 128], BF16) for gi in range(3)]
        acc = sb("b_acc", [128, SEQ], F32)
        rec = [sb(f"b_rec{i}", [64, 512], F32) for i in range(2)]
        yst = sb("b_yst", [64, SEQ], BF16)
        ident_f = sb("b_identf", [128, 128], F32)
        NB = 3
        PT = [sb(f"b_PT{i}", [128, 256], BF16) for i in range(NB)]
        ps_s = [psum(f"b_pss{i}") for i in range(NB)]
        ps_o = [psum(f"b_pso{i}") for i in range(NB)]
        ps_d = [psum(f"b_psd{i}") for i in range(2)]
        S.dma(S.q_sp, [], [ident_f], lambda e: e.dma_start(out=ident_f[:], in_=K.consts_d[:, 0:128]))
        for gi, dil in enumerate(dils):
            S.dma(S.q_sp, [], [Vd[gi]], lambda e, gi=gi, dil=dil: e.dma_start(
                out=Vd[gi][:].rearrange("j (b r) f -> j b r f", r=dil),
                in_=K.vb_tok.rearrange("(b j r) f -> j b r f", j=128, r=dil)))
            S.op(S.dve, [], [Va[gi]], lambda gi=gi: nc.vector.memset(Va[gi][:, :, 64:128], 1.0))
        it = 0
        pendq = []
        load_kq(0)
        for h in range(4):
            KT, QT = KTs[h % 2], QTs[h % 2]
            if h + 1 < 4:
                load_kq(h + 1)
            for gi in range(3):
                S.op(S.act, [Vd[gi]], [Va[gi]], lambda gi=gi, h=h: nc.scalar.activation(
                    out=Va[gi][:, :, 0:64], in_=Vd[gi][:, :, h * 64:(h + 1) * 64], func=AF.Copy))
            for gi, dil in enumerate(dils):
                nb = 32 // dil
                for r in range(dil):
                    for b in range(nb):
                        i3 = it % NB
                        it += 1
                        pss, pso, pt = ps_s[i3], ps_o[i3], PT[i3]

                        def tok(bb, r=r, dil=dil):
                            s0 = r + dil * 128 * bb
                            return slice(s0, s0 + dil * 127 + 1, dil)

                        def s1(b=b, gi=gi, tok=tok, pss=pss, pt=pt, KT=KT, QT=QT):
                            mms = [lambda: nc.tensor.matmul(pss[:, 0:128], KT[:, tok(b)], QT[:, gi, tok(b)], start=True, stop=True)]
                            if b > 0:
                                mms += [lambda: nc.tensor.matmul(pss[:, 128:256], KT[:, tok(b - 1)], QT[:, gi, tok(b)], start=True, stop=True)]
                            S.mm([KT, QT], [pss], mms)
                            w = 256 if b > 0 else 128
                            S.op(S.act, [pss], [pt], lambda: nc.scalar.activation(out=pt[:, 0:w], in_=pss[:, 0:w], func=AF.Exp, scale=0.125))
                            S.op(S.dve, [pt, K.m01], [pt], lambda: nc.vector.tensor_tensor(out=pt[:, 0:w], in0=pt[:, 0:w], in1=K.m01[:, 0:w], op=ALU.mult))

                        def s2(b=b, gi=gi, dil=dil, r=r, tok=tok, pso=pso, pt=pt):
                            vcur = Va[gi][:, b * dil + r, :]
                            if b > 0:
                                vprev = Va[gi][:, (b - 1) * dil + r, :]
                                mms = [
                                    lambda: nc.tensor.matmul(pso[:, 0:128], vprev, pt[:, 128:256], start=True, stop=False),
                                    lambda: nc.tensor.matmul(pso[:, 0:128], vcur, pt[:, 0:128], start=False, stop=True)]
                            else:
                                mms = [lambda: nc.tensor.matmul(pso[:, 0:128], vcur, pt[:, 0:128], start=True, stop=True)]
                            S.mm([Va[gi], pt], [pso], mms)
                            dst = acc[:, tok(b)]
                            if gi == 0:
                                S.op(S.dve, [pso], [acc], lambda: nc.vector.tensor_copy(out=dst, in_=pso[:, 0:128]))
                            else:
                                S.op(S.dve, [pso, acc], [acc], lambda: nc.vector.tensor_tensor(out=dst, in0=dst, in1=pso[:, 0:128], op=ALU.add))
                        s1()
                        pendq.append(s2)
                        if len(pendq) > 2:
                            pendq.pop(0)()
            while pendq:
                pendq.pop(0)()
            for c in range(SEQ // 512):
                cs_ = slice(c * 512, (c + 1) * 512)
                pd, rc = ps_d[c % 2], rec[c % 2]
                S.mm([ident_f, acc], [pd], [lambda pd=pd, cs_=cs_: nc.tensor.matmul(pd[0:64, :], ident_f[:, 64:128], acc[:, cs_], start=True, stop=True)])
                S.op(S.act, [pd], [rc], lambda pd=pd, rc=rc: nc.scalar.activation(out=rc[:], in_=pd[0:64, :], func=AF.Ln))
                S.op(S.act, [rc], [rc], lambda rc=rc: nc.scalar.activation(out=rc[:], in_=rc[:], func=AF.Exp, scale=-1.0))
                S.op(S.dve, [acc, rc], [yst], lambda rc=rc, cs_=cs_: nc.vector.tensor_tensor(out=yst[:, cs_], in0=acc[0:64, cs_], in1=rc[:], op=ALU.mult))
            S.dma(S.q_sp, [yst], [], lambda e, h=h: e.dma_start(out=K.ymix[512 + h * 64:512 + (h + 1) * 64, :], in_=yst[:]))
    S.barrier()


def phase_s5(K, l):
    nc, S = K.nc, K.S
    pp = K.pp[l]
    TWO_PI = 6.283185307179586
    I32 = mybir.dt.int32
    with ExitStack() as es:
        sb, psum = mk_alloc(nc, es)
        cs = sb("s_cs", [128, 16, TT], F32)
        sn = sb("s_sn", [128, 16, TT], F32)
        Bre = sb("s_Bre", [128, 16, 128], BF16)
        Bim = sb("s_Bim", [128, 16, 128], BF16)
        Cfr = sb("s_Cfr", [128, 16, 128], BF16)
        nCfi = sb("s_nCfi", [128, 16, 128], BF16)
        nCfr = sb("s_nCfr", [128, 16, 128], BF16)
        diagD = sb("s_diagD", [128, 4, 128], BF16)
        wglu = sb("s_wglu", [128, 4, 512], BF16)
        rho = sb("s_rho", [128, 16], F32)
        c9 = [sb(f"s_ck{k}", [128, 16], F32) for k in range(10)]
        s9 = [sb(f"s_sk{k}", [128, 16], F32) for k in range(10)]
        ns9 = sb("s_ns9", [128, 16], F32)
        fr = sb("s_fr", [128, 16], F32)
        fi = sb("s_fi", [128, 16], F32)
        nfi = sb("s_nfi", [128, 16], F32)
        nfr = sb("s_nfr", [128, 16], F32)
        init_r = [sb(f"s_ir{i}", [128, 16], F32) for i in range(2)]
        init_i = [sb(f"s_ii{i}", [128, 16], F32) for i in range(2)]

        def dv(reads, writes, f):
            S.op(S.dve, reads, writes, f)

        S.dma(S.q_pool, [], [Bre], lambda e: e.dma_start(out=Bre[:], in_=K.bmat[l, 0]))
        S.dma(S.q_pool, [], [Bim], lambda e: e.dma_start(out=Bim[:], in_=K.bmat[l, 1]))
        S.dma(S.q_pool, [], [wglu], lambda e: e.dma_start(out=wglu[:], in_=K.w_glu[l].rearrange("(kc p) m -> p kc m", p=128)))
        for kt in range(4):
            dv([K.cb16, pp], [diagD], lambda kt=kt: nc.vector.tensor_scalar(
                out=diagD[:, kt, :], in0=K.cb16[:, 0:128], scalar1=pp[:, PP_SSMD + kt:PP_SSMD + kt + 1], scalar2=None, op0=ALU.mult))

        with ExitStack() as es2:
            sb2, _ = mk_alloc(nc, es2)
            def c16(n):
                return sb2("s_p_" + n, [128, 16], F32)
            dt, lrdt, th, q, kf, r, abr, abi, t1, t2, den, nr = [c16(n) for n in
                                                                 ("dt", "lrdt", "th", "q", "kf", "r", "abr", "abi", "t1", "t2", "den", "nr")]
            ki = sb2("s_p_ki", [128, 16], I32)
            Cre_f = sb2("s_Cre_f", [128, 16, 128], F32)
            Cim_f = sb2("s_Cim_f", [128, 16, 128], F32)
            ctmp = sb2("s_ctmp", [128, 128], F32)
            S.dma(S.q_sp, [], [Cre_f], lambda e: e.dma_start(out=Cre_f[:], in_=K.cmat[l, 0]))
            S.dma(S.q_sp, [], [Cim_f], lambda e: e.dma_start(out=Cim_f[:], in_=K.cmat[l, 1]))
            LR = pp[:, PP_LR:PP_LR + 16]
            LI = pp[:, PP_LI:PP_LI + 16]
            S.op(S.act, [pp], [dt], lambda: nc.scalar.activation(out=dt[:], in_=pp[:, PP_LDT:PP_LDT + 16], func=AF.Exp))
            dv([pp, dt], [lrdt], lambda: nc.vector.tensor_tensor(out=lrdt[:], in0=LR, in1=dt[:], op=ALU.mult))
            dv([pp, dt], [th], lambda: nc.vector.tensor_tensor(out=th[:], in0=LI, in1=dt[:], op=ALU.mult))
            S.op(S.act, [lrdt], [rho], lambda: nc.scalar.activation(out=rho[:], in_=lrdt[:], func=AF.Exp))

            C1, C2 = 6.28125, 0.0019353071795864769
            y_, y2, sA, cA, sB, cB = [c16(n) for n in ("y", "y2", "sA", "cA", "sB", "cB")]
            dv([th], [q], lambda: nc.vector.tensor_scalar(out=q[:], in0=th[:], scalar1=1.0 / TWO_PI, scalar2=None, op0=ALU.mult))
            dv([q], [ki], lambda: nc.vector.tensor_copy(out=ki[:], in_=q[:]))
            dv([ki], [kf], lambda: nc.vector.tensor_copy(out=kf[:], in_=ki[:]))
            dv([kf, th], [r], lambda: nc.vector.scalar_tensor_tensor(out=r[:], in0=kf[:], scalar=-C1, in1=th[:], op0=ALU.mult, op1=ALU.add))
            dv([kf, r], [r], lambda: nc.vector.scalar_tensor_tensor(out=r[:], in0=kf[:], scalar=-C2, in1=r[:], op0=ALU.mult, op1=ALU.add))
            dv([r], [y_], lambda: nc.vector.tensor_scalar(out=y_[:], in0=r[:], scalar1=0.125, scalar2=None, op0=ALU.mult))
            dv([y_], [y2], lambda: nc.vector.tensor_tensor(out=y2[:], in0=y_[:], in1=y_[:], op=ALU.mult))

            def horner(dst, coefs, last_mul, last_add):
                dv([y2], [dst], lambda: nc.vector.tensor_scalar(out=dst[:], in0=y2[:], scalar1=coefs[0], scalar2=None, op0=ALU.mult))
                for cf in coefs[1:]:
                    dv([dst, y2], [dst], lambda cf=cf: nc.vector.scalar_tensor_tensor(out=dst[:], in0=dst[:], scalar=cf, in1=y2[:], op0=ALU.add, op1=ALU.mult))
                if last_mul is not None:
                    dv([dst, last_mul], [dst], lambda: nc.vector.scalar_tensor_tensor(out=dst[:], in0=dst[:], scalar=last_add, in1=last_mul[:], op0=ALU.add, op1=ALU.mult))
                else:
                    dv([dst], [dst], lambda: nc.vector.tensor_scalar(out=dst[:], in0=dst[:], scalar1=last_add, scalar2=None, op0=ALU.add))
            horner(sA, [1.0 / 362880, -1.0 / 5040, 1.0 / 120, -1.0 / 6], y_, 1.0)
            horner(cA, [-1.0 / 3628800, 1.0 / 40320, -1.0 / 720, 1.0 / 24, -0.5], None, 1.0)
            cur_s, cur_c, nxt_s, nxt_c = sA, cA, sB, cB
            for dbl in range(3):
                fin = (dbl == 2)
                ds_ = s9[0] if fin else nxt_s
                dc_ = c9[0] if fin else nxt_c
                dv([cur_s], [t1], lambda cur_s=cur_s: nc.vector.tensor_tensor(out=t1[:], in0=cur_s[:], in1=cur_s[:], op=ALU.mult))
                dv([cur_s, cur_c], [ds_], lambda cur_s=cur_s, cur_c=cur_c, ds_=ds_: nc.vector.scalar_tensor_tensor(out=ds_[:], in0=cur_s[:], scalar=2.0, in1=cur_c[:], op0=ALU.mult, op1=ALU.mult))
                dv([t1], [dc_], lambda dc_=dc_: nc.vector.tensor_scalar(out=dc_[:], in0=t1[:], scalar1=-2.0, scalar2=1.0, op0=ALU.mult, op1=ALU.add))
                cur_s, cur_c, nxt_s, nxt_c = ds_, dc_, cur_s, cur_c
            dv([rho, c9[0]], [abr], lambda: nc.vector.tensor_tensor(out=abr[:], in0=rho[:], in1=c9[0][:], op=ALU.mult))
            dv([rho, s9[0]], [abi], lambda: nc.vector.tensor_tensor(out=abi[:], in0=rho[:], in1=s9[0][:], op=ALU.mult))
            dv([abr], [nr], lambda: nc.vector.tensor_scalar(out=nr[:], in0=abr[:], scalar1=-1.0, scalar2=None, op0=ALU.add))
            dv([pp], [t1], lambda: nc.vector.tensor_tensor(out=t1[:], in0=LR, in1=LR, op=ALU.mult))
            dv([pp], [t2], lambda: nc.vector.tensor_tensor(out=t2[:], in0=LI, in1=LI, op=ALU.mult))
            dv([t1, t2], [den], lambda: nc.vector.tensor_tensor(out=den[:], in0=t1[:], in1=t2[:], op=ALU.add))
            dv([den], [den], lambda: nc.vector.reciprocal(out=den[:], in_=den[:]))
            dv([nr, pp], [t1], lambda: nc.vector.tensor_tensor(out=t1[:], in0=nr[:], in1=LR, op=ALU.mult))
            dv([abi, pp], [t2], lambda: nc.vector.tensor_tensor(out=t2[:], in0=abi[:], in1=LI, op=ALU.mult))
            dv([t1, t2], [t1], lambda: nc.vector.tensor_tensor(out=t1[:], in0=t1[:], in1=t2[:], op=ALU.add))
            dv([t1, den], [fr], lambda: nc.vector.tensor_tensor(out=fr[:], in0=t1[:], in1=den[:], op=ALU.mult))
            dv([abi, pp], [t1], lambda: nc.vector.tensor_tensor(out=t1[:], in0=abi[:], in1=LR, op=ALU.mult))
            dv([nr, pp], [t2], lambda: nc.vector.tensor_tensor(out=t2[:], in0=nr[:], in1=LI, op=ALU.mult))
            dv([t1, t2], [t1], lambda: nc.vector.tensor_tensor(out=t1[:], in0=t1[:], in1=t2[:], op=ALU.subtract))
            dv([t1, den], [fi], lambda: nc.vector.tensor_tensor(out=fi[:], in0=t1[:], in1=den[:], op=ALU.mult))
            dv([fi], [nfi], lambda: nc.vector.tensor_scalar(out=nfi[:], in0=fi[:], scalar1=-1.0, scalar2=None, op0=ALU.mult))
            dv([fr], [nfr], lambda: nc.vector.tensor_scalar(out=nfr[:], in0=fr[:], scalar1=-1.0, scalar2=None, op0=ALU.mult))
            for j in range(16):
                dv([Cre_f, fr], [ctmp], lambda j=j: nc.vector.tensor_scalar(out=ctmp[:], in0=Cre_f[:, j, :], scalar1=fr[:, j:j + 1], scalar2=None, op0=ALU.mult))
                dv([Cim_f, nfi, ctmp], [Cfr], lambda j=j: nc.vector.scalar_tensor_tensor(out=Cfr[:, j, :], in0=Cim_f[:, j, :], scalar=nfi[:, j:j + 1], in1=ctmp[:], op0=ALU.mult, op1=ALU.add))
                dv([Cre_f, nfi], [ctmp], lambda j=j: nc.vector.tensor_scalar(out=ctmp[:], in0=Cre_f[:, j, :], scalar1=nfi[:, j:j + 1], scalar2=None, op0=ALU.mult))
                dv([Cim_f, nfr, ctmp], [nCfi], lambda j=j: nc.vector.scalar_tensor_tensor(out=nCfi[:, j, :], in0=Cim_f[:, j, :], scalar=nfr[:, j:j + 1], in1=ctmp[:], op0=ALU.mult, op1=ALU.add))
                dv([Cre_f, nfr], [ctmp], lambda j=j: nc.vector.tensor_scalar(out=ctmp[:], in0=Cre_f[:, j, :], scalar1=nfr[:, j:j + 1], scalar2=None, op0=ALU.mult))
                dv([Cim_f, fi, ctmp], [nCfr], lambda j=j: nc.vector.scalar_tensor_tensor(out=nCfr[:, j, :], in0=Cim_f[:, j, :], scalar=fi[:, j:j + 1], in1=ctmp[:], op0=ALU.mult, op1=ALU.add))
            tmpb = sb2("s_tmpb", [128, 16, TT // 2], F32)
            dv([], [cs], lambda: nc.vector.memset(cs[:, :, 0:1], 1.0))
            dv([], [sn], lambda: nc.vector.memset(sn[:, :, 0:1], 0.0))
            for k in range(9):
                n = 1 << k
                ck, sk = c9[k], s9[k]
                ckb = ck[:, :].unsqueeze(2).to_broadcast([128, 16, n])
                skb = sk[:, :].unsqueeze(2).to_broadcast([128, 16, n])
                lo, hi = slice(0, n), slice(n, 2 * n)
                dv([cs, ck], [cs], lambda ckb=ckb, lo=lo, hi=hi: nc.vector.tensor_tensor(out=cs[:, :, hi], in0=cs[:, :, lo], in1=ckb, op=ALU.mult))
                dv([sn, sk], [tmpb], lambda skb=skb, lo=lo, n=n: nc.vector.tensor_tensor(out=tmpb[:, :, 0:n], in0=sn[:, :, lo], in1=skb, op=ALU.mult))
                dv([cs, tmpb], [cs], lambda hi=hi, n=n: nc.vector.tensor_tensor(out=cs[:, :, hi], in0=cs[:, :, hi], in1=tmpb[:, :, 0:n], op=ALU.subtract))
                dv([cs, sk], [tmpb], lambda skb=skb, lo=lo, n=n: nc.vector.tensor_tensor(out=tmpb[:, :, 0:n], in0=cs[:, :, lo], in1=skb, op=ALU.mult))
                dv([sn, ck], [sn], lambda ckb=ckb, lo=lo, hi=hi: nc.vector.tensor_tensor(out=sn[:, :, hi], in0=sn[:, :, lo], in1=ckb, op=ALU.mult))
                dv([sn, tmpb], [sn], lambda hi=hi, n=n: nc.vector.tensor_tensor(out=sn[:, :, hi], in0=sn[:, :, hi], in1=tmpb[:, :, 0:n], op=ALU.add))
                last = 2 * n - 1
                csl, snl = cs[:, :, last], sn[:, :, last]
                dv([cs, c9[0]], [t1], lambda csl=csl: nc.vector.tensor_tensor(out=t1[:], in0=csl, in1=c9[0][:], op=ALU.mult))
                dv([sn, s9[0]], [t2], lambda snl=snl: nc.vector.tensor_tensor(out=t2[:], in0=snl, in1=s9[0][:], op=ALU.mult))
                dv([t1, t2], [c9[k + 1]], lambda k=k: nc.vector.tensor_tensor(out=c9[k + 1][:], in0=t1[:], in1=t2[:], op=ALU.subtract))
                dv([sn, c9[0]], [t1], lambda snl=snl: nc.vector.tensor_tensor(out=t1[:], in0=snl, in1=c9[0][:], op=ALU.mult))
                dv([cs, s9[0]], [t2], lambda csl=csl: nc.vector.tensor_tensor(out=t2[:], in0=csl, in1=s9[0][:], op=ALU.mult))
                dv([t1, t2], [s9[k + 1]], lambda k=k: nc.vector.tensor_tensor(out=s9[k + 1][:], in0=t1[:], in1=t2[:], op=ALU.add))
            dv([s9[9]], [ns9], lambda: nc.vector.tensor_scalar(out=ns9[:], in0=s9[9][:], scalar1=-1.0, scalar2=None, op0=ALU.mult))
        cL, sL = c9[9], s9[9]
        S.barrier()
        dump(K, 0, cs, cs[:, 0, :])
        dump(K, 1, sn, sn[:, 0, :])
        for i_, b_ in enumerate((rho, fr, fi, c9[0], s9[0], c9[9], s9[9])):
            dump(K, 9 + i_, b_, b_[:], 16)

        wts = [[sb(f"s_wt{i}_{k}", [128, TT], F32) for k in range(4)] for i in range(2)]
        xq = [[sb(f"s_xq{i}_{k}", [128, TT], BF16) for k in range(4)] for i in range(3)]
        ident_f = sb("s_identf", [128, 128], F32)
        S.dma(S.q_sp, [], [ident_f], lambda e: e.dma_start(out=ident_f[:], in_=K.consts_d[:, 0:128]))
        zr = sb("s_zr", [128, TT], F32)
        zi = sb("s_zi", [128, TT], F32)
        zT = [sb(f"s_zT{i}", [128, 4, TT], BF16) for i in range(2)]
        sig = [sb(f"s_sig{i}", [128, TT], BF16) for i in range(4)]
        ycst = [sb(f"s_ycst{i}", [128, 4, TT], BF16) for i in range(2)]
        ps_br = [psum(f"s_psbr{i}") for i in range(1)]
        ps_bi = [psum(f"s_psbi{i}") for i in range(1)]
        btrs = [psum(f"s_psbtr{i}") for i in range(2)]
        btis = [psum(f"s_psbti{i}") for i in range(2)]
        ps_y = [psum(f"s_psy{i}") for i in range(1)]
        ps_g = [psum(f"s_psg{i}") for i in range(1)]
        dv([], [init_r[0]], lambda: nc.vector.memset(init_r[0][:], 0.0))
        dv([], [init_i[0]], lambda: nc.vector.memset(init_i[0][:], 0.0))
        uview = K.projT[2048:2560, :].rearrange("(kt p) t -> p kt t", p=128)
        uT = [sb(f"s_uT{i}", [128, 4, TT], BF16) for i in range(2)]
        NIT = NTT * 16
        ctmpA = [sb(f"s_ctA{i}", [128, 1], F32) for i in range(2)]
        ctmpB = [sb(f"s_ctB{i}", [128, 1], F32) for i in range(2)]
        st = {"iy": 0, "ig": 0}

        def load_u(tt):
            u = uT[tt % 2]
            S.dma(S.q_sp, [], [u], lambda e: e.dma_start(out=u[:], in_=uview[:, :, tt_sl(tt)]))

        def stageA(n):
            tt, j = divmod(n, 16)
            kt = j // 4
            u = uT[tt % 2]
            if j == 3 and tt + 1 < NTT:
                load_u(tt + 1)
            pbr, pbi = ps_br[0], ps_bi[0]
            w1, w2, w3, w4 = wts[n % 2]
            btr, bti = btrs[n % 2], btis[n % 2]
            S.mm([Bre, u], [pbr], [lambda: nc.tensor.matmul(pbr[:], Bre[:, j, :], u[:, kt, :], start=True, stop=True)])
            S.mm([Bim, u], [pbi], [lambda: nc.tensor.matmul(pbi[:], Bim[:, j, :], u[:, kt, :], start=True, stop=True)])

        def stageB(n):
            tt, j = divmod(n, 16)
            pbr, pbi = ps_br[0], ps_bi[0]
            w1, w2, w3, w4 = wts[n % 2]
            btr, bti = btrs[n % 2], btis[n % 2]
            csj, snj = cs[:, j, :], sn[:, j, :]
            dv([cs, pbr], [w1], lambda: nc.vector.tensor_tensor(out=w1[:], in0=csj, in1=pbr[:], op=ALU.mult))
            dv([sn, pbi], [w2], lambda: nc.vector.tensor_tensor(out=w2[:], in0=snj, in1=pbi[:], op=ALU.mult))
            dv([cs, pbi], [w3], lambda: nc.vector.tensor_tensor(out=w3[:], in0=csj, in1=pbi[:], op=ALU.mult))
            dv([sn, pbr], [w4], lambda: nc.vector.scalar_tensor_tensor(out=w4[:], in0=pbr[:], scalar=-1.0, in1=snj, op0=ALU.mult, op1=ALU.mult))
            S.mm([ident_f, w1, w2], [btr], [
                lambda: nc.tensor.matmul(btr[:], ident_f[:], w1[:], start=True, stop=False),
                lambda: nc.tensor.matmul(btr[:], ident_f[:], w2[:], start=False, stop=True)])
            S.mm([ident_f, w3, w4], [bti], [
                lambda: nc.tensor.matmul(bti[:], ident_f[:], w3[:], start=True, stop=False),
                lambda: nc.tensor.matmul(bti[:], ident_f[:], w4[:], start=False, stop=True)])

        def stageCd(n):
            tt, j = divmod(n, 16)
            kt = j // 4
            u = uT[tt % 2]
            btr, bti = btrs[n % 2], btis[n % 2]
            ir, ii = init_r[tt % 2], init_i[tt % 2]
            nir, nii = init_r[(tt + 1) % 2], init_i[(tt + 1) % 2]
            zt = zT[tt % 2]
            csj, snj = cs[:, j, :], sn[:, j, :]
            rb = rho[:, j:j + 1].to_broadcast([128, TT])
            dv([rho, btr, ir], [zr], lambda: nc.vector.tensor_tensor_scan(
                out=zr[:], data0=rb, data1=btr[:], initial=ir[:, j:j + 1], op0=ALU.mult, op1=ALU.add))
            dv([rho, bti, ii], [zi], lambda: nc.vector.tensor_tensor_scan(
                out=zi[:], data0=rb, data1=bti[:], initial=ii[:, j:j + 1], op0=ALU.mult, op1=ALU.add))
            if tt < NTT - 1:
                ca, cb_ = ctmpA[n % 2], ctmpB[n % 2]
                S.op(S.act, [zr, cL], [ca], lambda: nc.scalar.activation(out=ca[:], in_=zr[:, TT - 1:TT], func=AF.Identity, scale=cL[:, j:j + 1]))
                S.op(S.act, [zi, ns9, ca], [nir], lambda: nc.scalar.activation(out=nir[:, j:j + 1], in_=zi[:, TT - 1:TT], func=AF.Identity, scale=ns9[:, j:j + 1], bias=ca[:, 0:1]))
                S.op(S.act, [zr, sL], [cb_], lambda: nc.scalar.activation(out=cb_[:], in_=zr[:, TT - 1:TT], func=AF.Identity, scale=sL[:, j:j + 1]))
                S.op(S.act, [zi, cL, cb_], [nii], lambda: nc.scalar.activation(out=nii[:, j:j + 1], in_=zi[:, TT - 1:TT], func=AF.Identity, scale=cL[:, j:j + 1], bias=cb_[:, 0:1]))
            x1, x2, x3, x4 = xq[n % 3]
            dv([cs, zr], [x1], lambda: nc.vector.tensor_tensor(out=x1[:], in0=csj, in1=zr[:], op=ALU.mult))
            dv([sn, zi], [x2], lambda: nc.vector.tensor_tensor(out=x2[:], in0=snj, in1=zi[:], op=ALU.mult))
            dv([sn, zr], [x3], lambda: nc.vector.tensor_tensor(out=x3[:], in0=snj, in1=zr[:], op=ALU.mult))
            dv([cs, zi], [x4], lambda: nc.vector.tensor_tensor(out=x4[:], in0=csj, in1=zi[:], op=ALU.mult))

        def stageCp(n):
            tt, j = divmod(n, 16)
            kt = j // 4
            u = uT[tt % 2]
            zt = zT[tt % 2]
            x1, x2, x3, x4 = xq[n % 3]
            py = ps_y[0]
            mms = []
            if j % 4 == 0:
                mms.append(lambda: nc.tensor.matmul(py[:], diagD[:, kt, :], u[:, kt, :], start=True, stop=False))
            mms.append(lambda: nc.tensor.matmul(py[:], Cfr[:, j, :], x1[:], start=False, stop=False))
            mms.append(lambda: nc.tensor.matmul(py[:], nCfr[:, j, :], x2[:], start=False, stop=False))
            mms.append(lambda: nc.tensor.matmul(py[:], nCfi[:, j, :], x3[:], start=False, stop=False))
            mms.append(lambda: nc.tensor.matmul(py[:], nCfi[:, j, :], x4[:], start=False, stop=(j % 4 == 3)))
            S.mm([diagD, u, Cfr, nCfr, nCfi, x1, x2, x3, x4], [py], mms)
            if j % 4 == 3:
                S.op(S.act, [py], [zt], lambda: nc.scalar.activation(out=zt[:, kt, :], in_=py[:], func=AF.Gelu_apprx_tanh))
                st["iy"] += 1
            if j == 15:
                deferred.append(lambda tt=tt, zt=zt: emit_glu(tt, zt))

        def emit_glu(tt, zt):
            for mo in range(4):
                deferred3.append(lambda mo=mo: emit_glu_mo(tt, zt, mo))

        def emit_glu_mo(tt, zt, mo):
            pg = ps_g[0]
            sg = sig[mo]
            S.mm([wglu, zt], [pg], [(lambda kc=kc: nc.tensor.matmul(
                pg[:], wglu[:, kc, mo * 128:(mo + 1) * 128], zt[:, kc, :], start=(kc == 0), stop=(kc == 3))) for kc in range(4)])
            S.op(S.act, [pg, pp], [sg], lambda: nc.scalar.activation(
                out=sg[:], in_=pg[:], func=AF.Sigmoid, bias=pp[:, PP_BGLU + mo:PP_BGLU + mo + 1]))
            if mo == 3:
                deferred2.append(lambda: emit_glu_b(tt, zt))

        deferred3 = []

        def emit_glu_b(tt, zt):
            yc = ycst[tt % 2]
            for mo in range(4):
                sg = sig[mo]
                S.op(S.dve, [zt, sg], [yc], lambda mo=mo, sg=sg: nc.vector.tensor_tensor(
                    out=yc[:, mo, :], in0=zt[:, mo, :], in1=sg[:], op=ALU.mult))
            S.dma(S.q_sp, [yc], [], lambda e: e.dma_start(
                out=K.ymix[768:1280, :].rearrange("(mo p) t -> p mo t", p=128)[:, :, tt_sl(tt)], in_=yc[:]))

        deferred2 = []
        deferred = []
        load_u(0)
        stageA(0)
        stageB(0)
        for n in range(NIT):
            if n + 1 < NIT:
                stageA(n + 1)
                stageB(n + 1)
            if n >= 1:
                stageCp(n - 1)
            stageCd(n)
            if n % 16 == 2 and deferred:
                deferred.pop(0)()
            if n % 16 in (3, 4, 5, 6) and deferred3:
                deferred3.pop(0)()
            if n % 16 == 9 and deferred2:
                deferred2.pop(0)()
        stageCp(NIT - 1)
        while deferred:
            deferred.pop(0)()
        while deferred3:
            deferred3.pop(0)()
        while deferred2:
            deferred2.pop(0)()
    S.barrier()


def phase_merge(K, l, x_src):
    nc, S = K.nc, K.S
    with ExitStack() as es:
        sb, psum = mk_alloc(nc, es)
        wbr = sb("m_wbr", [128, 10, D], BF16)
        wout = sb("m_wout", [128, KC, D], BF16)
        yt = [sb(f"m_yt{i}", [128, 10, TT], BF16) for i in range(2)]
        gt = [sb(f"m_gt{i}", [128, 24, TT], BF16) for i in range(2)]
        xt = [sb(f"m_xt{i}", [128, KC, TT], F32) for i in range(2)]
        mg = [sb(f"m_mg{i}", [128, KC, TT], BF16) for i in range(2)]
        t1s = [sb(f"m_t1{i}", [128, TT], F32) for i in range(2)]
        t2s = [sb(f"m_t2{i}", [128, TT], F32) for i in range(2)]
        t3s = [sb(f"m_t3{i}", [128, TT], F32) for i in range(2)]
        pA = [psum(f"m_pA{i}") for i in range(2)]
        pB = [psum(f"m_pB{i}") for i in range(2)]
        pC = [psum(f"m_pC{i}") for i in range(2)]
        pO = [psum(f"m_pO{i}") for i in range(2)]
        S.dma(S.q_pool, [], [wbr], lambda e: e.dma_start(out=wbr[:, 0:4, :], in_=K.w_branch_a[l].rearrange("(kc p) m -> p kc m", p=128)))
        S.dma(S.q_pool, [], [wbr], lambda e: e.dma_start(out=wbr[:, 4:6, :], in_=K.w_branch_b[l].rearrange("(kc p) m -> p kc m", p=128)))
        S.dma(S.q_pool, [], [wbr], lambda e: e.dma_start(out=wbr[:, 6:10, :], in_=K.w_branch_c[l].rearrange("(kc p) m -> p kc m", p=128)))
        S.dma(S.q_pool, [], [wout], lambda e: e.dma_start(out=wout[:], in_=K.w_out[l].rearrange("(kc p) m -> p kc m", p=128)))
        yv = K.ymix.rearrange("(c p) t -> p c t", p=128)
        gv = K.projT[2560:5632, :].rearrange("(c p) t -> p c t", p=128)
        xv = x_src.rearrange("(c p) t -> p c t", p=128)
        xo = K.xres.rearrange("(c p) t -> p c t", p=128)
        st = {"im": 0, "io": 0}

        def load_yg(tt):
            y, g = yt[tt % 2], gt[tt % 2]
            S.dma(S.q_sp, [], [y], lambda e: e.dma_start(out=y[:], in_=yv[:, :, tt_sl(tt)]))
            S.dma(S.q_sp, [], [g], lambda e: e.dma_start(out=g[:], in_=gv[:, :, tt_sl(tt)]))

        def load_x(tt):
            x = xt[tt % 2]
            S.dma(S.q_sp, [], [x], lambda e: e.dma_start(out=x[:], in_=xv[:, :, tt_sl(tt)]))

        def branch(tt):
            y, g, m = yt[tt % 2], gt[tt % 2], mg[tt % 2]
            for mo in range(KC):
                im = st["im"]
                st["im"] += 1
                a, b, c = pA[im % 2], pB[im % 2], pC[im % 2]
                t1, t2, t3 = t1s[im % 2], t2s[im % 2], t3s[im % 2]
                ms = slice(mo * 128, (mo + 1) * 128)
                S.mm([wbr, y], [a], [(lambda kc=kc: nc.tensor.matmul(a[:], wbr[:, kc, ms], y[:, kc, :], start=(kc == 0), stop=(kc == 3))) for kc in range(0, 4)])
                S.mm([wbr, y], [b], [(lambda kc=kc: nc.tensor.matmul(b[:], wbr[:, kc, ms], y[:, kc, :], start=(kc == 4), stop=(kc == 5))) for kc in range(4, 6)])
                S.mm([wbr, y], [c], [(lambda kc=kc: nc.tensor.matmul(c[:], wbr[:, kc, ms], y[:, kc, :], start=(kc == 6), stop=(kc == 9))) for kc in range(6, 10)])
                S.op(S.dve, [a, g], [t1], lambda mo=mo: nc.vector.tensor_tensor(out=t1[:], in0=a[:], in1=g[:, mo, :], op=ALU.mult))
                S.op(S.dve, [b, g], [t2], lambda mo=mo: nc.vector.tensor_tensor(out=t2[:], in0=b[:], in1=g[:, 8 + mo, :], op=ALU.mult))
                S.op(S.dve, [t1, t2], [t1], lambda: nc.vector.tensor_tensor(out=t1[:], in0=t1[:], in1=t2[:], op=ALU.add))
                S.op(S.dve, [c, g], [t3], lambda mo=mo: nc.vector.tensor_tensor(out=t3[:], in0=c[:], in1=g[:, 16 + mo, :], op=ALU.mult))
                S.op(S.dve, [t1, t3], [m], lambda mo=mo: nc.vector.tensor_tensor(out=m[:, mo, :], in0=t1[:], in1=t3[:], op=ALU.add))

        def outproj(tt):
            x, m = xt[tt % 2], mg[tt % 2]
            for mo in range(KC):
                io = st["io"]
                st["io"] += 1
                o = pO[io % 2]
                ms = slice(mo * 128, (mo + 1) * 128)
                S.mm([wout, m], [o], [(lambda kc=kc: nc.tensor.matmul(o[:], wout[:, kc, ms], m[:, kc, :], start=(kc == 0), stop=(kc == KC - 1))) for kc in range(KC)])
                S.op(S.dve, [o, x], [x], lambda mo=mo: nc.vector.tensor_tensor(out=x[:, mo, :], in0=o[:], in1=x[:, mo, :], op=ALU.add))
            S.dma(S.q_act, [x], [], lambda e: e.dma_start(out=xo[:, :, tt_sl(tt)], in_=x[:]))

        load_yg(0)
        load_x(0)
        load_yg(1)
        load_x(1)
        branch(0)
        for tt in range(NTT):
            if tt + 1 < NTT:
                branch(tt + 1)
            if tt + 2 < NTT:
                load_yg(tt + 2)
            outproj(tt)
            if tt + 2 < NTT:
                load_x(tt + 2)
    S.barrier()


def phase_ffn_up(K, l):
    nc, S = K.nc, K.S
    pp = K.pp[l]
    NG = FFN // 128
    with ExitStack() as es:
        sb, psum = mk_alloc(nc, es)
        hT = [sb(f"f_hT{tt}", [128, KC, TT], BF16) for tt in range(NTT)]
        xv = K.xres.rearrange("(c p) t -> p c t", p=128)
        with ExitStack() as es2:
            sb2, psum2 = mk_alloc(nc, es2)
            xt = [sb2(f"f_xt{i}", [128, KC, TT], F32) for i in range(4)]
            sqs = [sb2(f"f_sq{i}", [128, KC, TT], BF16) for i in range(2)]
            rstd = [sb2(f"f_rstd{i}", [128, TT], F32) for i in range(2)]
            ps_n = [psum2(f"f_psn{i}") for i in range(2)]
            def load_x(tt):
                xb = xt[tt % 4]
                S.dma(S.q_sp, [], [xb], lambda e: e.dma_start(out=xb[:], in_=xv[:, :, tt_sl(tt)]))
            for tt in range(4):
                load_x(tt)
            for tt in range(NTT):
                xb = xt[tt % 4]
                rs = rstd[tt % 2]
                emit_rmsnorm_tile(K, (sqs[tt % 2], ps_n[tt % 2], rs), xb, None, None)
                for c in range(KC):
                    S.op(S.dve, [xb, rs, pp], [hT[tt]],
                         lambda c=c, xb=xb, rs=rs, tt=tt: nc.vector.scalar_tensor_tensor(
                             out=hT[tt][:, c, :], in0=xb[:, c, :], scalar=pp[:, PP_NFFN + c:PP_NFFN + c + 1],
                             in1=rs[:], op0=ALU.mult, op1=ALU.mult))
                if tt + 4 < NTT:
                    load_x(tt + 4)
            S.barrier()
        ps_g = [psum(f"f_psg{i}") for i in range(4)]
        ps_v = [psum(f"f_psv{i}") for i in range(4)]
        wg = [sb(f"f_wg{i}", [128, KC, 128], BF16) for i in range(2)]
        wv_ = [sb(f"f_wv{i}", [128, KC, 128], BF16) for i in range(2)]
        Ug = [sb(f"f_Ug{i}", [128, SEQ + 2], F32) for i in range(2)]
        Uv = [sb(f"f_Uv{i}", [128, SEQ + 2], F32) for i in range(2)]
        cgs = [sb(f"f_cg{i}", [128, TT], F32) for i in range(2)]
        cvs = [sb(f"f_cv{i}", [128, TT], F32) for i in range(2)]
        sgls = [sb(f"f_sgl{i}", [128, TT], F32) for i in range(2)]
        stage = [sb(f"f_st{i}", [128, SEQ], BF16) for i in range(2)]
        for ub in Ug + Uv:
            S.op(S.dve, [], [ub], lambda ub=ub: nc.vector.memset(ub[:, 0:2], 0.0))
        wup = K.w_up[l].rearrange("(c p) n -> p c n", p=128)
        ip = 0
        for fg in range(NG):
            fv = fg + NG
            wgb, wvb = wg[fg % 2], wv_[fg % 2]
            S.dma(S.q_pool, [], [wgb], lambda e, wgb=wgb, fg=fg: e.dma_start(out=wgb[:], in_=wup[:, :, fg * 128:(fg + 1) * 128]))
            S.dma(S.q_pool, [], [wvb], lambda e, wvb=wvb, fv=fv: e.dma_start(out=wvb[:], in_=wup[:, :, fv * 128:(fv + 1) * 128]))
            st = stage[fg % 2]
            ug, uv = Ug[fg % 2], Uv[fg % 2]
            for tt in range(NTT):
                pg, pv = ps_g[ip % 4], ps_v[ip % 4]
                cg, cv, sgl = cgs[ip % 2], cvs[ip % 2], sgls[ip % 2]
                ip += 1
                S.mm([hT[tt], wgb], [pg], [(lambda pg=pg, c=c, tt=tt, wgb=wgb: nc.tensor.matmul(pg[:], wgb[:, c, :], hT[tt][:, c, :], start=(c == 0), stop=(c == KC - 1))) for c in range(KC)])
                S.mm([hT[tt], wvb], [pv], [(lambda pv=pv, c=c, tt=tt, wvb=wvb: nc.tensor.matmul(pv[:], wvb[:, c, :], hT[tt][:, c, :], start=(c == 0), stop=(c == KC - 1))) for c in range(KC)])
                o = tt * TT
                for (p_, u_, c_, ch) in ((pg, ug, cg, fg), (pv, uv, cv, fv)):
                    w0 = pp[:, PP_CW + ch * 3 + 0:PP_CW + ch * 3 + 1]
                    w1 = pp[:, PP_CW + ch * 3 + 1:PP_CW + ch * 3 + 2]
                    w2 = pp[:, PP_CW + ch * 3 + 2:PP_CW + ch * 3 + 3]
                    bb = pp[:, PP_CB + ch:PP_CB + ch + 1]
                    S.op(S.act, [p_], [u_], lambda p_=p_, u_=u_, o=o: nc.scalar.activation(out=u_[:, o + 2:o + TT + 2], in_=p_[:], func=AF.Copy))
                    S.op(S.act, [p_, pp], [c_], lambda p_=p_, c_=c_, w2=w2, bb=bb: nc.scalar.activation(out=c_[:], in_=p_[:], func=AF.Identity, scale=w2, bias=bb))
                    S.op(S.dve, [u_, pp, c_], [c_], lambda u_=u_, c_=c_, w1=w1, o=o: nc.vector.scalar_tensor_tensor(out=c_[:], in0=u_[:, o + 1:o + TT + 1], scalar=w1, in1=c_[:], op0=ALU.mult, op1=ALU.add))
                    S.op(S.dve, [u_, pp, c_], [c_], lambda u_=u_, c_=c_, w0=w0, o=o: nc.vector.scalar_tensor_tensor(out=c_[:], in0=u_[:, o:o + TT], scalar=w0, in1=c_[:], op0=ALU.mult, op1=ALU.add))
                S.op(S.act, [cg], [sgl], lambda cg=cg, sgl=sgl: nc.scalar.activation(out=sgl[:], in_=cg[:], func=AF.Silu))
                S.op(S.dve, [sgl, cv], [st], lambda st=st, tt=tt, sgl=sgl, cv=cv: nc.vector.tensor_tensor(out=st[:, tt_sl(tt)], in0=sgl[:], in1=cv[:], op=ALU.mult))
            S.dma(S.q_sp, [st], [], lambda e, st=st, fg=fg: e.dma_start(out=K.gatedT[fg * 128:(fg + 1) * 128, :], in_=st[:]))
    S.barrier()


def phase_ffn_down(K, l):
    nc, S = K.nc, K.S
    NG = FFN // 128
    HALF = SEQ // 2
    with ExitStack() as es:
        sb, psum = mk_alloc(nc, es)
        gt = sb("d_gt", [128, NG, HALF], BF16)
        wd = [sb(f"d_wd{i}", [128, NG, 128], BF16) for i in range(2)]
        xc = [sb(f"d_xc{i}", [128, HALF], F32) for i in range(2)]
        ps = [psum(f"d_ps{i}") for i in range(4)]
        gv = K.gatedT.rearrange("(c p) t -> p c t", p=128)
        wdv = K.w_down[l].rearrange("(c p) m -> p c m", p=128)
        ip = 0
        iw = 0
        for half in range(2):
            hs = slice(half * HALF, (half + 1) * HALF)
            for q4 in range(2):
                S.dma(S.q_sp, [], [gt], lambda e, q4=q4, hs=hs: e.dma_start(out=gt[:, q4 * 11:(q4 + 1) * 11, :], in_=gv[:, q4 * 11:(q4 + 1) * 11, hs]))
            for mo in range(KC):
                w = wd[iw % 2]
                x = xc[iw % 2]
                iw += 1
                S.dma(S.q_pool, [], [w], lambda e, w=w, mo=mo: e.dma_start(out=w[:], in_=wdv[:, :, mo * 128:(mo + 1) * 128]))
                S.dma(S.q_sp, [], [x], lambda e, x=x, mo=mo, hs=hs: e.dma_start(out=x[:], in_=K.xres[mo * 128:(mo + 1) * 128, hs]))
                for t4 in range(HALF // TT):
                    p = ps[ip % 4]
                    ip += 1
                    ts_ = slice(t4 * TT, (t4 + 1) * TT)
                    S.mm([w, gt], [p], [(lambda p=p, c=c, w=w, ts_=ts_: nc.tensor.matmul(p[:], w[:, c, :], gt[:, c, ts_], start=(c == 0), stop=(c == NG - 1))) for c in range(NG)])
                    S.op(S.dve, [p, x], [x], lambda p=p, x=x, ts_=ts_: nc.vector.tensor_tensor(out=x[:, ts_], in0=p[:], in1=x[:, ts_], op=ALU.add))
                S.dma(S.q_act, [x], [], lambda e, x=x, mo=mo, hs=hs: e.dma_start(out=K.xres[mo * 128:(mo + 1) * 128, hs], in_=x[:]))
    S.barrier()


def phase_final(K):
    nc, S = K.nc, K.S
    with ExitStack() as es:
        sb, psum = mk_alloc(nc, es)
        xt = [sb(f"n_xt{i}", [128, KC, TT], F32) for i in range(4)]
        ot = [sb(f"n_ot{i}", [128, KC, TT], F32) for i in range(2)]
        sqs = [sb(f"n_sq{i}", [128, KC, TT], BF16) for i in range(2)]
        rstd = [sb(f"n_rstd{i}", [128, TT], F32) for i in range(2)]
        ps_n = [psum(f"n_psn{i}") for i in range(2)]
        xv = K.xres.rearrange("(c p) t -> p c t", p=128)
        ov = K.out.rearrange("(c p) t -> p c t", p=128)
        def load_x(tt):
            xb = xt[tt % 4]
            S.dma(S.q_sp, [], [xb], lambda e: e.dma_start(out=xb[:], in_=xv[:, :, tt_sl(tt)]))
        for tt in range(4):
            load_x(tt)
        for tt in range(NTT):
            xb, ob = xt[tt % 4], ot[tt % 2]
            rs = rstd[tt % 2]
            emit_rmsnorm_tile(K, (sqs[tt % 2], ps_n[tt % 2], rs), xb, None, None)
            for c in range(KC):
                S.op(S.dve, [xb, rs, K.nf], [ob],
                     lambda c=c, xb=xb, rs=rs, ob=ob: nc.vector.scalar_tensor_tensor(
                         out=ob[:, c, :], in0=xb[:, c, :], scalar=K.nf[:, c:c + 1], in1=rs[:], op0=ALU.mult, op1=ALU.mult))
            S.dma(S.q_act, [ob], [], lambda e, ob=ob, tt=tt: e.dma_start(out=ov[:, :, tt_sl(tt)], in_=ob[:]))
            if tt + 4 < NTT:
                load_x(tt + 4)
    S.barrier()


PHASES = ("p1", "attn_a", "attn_b", "s5", "merge", "ffn_up", "ffn_down")


def build_program():
    nc = bass.Bass("TRN2", target_bir_lowering=False)
    K = Ctx()
    K.nc = nc
    K.S = Sched(nc)
    S = K.S

    def din(name, shape, dt=F32):
        return nc.dram_tensor(name, shape, dt, kind="ExternalInput").ap()

    def dscr(name, shape, dt):
        kind = "ExternalOutput" if name in DEBUG else "Internal"
        return nc.dram_tensor(name, shape, dt, kind=kind).ap()

    K.xT = din("xT", [D, SEQ])
    K.w_in = din("w_in", [DEPTH, D, INW])
    K.pp_d = din("pp", [DEPTH, 128, PPW])
    K.nf_d = din("nf", [128, 8])
    K.consts_d = din("consts", [128, 896])
    K.bmat = din("bmat", [DEPTH, 2, 128, 16, 128])
    K.cmat = din("cmat", [DEPTH, 2, 128, 16, 128])
    K.w_glu = din("w_glu", [DEPTH, 512, 512])
    K.w_branch_a = din("w_branch_a", [DEPTH, 512, D])
    K.w_branch_b = din("w_branch_b", [DEPTH, 256, D])
    K.w_branch_c = din("w_branch_c", [DEPTH, 512, D])
    K.w_out = din("w_out", [DEPTH, D, D])
    K.w_up = din("w_up", [DEPTH, D, 2 * FFN])
    K.w_down = din("w_down", [DEPTH, FFN, D])
    K.out = nc.dram_tensor("outT", [D, SEQ], F32, kind="ExternalOutput").ap()
    K.projT = dscr("projT", [INW, SEQ], BF16)
    K.va_tok = dscr("va_tok", [SEQ, 128], BF16)
    K.vb_tok = dscr("vb_tok", [SEQ, 256], BF16)
    K.ymix = dscr("ymix", [1280, SEQ], BF16)
    K.xres = dscr("xres", [D, SEQ], F32)
    K.gatedT = dscr("gatedT", [FFN, SEQ], BF16)
    K.dbg = {}
    if "dbg" in DEBUG:
        K.dbgbuf = nc.dram_tensor("dbgbuf", [20, 128, 512], F32, kind="ExternalOutput").ap()
    if "h" in DEBUG:
        K.dbg["h"] = nc.dram_tensor("dbg_h", [D, SEQ], BF16, kind="ExternalOutput").ap()

    K.PP_NMIX = PP_NMIX
    K.ones_f = Buf(nc.alloc_sbuf_tensor("ones_f", [128, 128], F32))
    K.ones_b = Buf(nc.alloc_sbuf_tensor("ones_b", [128, 128], BF16))
    K.eps_col = Buf(nc.alloc_sbuf_tensor("eps_col", [128, 1], F32))
    K.cb16 = Buf(nc.alloc_sbuf_tensor("cb16", [128, 512], BF16))
    K.maskc4 = Buf(nc.alloc_sbuf_tensor("maskc4", [128, 512], BF16))
    K.maskpA4 = Buf(nc.alloc_sbuf_tensor("maskpA4", [128, 512], BF16))
    K.nf = Buf(nc.alloc_sbuf_tensor("nf_sb", [128, 8], F32))
    K.pp = [Buf(nc.alloc_sbuf_tensor(f"pp_sb{l}", [128, PPW], F32)) for l in range(DEPTH)]
    S.op(S.dve, [], [K.ones_f], lambda: nc.vector.memset(K.ones_f[:], 1.0))
    S.op(S.dve, [], [K.ones_b], lambda: nc.vector.memset(K.ones_b[:], 1.0))
    S.op(S.dve, [], [K.eps_col], lambda: nc.vector.memset(K.eps_col[:], EPS))
    S.dma(S.q_pool, [], [K.cb16], lambda e: e.dma_start(out=K.cb16[:], in_=K.consts_d[:, 0:512]))
    K.m01 = Buf(nc.alloc_sbuf_tensor("m01", [128, 256], BF16))
    S.dma(S.q_pool, [], [K.m01], lambda e: e.dma_start(out=K.m01[:], in_=K.consts_d[:, 512:768]))
    S.dma(S.q_sp, [], [K.nf], lambda e: e.dma_start(out=K.nf[:], in_=K.nf_d[:, :]))
    for l in range(DEPTH):
        S.dma(S.q_sp, [], [K.pp[l]], lambda e, l=l: e.dma_start(out=K.pp[l][:], in_=K.pp_d[l]))
    K.m01pA = Buf(nc.alloc_sbuf_tensor("m01pA", [128, 128], BF16))
    S.dma(S.q_pool, [], [K.m01pA], lambda e: e.dma_start(out=K.m01pA[:], in_=K.consts_d[:, 768:896]))
    K.m01c4 = Buf(nc.alloc_sbuf_tensor("m01c4", [128, 512], BF16))
    K.m01pA4 = Buf(nc.alloc_sbuf_tensor("m01pA4", [128, 512], BF16))
    for h in range(4):
        S.op(S.dve, [K.m01], [K.m01c4], lambda h=h: nc.vector.tensor_copy(out=K.m01c4[:, h * 128:(h + 1) * 128], in_=K.m01[:, 0:128]))
        S.op(S.dve, [K.m01pA], [K.m01pA4], lambda h=h: nc.vector.tensor_copy(out=K.m01pA4[:, h * 128:(h + 1) * 128], in_=K.m01pA[:]))
    for h in range(4):
        S.op(S.dve, [K.cb16], [K.maskc4], lambda h=h: nc.vector.tensor_copy(out=K.maskc4[:, h * 128:(h + 1) * 128], in_=K.cb16[:, 128:256]))
        S.op(S.dve, [K.cb16], [K.maskpA4], lambda h=h: nc.vector.tensor_copy(out=K.maskpA4[:, h * 128:(h + 1) * 128], in_=K.cb16[:, 256:384]))

    fns = {"p1": phase_p1, "attn_a": phase_attn_a, "attn_b": phase_attn_b, "s5": phase_s5, "merge": phase_merge,
           "ffn_up": phase_ffn_up, "ffn_down": phase_ffn_down}
    done = False
    for l in range(DEPTH):
        x_src = K.xT if l == 0 else K.xres
        for ph in PHASES:
            if ONLY is not None and (l, ph) not in ONLY:
                continue
            if ph in ("p1", "merge"):
                fns[ph](K, l, x_src)
            else:
                fns[ph](K, l)
            if STOP_AFTER is not None and (l, ph) == tuple(STOP_AFTER):
                done = True
                break
        if done:
            break
    if not done and ONLY is None:
        phase_final(K)
    S.barrier()
    return nc, K


ONLY = None


def prep_inputs(inputs):
    f = lambda k: np.asarray(inputs[k], dtype=np.float32)
    x = f("x")
    common = {}
    for k in ("w_in", "w_glu", "w_branch_a", "w_branch_b", "w_branch_c", "w_out", "w_up", "w_down"):
        common[k] = np.ascontiguousarray(f(k))
    pp = np.zeros((DEPTH, 128, PPW), np.float32)
    bmat = np.zeros((DEPTH, 2, 128, 16, 128), np.float32)
    cmat = np.zeros((DEPTH, 2, 128, 16, 128), np.float32)
    for l in range(DEPTH):
        pp[l, :, PP_NMIX:PP_NMIX + 8] = f("norm_mix")[l].reshape(8, 128).T
        pp[l, :, PP_NFFN:PP_NFFN + 8] = f("norm_ffn")[l].reshape(8, 128).T
        cw = f("conv_w")[l]
        pp[l, :, PP_CW:PP_CW + 132] = cw.reshape(3, 44, 128).transpose(2, 1, 0).reshape(128, 132)
        pp[l, :, PP_CB:PP_CB + 44] = f("conv_b")[l].reshape(44, 128).T
        pp[l, :, PP_BGLU:PP_BGLU + 4] = f("b_glu")[l].reshape(4, 128).T
        pp[l, :, PP_SSMD:PP_SSMD + 4] = f("ssm_d")[l].reshape(4, 128).T
        pp[l, :, PP_SINK:PP_SINK + 8] = np.broadcast_to(f("attn_sinks")[l][None, :], (128, 8))
        pp[l, :, PP_LR:PP_LR + 16] = f("ssm_lambda_re")[l].reshape(16, 128).T
        pp[l, :, PP_LI:PP_LI + 16] = f("ssm_lambda_im")[l].reshape(16, 128).T
        pp[l, :, PP_LDT:PP_LDT + 16] = np.repeat(f("ssm_log_dt")[l], 64).reshape(16, 128).T
        bre, bim = f("ssm_b_re")[l], f("ssm_b_im")[l]
        cre, cim = f("ssm_c_re")[l], f("ssm_c_im")[l]
        for g in range(32):
            j = g // 2
            ks = slice((g % 8) * 16, (g % 8) * 16 + 16)
            ms = slice((g % 2) * 64, (g % 2) * 64 + 64)
            bmat[l, 0, ks, j, ms] = bre[g].T
            bmat[l, 1, ks, j, ms] = bim[g].T
            cmat[l, 0, ms, j, ks] = cre[g].T
            cmat[l, 1, ms, j, ks] = cim[g].T
    common["pp"] = pp
    common["bmat"] = bmat
    common["cmat"] = cmat
    common["nf"] = np.ascontiguousarray(f("norm_final").reshape(8, 128).T)
    k_idx = np.arange(128)[:, None]
    q_idx = np.arange(128)[None, :]
    consts = np.zeros((128, 896), np.float32)
    consts[:, 768:896] = np.where(k_idx >= q_idx + 1, 1.0, 0.0)
    consts[:, 512:640] = np.where(k_idx <= q_idx, 1.0, 0.0)
    consts[:, 640:768] = np.where(k_idx >= q_idx, 1.0, 0.0)
    consts[:, 0:128] = np.eye(128, dtype=np.float32)
    consts[:, 128:256] = np.where(k_idx <= q_idx, 0.0, NEG)
    consts[:, 256:384] = np.where(k_idx >= q_idx + 1, 0.0, NEG)
    consts[:, 384:512] = np.where(k_idx >= q_idx, 0.0, NEG)
    common["consts"] = consts
    in_maps = []
    for b in range(NCORES):
        m = dict(common)
        m["xT"] = np.ascontiguousarray(x[b].T)
        in_maps.append(m)
    return in_maps


def kernel(**inputs):
    nc, K = build_program()
    in_maps = prep_inputs(inputs)
    res = run_bass_kernel_spmd(nc, in_maps, core_ids=list(range(NCORES)))
    outs = [np.asarray(r["outT"]).T for r in res.results]
    return np.ascontiguousarray(np.stack(outs, axis=0).astype(np.float32))
```

```python
import numpy as np
from contextlib import ExitStack
import concourse.bass as bass
import concourse.mybir as mybir
from concourse.bass_utils import run_bass_kernel_spmd

F32 = mybir.dt.float32
BF16 = mybir.dt.bfloat16
AF = mybir.ActivationFunctionType
ALU = mybir.AluOpType

D = 1024
SEQ = 4096
DEPTH = 2
NCORES = 8
INW = 5632
FFN = 2816
EPS = 1e-6
TT = 512
NTT = SEQ // TT
KC = D // 128

DEBUG = {}
STOP_AFTER = None


class Buf:
    __slots__ = ("t", "w", "r", "name")

    def __init__(self, t, name=""):
        self.t = t
        self.w = None
        self.r = {}
        self.name = name

    def __getitem__(self, k):
        return self.t[k]


class Eng:
    def __init__(self, name, eng, sem):
        self.name = name
        self.eng = eng
        self.sem = sem
        self.known = {}


class DmaQ:
    def __init__(self, S, E, k, name):
        self.S = S
        self.E = E
        self.sems = [S.new_sem(f"dq_{name}{i}") for i in range(k)]
        self.n = 0


class Sched:
    def __init__(self, nc):
        self.nc = nc
        self.sems = []
        self.issued = []
        self.pe = Eng("pe", nc.tensor, self.new_sem("c_pe"))
        self.act = Eng("act", nc.scalar, self.new_sem("c_act"))
        self.dve = Eng("dve", nc.vector, self.new_sem("c_dve"))
        self.pool = Eng("pool", nc.gpsimd, self.new_sem("c_pool"))
        self.sp = Eng("sp", nc.sync, None)
        self.engs = [self.pe, self.act, self.dve, self.pool, self.sp]
        self.q_sp = DmaQ(self, self.sp, 8, "sp")
        self.q_pool = DmaQ(self, self.pool, 8, "pool")
        self.q_act = DmaQ(self, self.act, 6, "act")
        self.nops = 0

    def new_sem(self, name):
        self.sems.append(self.nc.alloc_semaphore(name))
        self.issued.append(0)
        return len(self.sems) - 1

    def wait(self, E, s, v):
        if v > 0 and E.known.get(s, 0) < v:
            E.eng.wait_ge(self.sems[s], v)
            E.known[s] = v

    def _sync(self, E, reads, writes, is_dma):
        need = {}

        def add(ev):
            if ev is not None and ev[1] > need.get(ev[0], 0):
                need[ev[0]] = ev[1]

        for b in reads:
            add(b.w)
        for b in writes:
            if b.w is not None and (is_dma or b.w[0] != E.sem):
                add(b.w)
            for s, v in b.r.items():
                if is_dma or s != E.sem:
                    add((s, v))
        for s, v in need.items():
            self.wait(E, s, v)

    def _mark(self, ev, reads, writes):
        s, v = ev
        for b in reads:
            if b.r.get(s, 0) < v:
                b.r[s] = v
        for b in writes:
            b.w = ev
            b.r = {}

    def op(self, E, reads, writes, emit):
        self._sync(E, reads, writes, False)
        ins = emit()
        self.issued[E.sem] += 1
        ins.then_inc(self.sems[E.sem], 1)
        self._mark((E.sem, self.issued[E.sem]), reads, writes)
        self.nops += 1

    def mm(self, reads, writes, emits):
        E = self.pe
        self._sync(E, reads, writes, False)
        ins = None
        for e in emits:
            ins = e()
            self.nops += 1
        self.issued[E.sem] += 1
        ins.then_inc(self.sems[E.sem], 1)
        self._mark((E.sem, self.issued[E.sem]), reads, writes)

    def dma(self, Q, reads, writes, emit):
        E = Q.E
        s = Q.sems[Q.n % len(Q.sems)]
        Q.n += 1
        self.wait(E, s, self.issued[s])
        self._sync(E, reads, writes, True)
        ins = emit(E.eng)
        self.issued[s] += 16
        ins.then_inc(self.sems[s], 16)
        self._mark((s, self.issued[s]), reads, writes)
        self.nops += 1

    def barrier(self):
        for E in self.engs:
            for s in range(len(self.sems)):
                self.wait(E, s, self.issued[s])


class Ctx:
    pass


def tt_sl(tt):
    return slice(tt * TT, (tt + 1) * TT)


def emit_rmsnorm_tile(K, es_bufs, xt, gain_ap_fn, out_fn):
    S, nc = K.S, K.nc
    sq, ps, rstd = es_bufs
    S.op(S.act, [xt], [sq], lambda: nc.scalar.activation(out=sq[:], in_=xt[:], func=AF.Square))
    S.mm([sq, K.ones_b], [ps],
         [(lambda c=c: nc.tensor.matmul(ps[:], K.ones_b[:], sq[:, c, :], start=(c == 0), stop=(c == KC - 1)))
          for c in range(KC)])
    S.op(S.act, [ps], [rstd], lambda: nc.scalar.activation(out=rstd[:], in_=ps[:], func=AF.Ln,
                                                             scale=1.0 / D, bias=K.eps_col[:, 0:1]))
    S.op(S.act, [rstd], [rstd], lambda: nc.scalar.activation(out=rstd[:], in_=rstd[:], func=AF.Exp, scale=-0.5))


def phase_p1(K, l, x_src):
    nc, S = K.nc, K.S
    with ExitStack() as es:
        sb, psum = mk_alloc(nc, es)

        hT = [sb(f"p1_hT{tt}", [128, KC, TT], BF16) for tt in range(NTT)]
        g = K.pp[l]
        xv = x_src.rearrange("(c p) t -> p c t", p=128)
        with ExitStack() as es2:
            sb2, psum2 = mk_alloc(nc, es2)
            xt = [sb2(f"p1_xt{i}", [128, KC, TT], F32) for i in range(4)]
            sqs = [sb2(f"p1_sq{i}", [128, KC, TT], BF16) for i in range(2)]
            rstd = [sb2(f"p1_rstd{i}", [128, TT], F32) for i in range(2)]
            ps_n = [psum2(f"p1_psn{i}") for i in range(2)]
            def load_x(tt):
                xb = xt[tt % 4]
                S.dma(S.q_sp, [], [xb], lambda e: e.dma_start(out=xb[:], in_=xv[:, :, tt_sl(tt)]))
            for tt in range(4):
                load_x(tt)
            for tt in range(NTT):
                xb = xt[tt % 4]
                rs = rstd[tt % 2]
                emit_rmsnorm_tile(K, (sqs[tt % 2], ps_n[tt % 2], rs), xb, None, None)
                for c in range(KC):
                    S.op(S.dve, [xb, rs, g], [hT[tt]],
                         lambda c=c, xb=xb, rs=rs, tt=tt: nc.vector.scalar_tensor_tensor(
                             out=hT[tt][:, c, :], in0=xb[:, c, :], scalar=g[:, PP_NMIX + c:PP_NMIX + c + 1],
                             in1=rs[:], op0=ALU.mult, op1=ALU.mult))
                if tt + 4 < NTT:
                    load_x(tt + 4)
            S.barrier()
        ps_m = [psum(f"p1_psm{i}") for i in range(4)]
        wbuf = [sb(f"p1_w{i}", [128, KC, 512], BF16) for i in range(2)]
        stage = [sb(f"p1_st{i}", [128, SEQ], BF16) for i in range(2)]
        vst = sb("p1_vst", [128, 32, 384], BF16)

        w_in = K.w_in[l]
        wv = w_in.rearrange("(c p) n -> p c n", p=128)
        wvb = wbuf[0]
        S.dma(S.q_pool, [], [wvb], lambda e: e.dma_start(out=wvb[:, :, 0:128], in_=wv[:, :, 640:768]))
        S.dma(S.q_pool, [], [wvb], lambda e: e.dma_start(out=wvb[:, :, 128:384], in_=wv[:, :, 1792:2048]))
        pi = 0
        for blk in range(32):
            tt, o = blk // 4, (blk % 4) * 128
            ps = ps_m[pi % 4]
            pi += 1
            S.mm([hT[tt], wvb], [ps],
                 [(lambda c=c, tt=tt, o=o, ps=ps: nc.tensor.matmul(ps[:, 0:384], hT[tt][:, c, o:o + 128], wvb[:, c, 0:384],
                                                                   start=(c == 0), stop=(c == KC - 1)))
                  for c in range(KC)])
            S.op(S.act, [ps], [vst], lambda blk=blk, ps=ps: nc.scalar.activation(out=vst[:, blk, :], in_=ps[:, 0:384], func=AF.Copy))
        S.dma(S.q_act, [vst], [], lambda e: e.dma_start(out=K.va_tok.rearrange("(b p) f -> p b f", p=128), in_=vst[:, :, 0:128]))
        S.dma(S.q_act, [vst], [], lambda e: e.dma_start(out=K.vb_tok.rearrange("(b p) f -> p b f", p=128), in_=vst[:, :, 128:384]))

        ci = 0
        for grp in range(INW // 512):
            wb = wbuf[(grp + 1) % 2]
            S.dma(S.q_pool, [], [wb], lambda e, wb=wb, grp=grp: e.dma_start(out=wb[:], in_=wv[:, :, grp * 512:(grp + 1) * 512]))
            for j in range(4):
                ch = grp * 4 + j
                if ch in (5, 14, 15):
                    continue
                st = stage[ci % 2]
                ci += 1
                func = AF.Sigmoid if ch >= 20 else AF.Copy
                for tt in range(NTT):
                    ps = ps_m[pi % 4]
                    pi += 1
                    S.mm([hT[tt], wb], [ps],
                         [(lambda c=c, tt=tt, j=j, ps=ps, wb=wb: nc.tensor.matmul(
                             ps[:], wb[:, c, j * 128:(j + 1) * 128], hT[tt][:, c, :], start=(c == 0), stop=(c == KC - 1)))
                          for c in range(KC)])
                    S.op(S.act, [ps], [st], lambda tt=tt, ps=ps, st=st, func=func: nc.scalar.activation(
                        out=st[:, tt_sl(tt)], in_=ps[:], func=func))
                S.dma(S.q_sp, [st], [], lambda e, st=st, ch=ch: e.dma_start(out=K.projT[ch * 128:(ch + 1) * 128, :], in_=st[:]))
    S.barrier()


NEG = -30000.0
PP_NMIX, PP_NFFN, PP_CW, PP_CB, PP_BGLU, PP_SSMD, PP_SINK, PP_LR, PP_LI, PP_LDT = 0, 8, 16, 148, 192, 196, 200, 208, 224, 240
PPW = 256


_UID = [0]


def mk_alloc(nc, es):
    _UID[0] += 1
    uid = _UID[0]

    def sb(name, shape, dt):
        return Buf(es.enter_context(nc.sbuf_tensor(f"{name}_u{uid}", shape, dt)), name)

    def psum(name):
        return Buf(es.enter_context(nc.psum_tensor(f"{name}_u{uid}", [128, 512], F32)), name)
    return sb, psum


def dump(K, slot, buf, ap, cols=512):
    if "dbg" not in DEBUG:
        return
    K.S.dma(K.S.q_pool, [buf], [], lambda e: e.dma_start(out=K.dbgbuf[slot, :, 0:cols], in_=ap))


def phase_attn_a(K, l):
    nc, S = K.nc, K.S
    pp = K.pp[l]
    with ExitStack() as es:
        sb, psum = mk_alloc(nc, es)
        KTs = [sb(f"a_KT{i}", [64, SEQ], BF16) for i in range(2)]
        QTs = [sb(f"a_QT{i}", [64, 4, SEQ], BF16) for i in range(2)]
        Vs = [sb(f"a_V{i}", [128, 32, 64], BF16) for i in range(2)]
        yst = sb("a_yst", [64, 4, SEQ], BF16)
        esink = sb("a_es", [128, 8], F32)
        sinkt = sb("a_sinkt", [64, 8, 128], F32)
        zeros = sb("a_zeros", [64, 128], F32)
        PTc = [sb(f"a_PTc{i}", [128, 512], BF16) for i in range(2)]
        PTp = [sb(f"a_PTp{i}", [128, 512], BF16) for i in range(2)]
        tmp = [sb(f"a_tmp{i}", [64, 512], F32) for i in range(2)]
        ps_c = [psum(f"a_psc{i}") for i in range(2)]
        ps_p = [psum(f"a_psp{i}") for i in range(2)]
        ps_o = [psum(f"a_pso{i}") for i in range(2)]
        ps_l = [psum(f"a_psl{i}") for i in range(2)]

        S.op(S.act, [pp], [esink], lambda: nc.scalar.activation(out=esink[:], in_=pp[:, PP_SINK:PP_SINK + 8], func=AF.Exp))
        S.op(S.dve, [], [zeros], lambda: nc.vector.memset(zeros[:], 0.0))
        for h8 in range(8):
            S.op(S.dve, [zeros, esink], [sinkt], lambda h8=h8: nc.vector.tensor_scalar(
                out=sinkt[:, h8, :], in0=zeros[:], scalar1=esink[0:64, h8:h8 + 1], scalar2=None, op0=ALU.add))

        it = 0
        pend = None
        for g in range(2):
            KT, QT, V = KTs[g], QTs[g], Vs[g]
            S.dma(S.q_sp, [], [KT], lambda e, g=g, KT=KT: e.dma_start(out=KT[:], in_=K.projT[512 + g * 64:512 + (g + 1) * 64, :]))
            S.dma(S.q_sp, [], [QT], lambda e, g=g, QT=QT: e.dma_start(
                out=QT[:], in_=K.projT[g * 256:(g + 1) * 256, :].rearrange("(h d) t -> d h t", d=64)))
            S.dma(S.q_sp, [], [V], lambda e, g=g, V=V: e.dma_start(
                out=V[:], in_=K.va_tok[:, g * 64:(g + 1) * 64].rearrange("(b p) f -> p b f", p=128)))
        for g in range(2):
            KT, QT, V = KTs[g], QTs[g], Vs[g]
            for b in range(32):
                i2 = it % 2
                it += 1
                q_ap = QT[:, :, b * 128:(b + 1) * 128]
                pc, pp_, po, pl = ps_c[i2], ps_p[i2], ps_o[i2], ps_l[i2]
                ptc, ptp, tm = PTc[i2], PTp[i2], tmp[i2]

                def s1(b=b, q_ap=q_ap, pc=pc, pp_=pp_, ptc=ptc, ptp=ptp, KT=KT, QT=QT):
                    S.mm([KT, QT], [pc], [
                        lambda: nc.tensor.matmul(pc[:].rearrange("p (h q) -> p h q", h=4), KT[:, b * 128:(b + 1) * 128], q_ap, start=True, stop=True)])
                    S.op(S.act, [pc], [ptc], lambda: nc.scalar.activation(out=ptc[:], in_=pc[:], func=AF.Exp, scale=0.125))
                    S.op(S.dve, [ptc, K.m01c4], [ptc], lambda: nc.vector.tensor_tensor(out=ptc[:], in0=ptc[:], in1=K.m01c4[:], op=ALU.mult))
                    if b > 0:
                        S.mm([KT, QT], [pp_], [
                            lambda: nc.tensor.matmul(pp_[:].rearrange("p (h q) -> p h q", h=4), KT[:, (b - 1) * 128:b * 128], q_ap, start=True, stop=True)])
                        S.op(S.act, [pp_], [ptp], lambda: nc.scalar.activation(out=ptp[:], in_=pp_[:], func=AF.Exp, scale=0.125))
                        S.op(S.dve, [ptp, K.m01pA4], [ptp], lambda: nc.vector.tensor_tensor(out=ptp[:], in0=ptp[:], in1=K.m01pA4[:], op=ALU.mult))

                def s2(b=b, g=g, po=po, pl=pl, ptc=ptc, ptp=ptp, tm=tm, V=V):
                    if b > 0:
                        S.mm([V, ptc, ptp], [po], [
                            lambda: nc.tensor.matmul(po[0:64, :], V[:, b - 1, :], ptp[:], start=True, stop=False),
                            lambda: nc.tensor.matmul(po[0:64, :], V[:, b, :], ptc[:], start=False, stop=True)])
                        S.mm([K.ones_b, ptc, ptp], [pl], [
                            lambda: nc.tensor.matmul(pl[0:64, :], K.ones_b[:, 0:64], ptp[:], start=True, stop=False),
                            lambda: nc.tensor.matmul(pl[0:64, :], K.ones_b[:, 0:64], ptc[:], start=False, stop=True)])
                    else:
                        S.mm([V, ptc], [po], [lambda: nc.tensor.matmul(po[0:64, :], V[:, b, :], ptc[:], start=True, stop=True)])
                        S.mm([K.ones_b, ptc], [pl], [lambda: nc.tensor.matmul(pl[0:64, :], K.ones_b[:, 0:64], ptc[:], start=True, stop=True)])
                    S.op(S.dve, [pl, sinkt], [tm], lambda: nc.vector.tensor_tensor(
                        out=tm[:], in0=pl[0:64, :], in1=sinkt[:, g * 4:(g + 1) * 4, :].rearrange("p h q -> p (h q)"), op=ALU.add))
                    S.op(S.act, [tm], [tm], lambda: nc.scalar.activation(out=tm[:], in_=tm[:], func=AF.Ln))
                    S.op(S.act, [tm], [tm], lambda: nc.scalar.activation(out=tm[:], in_=tm[:], func=AF.Exp, scale=-1.0))
                    S.op(S.dve, [po, tm], [yst], lambda: nc.vector.tensor_tensor(
                        out=yst[:, :, b * 128:(b + 1) * 128], in0=po[0:64, :].rearrange("p (h q) -> p h q", h=4),
                        in1=tm[:].rearrange("p (h q) -> p h q", h=4), op=ALU.mult))
                s1()
                if pend is not None:
                    pend()
                pend = s2
            pend()
            pend = None
            S.dma(S.q_sp, [yst], [], lambda e, g=g: e.dma_start(
                out=K.ymix[g * 256:(g + 1) * 256, :].rearrange("(h d) t -> d h t", d=64), in_=yst[:]))
    S.barrier()


def phase_attn_b(K, l):
    nc, S = K.nc, K.S
    dils = (1, 4, 16)
    with ExitStack() as es:
        sb, psum = mk_alloc(nc, es)
        KTs = [sb(f"b_KT{i}", [64, SEQ], BF16) for i in range(2)]
        QTs = [sb(f"b_QT{i}", [64, 3, SEQ], BF16) for i in range(2)]

        def load_kq(h):
            KT, QT = KTs[h % 2], QTs[h % 2]
            S.dma(S.q_sp, [], [KT], lambda e: e.dma_start(out=KT[:], in_=K.projT[1536 + h * 64:1536 + (h + 1) * 64, :]))
            S.dma(S.q_sp, [], [QT], lambda e: e.dma_start(
                out=QT[:], in_=K.projT[768:1536, :].rearrange("(gi hh d) t -> d gi hh t", gi=3, hh=4, d=64)[:, :, h, :]))
        Vd = [sb(f"b_V{gi}", [128, 32, 256], BF16) for gi in range(3)]
        Va = [sb(f"b_Va{gi}", [128, 32, 128], BF16) for gi in range(3)]
        acc = sb("b_acc", [128, SEQ], F32)
        rec = [sb(f"b_rec{i}", [64, 512], F32) for i in range(2)]
        yst = sb("b_yst", [64, SEQ], BF16)
        ident_f = sb("b_identf", [128, 128], F32)
        NB = 3
        PT = [sb(f"b_PT{i}", [128, 256], BF16) for i in range(NB)]
        ps_s = [psum(f"b_pss{i}") for i in range(NB)]
        ps_o = [psum(f"b_pso{i}") for i in range(NB)]
        ps_d = [psum(f"b_psd{i}") for i in range(2)]
        S.dma(S.q_sp, [], [ident_f], lambda e: e.dma_start(out=ident_f[:], in_=K.consts_d[:, 0:128]))
        for gi, dil in enumerate(dils):
            S.dma(S.q_sp, [], [Vd[gi]], lambda e, gi=gi, dil=dil: e.dma_start(
                out=Vd[gi][:].rearrange("j (b r) f -> j b r f", r=dil),
                in_=K.vb_tok.rearrange("(b j r) f -> j b r f", j=128, r=dil)))
            S.op(S.dve, [], [Va[gi]], lambda gi=gi: nc.vector.memset(Va[gi][:, :, 64:128], 1.0))
        it = 0
        pendq = []
        load_kq(0)
        for h in range(4):
            KT, QT = KTs[h % 2], QTs[h % 2]
            if h + 1 < 4:
                load_kq(h + 1)
            for gi in range(3):
                S.op(S.act, [Vd[gi]], [Va[gi]], lambda gi=gi, h=h: nc.scalar.activation(
                    out=Va[gi][:, :, 0:64], in_=Vd[gi][:, :, h * 64:(h + 1) * 64], func=AF.Copy))
            for gi, dil in enumerate(dils):
                nb = 32 // dil
                for r in range(dil):
                    for b in range(nb):
                        i3 = it % NB
                        it += 1
                        pss, pso, pt = ps_s[i3], ps_o[i3], PT[i3]

                        def tok(bb, r=r, dil=dil):
                            s0 = r + dil * 128 * bb
                            return slice(s0, s0 + dil * 127 + 1, dil)

                        def s1(b=b, gi=gi, tok=tok, pss=pss, pt=pt, KT=KT, QT=QT):
                            mms = [lambda: nc.tensor.matmul(pss[:, 0:128], KT[:, tok(b)], QT[:, gi, tok(b)], start=True, stop=True)]
                            if b > 0:
                                mms += [lambda: nc.tensor.matmul(pss[:, 128:256], KT[:, tok(b - 1)], QT[:, gi, tok(b)], start=True, stop=True)]
                            S.mm([KT, QT], [pss], mms)
                            w = 256 if b > 0 else 128
                            S.op(S.act, [pss], [pt], lambda: nc.scalar.activation(out=pt[:, 0:w], in_=pss[:, 0:w], func=AF.Exp, scale=0.125))
                            S.op(S.dve, [pt, K.m01], [pt], lambda: nc.vector.tensor_tensor(out=pt[:, 0:w], in0=pt[:, 0:w], in1=K.m01[:, 0:w], op=ALU.mult))

                        def s2(b=b, gi=gi, dil=dil, r=r, tok=tok, pso=pso, pt=pt):
                            vcur = Va[gi][:, b * dil + r, :]
                            if b > 0:
                                vprev = Va[gi][:, (b - 1) * dil + r, :]
                                mms = [
                                    lambda: nc.tensor.matmul(pso[:, 0:128], vprev, pt[:, 128:256], start=True, stop=False),
                                    lambda: nc.tensor.matmul(pso[:, 0:128], vcur, pt[:, 0:128], start=False, stop=True)]
                            else:
                                mms = [lambda: nc.tensor.matmul(pso[:, 0:128], vcur, pt[:, 0:128], start=True, stop=True)]
                            S.mm([Va[gi], pt], [pso], mms)
                            dst = acc[:, tok(b)]
                            if gi == 0:
                                S.op(S.dve, [pso], [acc], lambda: nc.vector.tensor_copy(out=dst, in_=pso[:, 0:128]))
                            else:
                                S.op(S.dve, [pso, acc], [acc], lambda: nc.vector.tensor_tensor(out=dst, in0=dst, in1=pso[:, 0:128], op=ALU.add))
                        s1()
                        pendq.append(s2)
                        if len(pendq) > 2:
                            pendq.pop(0)()
            while pendq:
                pendq.pop(0)()
            for c in range(SEQ // 512):
                cs_ = slice(c * 512, (c + 1) * 512)
                pd, rc = ps_d[c % 2], rec[c % 2]
                S.mm([ident_f, acc], [pd], [lambda pd=pd, cs_=cs_: nc.tensor.matmul(pd[0:64, :], ident_f[:, 64:128], acc[:, cs_], start=True, stop=True)])
                S.op(S.act, [pd], [rc], lambda pd=pd, rc=rc: nc.scalar.activation(out=rc[:], in_=pd[0:64, :], func=AF.Ln))
                S.op(S.act, [rc], [rc], lambda rc=rc: nc.scalar.activation(out=rc[:], in_=rc[:], func=AF.Exp, scale=-1.0))
                S.op(S.dve, [acc, rc], [yst], lambda rc=rc, cs_=cs_: nc.vector.tensor_tensor(out=yst[:, cs_], in0=acc[0:64, cs_], in1=rc[:], op=ALU.mult))
            S.dma(S.q_sp, [yst], [], lambda e, h=h: e.dma_start(out=K.ymix[512 + h * 64:512 + (h + 1) * 64, :], in_=yst[:]))
    S.barrier()


def phase_s5(K, l):
    nc, S = K.nc, K.S
    pp = K.pp[l]
    TWO_PI = 6.283185307179586
    I32 = mybir.dt.int32
    with ExitStack() as es:
        sb, psum = mk_alloc(nc, es)
        cs = sb("s_cs", [128, 16, TT], F32)
        sn = sb("s_sn", [128, 16, TT], F32)
        Bre = sb("s_Bre", [128, 16, 128], BF16)
        Bim = sb("s_Bim", [128, 16, 128], BF16)
        Cfr = sb("s_Cfr", [128, 16, 128], BF16)
        nCfi = sb("s_nCfi", [128, 16, 128], BF16)
        nCfr = sb("s_nCfr", [128, 16, 128], BF16)
        diagD = sb("s_diagD", [128, 4, 128], BF16)
        wglu = sb("s_wglu", [128, 4, 512], BF16)
        rho = sb("s_rho", [128, 16], F32)
        c9 = [sb(f"s_ck{k}", [128, 16], F32) for k in range(10)]
        s9 = [sb(f"s_sk{k}", [128, 16], F32) for k in range(10)]
        ns9 = sb("s_ns9", [128, 16], F32)
        fr = sb("s_fr", [128, 16], F32)
        fi = sb("s_fi", [128, 16], F32)
        nfi = sb("s_nfi", [128, 16], F32)
        nfr = sb("s_nfr", [128, 16], F32)
        init_r = [sb(f"s_ir{i}", [128, 16], F32) for i in range(2)]
        init_i = [sb(f"s_ii{i}", [128, 16], F32) for i in range(2)]

        def dv(reads, writes, f):
            S.op(S.dve, reads, writes, f)

        S.dma(S.q_pool, [], [Bre], lambda e: e.dma_start(out=Bre[:], in_=K.bmat[l, 0]))
        S.dma(S.q_pool, [], [Bim], lambda e: e.dma_start(out=Bim[:], in_=K.bmat[l, 1]))
        S.dma(S.q_pool, [], [wglu], lambda e: e.dma_start(out=wglu[:], in_=K.w_glu[l].rearrange("(kc p) m -> p kc m", p=128)))
        for kt in range(4):
            dv([K.cb16, pp], [diagD], lambda kt=kt: nc.vector.tensor_scalar(
                out=diagD[:, kt, :], in0=K.cb16[:, 0:128], scalar1=pp[:, PP_SSMD + kt:PP_SSMD + kt + 1], scalar2=None, op0=ALU.mult))

        with ExitStack() as es2:
            sb2, _ = mk_alloc(nc, es2)
            def c16(n):
                return sb2("s_p_" + n, [128, 16], F32)
            dt, lrdt, th, q, kf, r, abr, abi, t1, t2, den, nr = [c16(n) for n in
                                                                 ("dt", "lrdt", "th", "q", "kf", "r", "abr", "abi", "t1", "t2", "den", "nr")]
            ki = sb2("s_p_ki", [128, 16], I32)
            Cre_f = sb2("s_Cre_f", [128, 16, 128], F32)
            Cim_f = sb2("s_Cim_f", [128, 16, 128], F32)
            ctmp = sb2("s_ctmp", [128, 128], F32)
            S.dma(S.q_sp, [], [Cre_f], lambda e: e.dma_start(out=Cre_f[:], in_=K.cmat[l, 0]))
            S.dma(S.q_sp, [], [Cim_f], lambda e: e.dma_start(out=Cim_f[:], in_=K.cmat[l, 1]))
            LR = pp[:, PP_LR:PP_LR + 16]
            LI = pp[:, PP_LI:PP_LI + 16]
            S.op(S.act, [pp], [dt], lambda: nc.scalar.activation(out=dt[:], in_=pp[:, PP_LDT:PP_LDT + 16], func=AF.Exp))
            dv([pp, dt], [lrdt], lambda: nc.vector.tensor_tensor(out=lrdt[:], in0=LR, in1=dt[:], op=ALU.mult))
            dv([pp, dt], [th], lambda: nc.vector.tensor_tensor(out=th[:], in0=LI, in1=dt[:], op=ALU.mult))
            S.op(S.act, [lrdt], [rho], lambda: nc.scalar.activation(out=rho[:], in_=lrdt[:], func=AF.Exp))

            C1, C2 = 6.28125, 0.0019353071795864769
            y_, y2, sA, cA, sB, cB = [c16(n) for n in ("y", "y2", "sA", "cA", "sB", "cB")]
            dv([th], [q], lambda: nc.vector.tensor_scalar(out=q[:], in0=th[:], scalar1=1.0 / TWO_PI, scalar2=None, op0=ALU.mult))
            dv([q], [ki], lambda: nc.vector.tensor_copy(out=ki[:], in_=q[:]))
            dv([ki], [kf], lambda: nc.vector.tensor_copy(out=kf[:], in_=ki[:]))
            dv([kf, th], [r], lambda: nc.vector.scalar_tensor_tensor(out=r[:], in0=kf[:], scalar=-C1, in1=th[:], op0=ALU.mult, op1=ALU.add))
            dv([kf, r], [r], lambda: nc.vector.scalar_tensor_tensor(out=r[:], in0=kf[:], scalar=-C2, in1=r[:], op0=ALU.mult, op1=ALU.add))
            dv([r], [y_], lambda: nc.vector.tensor_scalar(out=y_[:], in0=r[:], scalar1=0.125, scalar2=None, op0=ALU.mult))
            dv([y_], [y2], lambda: nc.vector.tensor_tensor(out=y2[:], in0=y_[:], in1=y_[:], op=ALU.mult))

            def horner(dst, coefs, last_mul, last_add):
                dv([y2], [dst], lambda: nc.vector.tensor_scalar(out=dst[:], in0=y2[:], scalar1=coefs[0], scalar2=None, op0=ALU.mult))
                for cf in coefs[1:]:
                    dv([dst, y2], [dst], lambda cf=cf: nc.vector.scalar_tensor_tensor(out=dst[:], in0=dst[:], scalar=cf, in1=y2[:], op0=ALU.add, op1=ALU.mult))
                if last_mul is not None:
                    dv([dst, last_mul], [dst], lambda: nc.vector.scalar_tensor_tensor(out=dst[:], in0=dst[:], scalar=last_add, in1=last_mul[:], op0=ALU.add, op1=ALU.mult))
                else:
                    dv([dst], [dst], lambda: nc.vector.tensor_scalar(out=dst[:], in0=dst[:], scalar1=last_add, scalar2=None, op0=ALU.add))
            horner(sA, [1.0 / 362880, -1.0 / 5040, 1.0 / 120, -1.0 / 6], y_, 1.0)
            horner(cA, [-1.0 / 3628800, 1.0 / 40320, -1.0 / 720, 1.0 / 24, -0.5], None, 1.0)
            cur_s, cur_c, nxt_s, nxt_c = sA, cA, sB, cB
            for dbl in range(3):
                fin = (dbl == 2)
                ds_ = s9[0] if fin else nxt_s
                dc_ = c9[0] if fin else nxt_c
                dv([cur_s], [t1], lambda cur_s=cur_s: nc.vector.tensor_tensor(out=t1[:], in0=cur_s[:], in1=cur_s[:], op=ALU.mult))
                dv([cur_s, cur_c], [ds_], lambda cur_s=cur_s, cur_c=cur_c, ds_=ds_: nc.vector.scalar_tensor_tensor(out=ds_[:], in0=cur_s[:], scalar=2.0, in1=cur_c[:], op0=ALU.mult, op1=ALU.mult))
                dv([t1], [dc_], lambda dc_=dc_: nc.vector.tensor_scalar(out=dc_[:], in0=t1[:], scalar1=-2.0, scalar2=1.0, op0=ALU.mult, op1=ALU.add))
                cur_s, cur_c, nxt_s, nxt_c = ds_, dc_, cur_s, cur_c
            dv([rho, c9[0]], [abr], lambda: nc.vector.tensor_tensor(out=abr[:], in0=rho[:], in1=c9[0][:], op=ALU.mult))
            dv([rho, s9[0]], [abi], lambda: nc.vector.tensor_tensor(out=abi[:], in0=rho[:], in1=s9[0][:], op=ALU.mult))
            dv([abr], [nr], lambda: nc.vector.tensor_scalar(out=nr[:], in0=abr[:], scalar1=-1.0, scalar2=None, op0=ALU.add))
            dv([pp], [t1], lambda: nc.vector.tensor_tensor(out=t1[:], in0=LR, in1=LR, op=ALU.mult))
            dv([pp], [t2], lambda: nc.vector.tensor_tensor(out=t2[:], in0=LI, in1=LI, op=ALU.mult))
            dv([t1, t2], [den], lambda: nc.vector.tensor_tensor(out=den[:], in0=t1[:], in1=t2[:], op=ALU.add))
            dv([den], [den], lambda: nc.vector.reciprocal(out=den[:], in_=den[:]))
            dv([nr, pp], [t1], lambda: nc.vector.tensor_tensor(out=t1[:], in0=nr[:], in1=LR, op=ALU.mult))
            dv([abi, pp], [t2], lambda: nc.vector.tensor_tensor(out=t2[:], in0=abi[:], in1=LI, op=ALU.mult))
            dv([t1, t2], [t1], lambda: nc.vector.tensor_tensor(out=t1[:], in0=t1[:], in1=t2[:], op=ALU.add))
            dv([t1, den], [fr], lambda: nc.vector.tensor_tensor(out=fr[:], in0=t1[:], in1=den[:], op=ALU.mult))
            dv([abi, pp], [t1], lambda: nc.vector.tensor_tensor(out=t1[:], in0=abi[:], in1=LR, op=ALU.mult))
            dv([nr, pp], [t2], lambda: nc.vector.tensor_tensor(out=t2[:], in0=nr[:], in1=LI, op=ALU.mult))
            dv([t1, t2], [t1], lambda: nc.vector.tensor_tensor(out=t1[:], in0=t1[:], in1=t2[:], op=ALU.subtract))
            dv([t1, den], [fi], lambda: nc.vector.tensor_tensor(out=fi[:], in0=t1[:], in1=den[:], op=ALU.mult))
            dv([fi], [nfi], lambda: nc.vector.tensor_scalar(out=nfi[:], in0=fi[:], scalar1=-1.0, scalar2=None, op0=ALU.mult))
            dv([fr], [nfr], lambda: nc.vector.tensor_scalar(out=nfr[:], in0=fr[:], scalar1=-1.0, scalar2=None, op0=ALU.mult))
            for j in range(16):
                dv([Cre_f, fr], [ctmp], lambda j=j: nc.vector.tensor_scalar(out=ctmp[:], in0=Cre_f[:, j, :], scalar1=fr[:, j:j + 1], scalar2=None, op0=ALU.mult))
                dv([Cim_f, nfi, ctmp], [Cfr], lambda j=j: nc.vector.scalar_tensor_tensor(out=Cfr[:, j, :], in0=Cim_f[:, j, :], scalar=nfi[:, j:j + 1], in1=ctmp[:], op0=ALU.mult, op1=ALU.add))
                dv([Cre_f, nfi], [ctmp], lambda j=j: nc.vector.tensor_scalar(out=ctmp[:], in0=Cre_f[:, j, :], scalar1=nfi[:, j:j + 1], scalar2=None, op0=ALU.mult))
                dv([Cim_f, nfr, ctmp], [nCfi], lambda j=j: nc.vector.scalar_tensor_tensor(out=nCfi[:, j, :], in0=Cim_f[:, j, :], scalar=nfr[:, j:j + 1], in1=ctmp[:], op0=ALU.mult, op1=ALU.add))
                dv([Cre_f, nfr], [ctmp], lambda j=j: nc.vector.tensor_scalar(out=ctmp[:], in0=Cre_f[:, j, :], scalar1=nfr[:, j:j + 1], scalar2=None, op0=ALU.mult))
                dv([Cim_f, fi, ctmp], [nCfr], lambda j=j: nc.vector.scalar_tensor_tensor(out=nCfr[:, j, :], in0=Cim_f[:, j, :], scalar=fi[:, j:j + 1], in1=ctmp[:], op0=ALU.mult, op1=ALU.add))
            tmpb = sb2("s_tmpb", [128, 16, TT // 2], F32)
            dv([], [cs], lambda: nc.vector.memset(cs[:, :, 0:1], 1.0))
            dv([], [sn], lambda: nc.vector.memset(sn[:, :, 0:1], 0.0))
            for k in range(9):
                n = 1 << k
                ck, sk = c9[k], s9[k]
                ckb = ck[:, :].unsqueeze(2).to_broadcast([128, 16, n])
                skb = sk[:, :].unsqueeze(2).to_broadcast([128, 16, n])
                lo, hi = slice(0, n), slice(n, 2 * n)
                dv([cs, ck], [cs], lambda ckb=ckb, lo=lo, hi=hi: nc.vector.tensor_tensor(out=cs[:, :, hi], in0=cs[:, :, lo], in1=ckb, op=ALU.mult))
                dv([sn, sk], [tmpb], lambda skb=skb, lo=lo, n=n: nc.vector.tensor_tensor(out=tmpb[:, :, 0:n], in0=sn[:, :, lo], in1=skb, op=ALU.mult))
                dv([cs, tmpb], [cs], lambda hi=hi, n=n: nc.vector.tensor_tensor(out=cs[:, :, hi], in0=cs[:, :, hi], in1=tmpb[:, :, 0:n], op=ALU.subtract))
                dv([cs, sk], [tmpb], lambda skb=skb, lo=lo, n=n: nc.vector.tensor_tensor(out=tmpb[:, :, 0:n], in0=cs[:, :, lo], in1=skb, op=ALU.mult))
                dv([sn, ck], [sn], lambda ckb=ckb, lo=lo, hi=hi: nc.vector.tensor_tensor(out=sn[:, :, hi], in0=sn[:, :, lo], in1=ckb, op=ALU.mult))
                dv([sn, tmpb], [sn], lambda hi=hi, n=n: nc.vector.tensor_tensor(out=sn[:, :, hi], in0=sn[:, :, hi], in1=tmpb[:, :, 0:n], op=ALU.add))
                last = 2 * n - 1
                csl, snl = cs[:, :, last], sn[:, :, last]
                dv([cs, c9[0]], [t1], lambda csl=csl: nc.vector.tensor_tensor(out=t1[:], in0=csl, in1=c9[0][:], op=ALU.mult))
                dv([sn, s9[0]], [t2], lambda snl=snl: nc.vector.tensor_tensor(out=t2[:], in0=snl, in1=s9[0][:], op=ALU.mult))
                dv([t1, t2], [c9[k + 1]], lambda k=k: nc.vector.tensor_tensor(out=c9[k + 1][:], in0=t1[:], in1=t2[:], op=ALU.subtract))
                dv([sn, c9[0]], [t1], lambda snl=snl: nc.vector.tensor_tensor(out=t1[:], in0=snl, in1=c9[0][:], op=ALU.mult))
                dv([cs, s9[0]], [t2], lambda csl=csl: nc.vector.tensor_tensor(out=t2[:], in0=csl, in1=s9[0][:], op=ALU.mult))
                dv([t1, t2], [s9[k + 1]], lambda k=k: nc.vector.tensor_tensor(out=s9[k + 1][:], in0=t1[:], in1=t2[:], op=ALU.add))
            dv([s9[9]], [ns9], lambda: nc.vector.tensor_scalar(out=ns9[:], in0=s9[9][:], scalar1=-1.0, scalar2=None, op0=ALU.mult))
        cL, sL = c9[9], s9[9]
        S.barrier()
        dump(K, 0, cs, cs[:, 0, :])
        dump(K, 1, sn, sn[:, 0, :])
        for i_, b_ in enumerate((rho, fr, fi, c9[0], s9[0], c9[9], s9[9])):
            dump(K, 9 + i_, b_, b_[:], 16)

        wts = [[sb(f"s_wt{i}_{k}", [128, TT], mybir.dt.float32r) for k in range(4)] for i in range(2)]
        xq = [[sb(f"s_xq{i}_{k}", [128, TT], BF16) for k in range(4)] for i in range(3)]
        ident_f0 = sb("s_identf0", [128, 128], F32)
        S.dma(S.q_sp, [], [ident_f0], lambda e: e.dma_start(out=ident_f0[:], in_=K.consts_d[:, 0:128]))
        ident_f = sb("s_identr", [128, 128], mybir.dt.float32r)
        S.op(S.dve, [ident_f0], [ident_f], lambda: nc.vector.tensor_copy(out=ident_f[:], in_=ident_f0[:]))
        zr = sb("s_zr", [128, TT], F32)
        zi = sb("s_zi", [128, TT], F32)
        zT = [sb(f"s_zT{i}", [128, 4, TT], BF16) for i in range(2)]
        sig = [sb(f"s_sig{i}", [128, TT], BF16) for i in range(4)]
        ycst = [sb(f"s_ycst{i}", [128, 4, TT], BF16) for i in range(2)]
        ps_br = [psum(f"s_psbr{i}") for i in range(1)]
        ps_bi = [psum(f"s_psbi{i}") for i in range(1)]
        btrs = [psum(f"s_psbtr{i}") for i in range(2)]
        btis = [psum(f"s_psbti{i}") for i in range(2)]
        ps_y = [psum(f"s_psy{i}") for i in range(1)]
        ps_g = [psum(f"s_psg{i}") for i in range(1)]
        dv([], [init_r[0]], lambda: nc.vector.memset(init_r[0][:], 0.0))
        dv([], [init_i[0]], lambda: nc.vector.memset(init_i[0][:], 0.0))
        uview = K.projT[2048:2560, :].rearrange("(kt p) t -> p kt t", p=128)
        uT = [sb(f"s_uT{i}", [128, 4, TT], BF16) for i in range(2)]
        NIT = NTT * 16
        ctmpA = [sb(f"s_ctA{i}", [128, 1], F32) for i in range(2)]
        ctmpB = [sb(f"s_ctB{i}", [128, 1], F32) for i in range(2)]
        st = {"iy": 0, "ig": 0}

        def load_u(tt):
            u = uT[tt % 2]
            S.dma(S.q_sp, [], [u], lambda e: e.dma_start(out=u[:], in_=uview[:, :, tt_sl(tt)]))

        def stageA(n):
            tt, j = divmod(n, 16)
            kt = j // 4
            u = uT[tt % 2]
            if j == 3 and tt + 1 < NTT:
                load_u(tt + 1)
            pbr, pbi = ps_br[0], ps_bi[0]
            w1, w2, w3, w4 = wts[n % 2]
            btr, bti = btrs[n % 2], btis[n % 2]
            S.mm([Bre, u], [pbr], [lambda: nc.tensor.matmul(pbr[:], Bre[:, j, :], u[:, kt, :], start=True, stop=True)])
            S.mm([Bim, u], [pbi], [lambda: nc.tensor.matmul(pbi[:], Bim[:, j, :], u[:, kt, :], start=True, stop=True)])

        def stageB(n):
            tt, j = divmod(n, 16)
            pbr, pbi = ps_br[0], ps_bi[0]
            w1, w2, w3, w4 = wts[n % 2]
            btr, bti = btrs[n % 2], btis[n % 2]
            csj, snj = cs[:, j, :], sn[:, j, :]
            dv([cs, pbr], [w1], lambda: nc.vector.tensor_tensor(out=w1[:], in0=csj, in1=pbr[:], op=ALU.mult))
            dv([sn, pbi], [w2], lambda: nc.vector.tensor_tensor(out=w2[:], in0=snj, in1=pbi[:], op=ALU.mult))
            dv([cs, pbi], [w3], lambda: nc.vector.tensor_tensor(out=w3[:], in0=csj, in1=pbi[:], op=ALU.mult))
            dv([sn, pbr], [w4], lambda: nc.vector.scalar_tensor_tensor(out=w4[:], in0=pbr[:], scalar=-1.0, in1=snj, op0=ALU.mult, op1=ALU.mult))
            S.mm([ident_f, w1, w2], [btr], [
                lambda: nc.tensor.matmul(btr[:], ident_f[:], w1[:], start=True, stop=False),
                lambda: nc.tensor.matmul(btr[:], ident_f[:], w2[:], start=False, stop=True)])
            S.mm([ident_f, w3, w4], [bti], [
                lambda: nc.tensor.matmul(bti[:], ident_f[:], w3[:], start=True, stop=False),
                lambda: nc.tensor.matmul(bti[:], ident_f[:], w4[:], start=False, stop=True)])

        def stageCd(n):
            tt, j = divmod(n, 16)
            kt = j // 4
            u = uT[tt % 2]
            btr, bti = btrs[n % 2], btis[n % 2]
            ir, ii = init_r[tt % 2], init_i[tt % 2]
            nir, nii = init_r[(tt + 1) % 2], init_i[(tt + 1) % 2]
            zt = zT[tt % 2]
            csj, snj = cs[:, j, :], sn[:, j, :]
            rb = rho[:, j:j + 1].to_broadcast([128, TT])
            dv([rho, btr, ir], [zr], lambda: nc.vector.tensor_tensor_scan(
                out=zr[:], data0=rb, data1=btr[:], initial=ir[:, j:j + 1], op0=ALU.mult, op1=ALU.add))
            dv([rho, bti, ii], [zi], lambda: nc.vector.tensor_tensor_scan(
                out=zi[:], data0=rb, data1=bti[:], initial=ii[:, j:j + 1], op0=ALU.mult, op1=ALU.add))
            if tt < NTT - 1:
                ca, cb_ = ctmpA[n % 2], ctmpB[n % 2]
                S.op(S.act, [zr, cL], [ca], lambda: nc.scalar.activation(out=ca[:], in_=zr[:, TT - 1:TT], func=AF.Identity, scale=cL[:, j:j + 1]))
                S.op(S.act, [zi, ns9, ca], [nir], lambda: nc.scalar.activation(out=nir[:, j:j + 1], in_=zi[:, TT - 1:TT], func=AF.Identity, scale=ns9[:, j:j + 1], bias=ca[:, 0:1]))
                S.op(S.act, [zr, sL], [cb_], lambda: nc.scalar.activation(out=cb_[:], in_=zr[:, TT - 1:TT], func=AF.Identity, scale=sL[:, j:j + 1]))
                S.op(S.act, [zi, cL, cb_], [nii], lambda: nc.scalar.activation(out=nii[:, j:j + 1], in_=zi[:, TT - 1:TT], func=AF.Identity, scale=cL[:, j:j + 1], bias=cb_[:, 0:1]))
            x1, x2, x3, x4 = xq[n % 3]
            dv([cs, zr], [x1], lambda: nc.vector.tensor_tensor(out=x1[:], in0=csj, in1=zr[:], op=ALU.mult))
            dv([sn, zi], [x2], lambda: nc.vector.tensor_tensor(out=x2[:], in0=snj, in1=zi[:], op=ALU.mult))
            dv([sn, zr], [x3], lambda: nc.vector.tensor_tensor(out=x3[:], in0=snj, in1=zr[:], op=ALU.mult))
            dv([cs, zi], [x4], lambda: nc.vector.tensor_tensor(out=x4[:], in0=csj, in1=zi[:], op=ALU.mult))

        def stageCp(n):
            tt, j = divmod(n, 16)
            kt = j // 4
            u = uT[tt % 2]
            zt = zT[tt % 2]
            x1, x2, x3, x4 = xq[n % 3]
            py = ps_y[0]
            mms = []
            if j % 4 == 0:
                mms.append(lambda: nc.tensor.matmul(py[:], diagD[:, kt, :], u[:, kt, :], start=True, stop=False))
            mms.append(lambda: nc.tensor.matmul(py[:], Cfr[:, j, :], x1[:], start=False, stop=False))
            mms.append(lambda: nc.tensor.matmul(py[:], nCfr[:, j, :], x2[:], start=False, stop=False))
            mms.append(lambda: nc.tensor.matmul(py[:], nCfi[:, j, :], x3[:], start=False, stop=False))
            mms.append(lambda: nc.tensor.matmul(py[:], nCfi[:, j, :], x4[:], start=False, stop=(j % 4 == 3)))
            S.mm([diagD, u, Cfr, nCfr, nCfi, x1, x2, x3, x4], [py], mms)
            if j % 4 == 3:
                S.op(S.act, [py], [zt], lambda: nc.scalar.activation(out=zt[:, kt, :], in_=py[:], func=AF.Gelu_apprx_tanh))
                st["iy"] += 1
            if j == 15:
                deferred.append(lambda tt=tt, zt=zt: emit_glu(tt, zt))

        def emit_glu(tt, zt):
            for mo in range(4):
                deferred3.append(lambda mo=mo: emit_glu_mo(tt, zt, mo))

        def emit_glu_mo(tt, zt, mo):
            pg = ps_g[0]
            sg = sig[mo]
            S.mm([wglu, zt], [pg], [(lambda kc=kc: nc.tensor.matmul(
                pg[:], wglu[:, kc, mo * 128:(mo + 1) * 128], zt[:, kc, :], start=(kc == 0), stop=(kc == 3))) for kc in range(4)])
            S.op(S.act, [pg, pp], [sg], lambda: nc.scalar.activation(
                out=sg[:], in_=pg[:], func=AF.Sigmoid, bias=pp[:, PP_BGLU + mo:PP_BGLU + mo + 1]))
            if mo == 3:
                deferred2.append(lambda: emit_glu_b(tt, zt))

        deferred3 = []

        def emit_glu_b(tt, zt):
            yc = ycst[tt % 2]
            for mo in range(4):
                sg = sig[mo]
                S.op(S.dve, [zt, sg], [yc], lambda mo=mo, sg=sg: nc.vector.tensor_tensor(
                    out=yc[:, mo, :], in0=zt[:, mo, :], in1=sg[:], op=ALU.mult))
            S.dma(S.q_sp, [yc], [], lambda e: e.dma_start(
                out=K.ymix[768:1280, :].rearrange("(mo p) t -> p mo t", p=128)[:, :, tt_sl(tt)], in_=yc[:]))

        deferred2 = []
        deferred = []
        load_u(0)
        stageA(0)
        stageB(0)
        for n in range(NIT):
            if n + 1 < NIT:
                stageA(n + 1)
                stageB(n + 1)
            if n >= 1:
                stageCp(n - 1)
            stageCd(n)
            if n % 16 == 2 and deferred:
                deferred.pop(0)()
            if n % 16 in (3, 4, 5, 6) and deferred3:
                deferred3.pop(0)()
            if n % 16 == 9 and deferred2:
                deferred2.pop(0)()
        stageCp(NIT - 1)
        while deferred:
            deferred.pop(0)()
        while deferred3:
            deferred3.pop(0)()
        while deferred2:
            deferred2.pop(0)()
    S.barrier()


def phase_merge(K, l, x_src):
    nc, S = K.nc, K.S
    with ExitStack() as es:
        sb, psum = mk_alloc(nc, es)
        wbr = sb("m_wbr", [128, 10, D], BF16)
        wout = sb("m_wout", [128, KC, D], BF16)
        yt = [sb(f"m_yt{i}", [128, 10, TT], BF16) for i in range(2)]
        gt = [sb(f"m_gt{i}", [128, 24, TT], BF16) for i in range(2)]
        xt = [sb(f"m_xt{i}", [128, KC, TT], F32) for i in range(2)]
        mg = [sb(f"m_mg{i}", [128, KC, TT], BF16) for i in range(2)]
        t1s = [sb(f"m_t1{i}", [128, TT], F32) for i in range(2)]
        t2s = [sb(f"m_t2{i}", [128, TT], F32) for i in range(2)]
        t3s = [sb(f"m_t3{i}", [128, TT], F32) for i in range(2)]
        pA = [psum(f"m_pA{i}") for i in range(2)]
        pB = [psum(f"m_pB{i}") for i in range(2)]
        pC = [psum(f"m_pC{i}") for i in range(2)]
        pO = [psum(f"m_pO{i}") for i in range(2)]
        S.dma(S.q_pool, [], [wbr], lambda e: e.dma_start(out=wbr[:, 0:4, :], in_=K.w_branch_a[l].rearrange("(kc p) m -> p kc m", p=128)))
        S.dma(S.q_pool, [], [wbr], lambda e: e.dma_start(out=wbr[:, 4:6, :], in_=K.w_branch_b[l].rearrange("(kc p) m -> p kc m", p=128)))
        S.dma(S.q_pool, [], [wbr], lambda e: e.dma_start(out=wbr[:, 6:10, :], in_=K.w_branch_c[l].rearrange("(kc p) m -> p kc m", p=128)))
        S.dma(S.q_pool, [], [wout], lambda e: e.dma_start(out=wout[:], in_=K.w_out[l].rearrange("(kc p) m -> p kc m", p=128)))
        yv = K.ymix.rearrange("(c p) t -> p c t", p=128)
        gv = K.projT[2560:5632, :].rearrange("(c p) t -> p c t", p=128)
        xv = x_src.rearrange("(c p) t -> p c t", p=128)
        xo = K.xres.rearrange("(c p) t -> p c t", p=128)
        st = {"im": 0, "io": 0}

        def load_yg(tt):
            y, g = yt[tt % 2], gt[tt % 2]
            S.dma(S.q_sp, [], [y], lambda e: e.dma_start(out=y[:], in_=yv[:, :, tt_sl(tt)]))
            S.dma(S.q_sp, [], [g], lambda e: e.dma_start(out=g[:], in_=gv[:, :, tt_sl(tt)]))

        def load_x(tt):
            x = xt[tt % 2]
            S.dma(S.q_sp, [], [x], lambda e: e.dma_start(out=x[:], in_=xv[:, :, tt_sl(tt)]))

        def branch(tt):
            y, g, m = yt[tt % 2], gt[tt % 2], mg[tt % 2]
            for mo in range(KC):
                im = st["im"]
                st["im"] += 1
                a, b, c = pA[im % 2], pB[im % 2], pC[im % 2]
                t1, t2, t3 = t1s[im % 2], t2s[im % 2], t3s[im % 2]
                ms = slice(mo * 128, (mo + 1) * 128)
                S.mm([wbr, y], [a], [(lambda kc=kc: nc.tensor.matmul(a[:], wbr[:, kc, ms], y[:, kc, :], start=(kc == 0), stop=(kc == 3))) for kc in range(0, 4)])
                S.mm([wbr, y], [b], [(lambda kc=kc: nc.tensor.matmul(b[:], wbr[:, kc, ms], y[:, kc, :], start=(kc == 4), stop=(kc == 5))) for kc in range(4, 6)])
                S.mm([wbr, y], [c], [(lambda kc=kc: nc.tensor.matmul(c[:], wbr[:, kc, ms], y[:, kc, :], start=(kc == 6), stop=(kc == 9))) for kc in range(6, 10)])
                S.op(S.dve, [a, g], [t1], lambda mo=mo: nc.vector.tensor_tensor(out=t1[:], in0=a[:], in1=g[:, mo, :], op=ALU.mult))
                S.op(S.dve, [b, g], [t2], lambda mo=mo: nc.vector.tensor_tensor(out=t2[:], in0=b[:], in1=g[:, 8 + mo, :], op=ALU.mult))
                S.op(S.dve, [t1, t2], [t1], lambda: nc.vector.tensor_tensor(out=t1[:], in0=t1[:], in1=t2[:], op=ALU.add))
                S.op(S.dve, [c, g], [t3], lambda mo=mo: nc.vector.tensor_tensor(out=t3[:], in0=c[:], in1=g[:, 16 + mo, :], op=ALU.mult))
                S.op(S.dve, [t1, t3], [m], lambda mo=mo: nc.vector.tensor_tensor(out=m[:, mo, :], in0=t1[:], in1=t3[:], op=ALU.add))

        def outproj(tt):
            x, m = xt[tt % 2], mg[tt % 2]
            for mo in range(KC):
                io = st["io"]
                st["io"] += 1
                o = pO[io % 2]
                ms = slice(mo * 128, (mo + 1) * 128)
                S.mm([wout, m], [o], [(lambda kc=kc: nc.tensor.matmul(o[:], wout[:, kc, ms], m[:, kc, :], start=(kc == 0), stop=(kc == KC - 1))) for kc in range(KC)])
                S.op(S.dve, [o, x], [x], lambda mo=mo: nc.vector.tensor_tensor(out=x[:, mo, :], in0=o[:], in1=x[:, mo, :], op=ALU.add))
            S.dma(S.q_act, [x], [], lambda e: e.dma_start(out=xo[:, :, tt_sl(tt)], in_=x[:]))

        load_yg(0)
        load_x(0)
        load_yg(1)
        load_x(1)
        branch(0)
        for tt in range(NTT):
            if tt + 1 < NTT:
                branch(tt + 1)
            if tt + 2 < NTT:
                load_yg(tt + 2)
            outproj(tt)
            if tt + 2 < NTT:
                load_x(tt + 2)
    S.barrier()


def phase_ffn_up(K, l):
    nc, S = K.nc, K.S
    pp = K.pp[l]
    NG = FFN // 128
    with ExitStack() as es:
        sb, psum = mk_alloc(nc, es)
        hT = [sb(f"f_hT{tt}", [128, KC, TT], BF16) for tt in range(NTT)]
        xv = K.xres.rearrange("(c p) t -> p c t", p=128)
        with ExitStack() as es2:
            sb2, psum2 = mk_alloc(nc, es2)
            xt = [sb2(f"f_xt{i}", [128, KC, TT], F32) for i in range(4)]
            sqs = [sb2(f"f_sq{i}", [128, KC, TT], BF16) for i in range(2)]
            rstd = [sb2(f"f_rstd{i}", [128, TT], F32) for i in range(2)]
            ps_n = [psum2(f"f_psn{i}") for i in range(2)]
            def load_x(tt):
                xb = xt[tt % 4]
                S.dma(S.q_sp, [], [xb], lambda e: e.dma_start(out=xb[:], in_=xv[:, :, tt_sl(tt)]))
            for tt in range(4):
                load_x(tt)
            for tt in range(NTT):
                xb = xt[tt % 4]
                rs = rstd[tt % 2]
                emit_rmsnorm_tile(K, (sqs[tt % 2], ps_n[tt % 2], rs), xb, None, None)
                for c in range(KC):
                    S.op(S.dve, [xb, rs, pp], [hT[tt]],
                         lambda c=c, xb=xb, rs=rs, tt=tt: nc.vector.scalar_tensor_tensor(
                             out=hT[tt][:, c, :], in0=xb[:, c, :], scalar=pp[:, PP_NFFN + c:PP_NFFN + c + 1],
                             in1=rs[:], op0=ALU.mult, op1=ALU.mult))
                if tt + 4 < NTT:
                    load_x(tt + 4)
            S.barrier()
        ps_g = [psum(f"f_psg{i}") for i in range(4)]
        ps_v = [psum(f"f_psv{i}") for i in range(4)]
        wg = [sb(f"f_wg{i}", [128, KC, 128], BF16) for i in range(2)]
        wv_ = [sb(f"f_wv{i}", [128, KC, 128], BF16) for i in range(2)]
        Ug = [sb(f"f_Ug{i}", [128, SEQ + 2], F32) for i in range(2)]
        Uv = [sb(f"f_Uv{i}", [128, SEQ + 2], F32) for i in range(2)]
        cgs = [sb(f"f_cg{i}", [128, TT], F32) for i in range(2)]
        cvs = [sb(f"f_cv{i}", [128, TT], F32) for i in range(2)]
        sgls = [sb(f"f_sgl{i}", [128, TT], F32) for i in range(2)]
        stage = [sb(f"f_st{i}", [128, SEQ], BF16) for i in range(2)]
        for ub in Ug + Uv:
            S.op(S.dve, [], [ub], lambda ub=ub: nc.vector.memset(ub[:, 0:2], 0.0))
        wup = K.w_up[l].rearrange("(c p) n -> p c n", p=128)
        ip = 0
        for fg in range(NG):
            fv = fg + NG
            wgb, wvb = wg[fg % 2], wv_[fg % 2]
            S.dma(S.q_pool, [], [wgb], lambda e, wgb=wgb, fg=fg: e.dma_start(out=wgb[:], in_=wup[:, :, fg * 128:(fg + 1) * 128]))
            S.dma(S.q_pool, [], [wvb], lambda e, wvb=wvb, fv=fv: e.dma_start(out=wvb[:], in_=wup[:, :, fv * 128:(fv + 1) * 128]))
            st = stage[fg % 2]
            ug, uv = Ug[fg % 2], Uv[fg % 2]
            for tt in range(NTT):
                pg, pv = ps_g[ip % 4], ps_v[ip % 4]
                cg, cv, sgl = cgs[ip % 2], cvs[ip % 2], sgls[ip % 2]
                ip += 1
                S.mm([hT[tt], wgb], [pg], [(lambda pg=pg, c=c, tt=tt, wgb=wgb: nc.tensor.matmul(pg[:], wgb[:, c, :], hT[tt][:, c, :], start=(c == 0), stop=(c == KC - 1))) for c in range(KC)])
                S.mm([hT[tt], wvb], [pv], [(lambda pv=pv, c=c, tt=tt, wvb=wvb: nc.tensor.matmul(pv[:], wvb[:, c, :], hT[tt][:, c, :], start=(c == 0), stop=(c == KC - 1))) for c in range(KC)])
                o = tt * TT
                for (p_, u_, c_, ch) in ((pg, ug, cg, fg), (pv, uv, cv, fv)):
                    w0 = pp[:, PP_CW + ch * 3 + 0:PP_CW + ch * 3 + 1]
                    w1 = pp[:, PP_CW + ch * 3 + 1:PP_CW + ch * 3 + 2]
                    w2 = pp[:, PP_CW + ch * 3 + 2:PP_CW + ch * 3 + 3]
                    bb = pp[:, PP_CB + ch:PP_CB + ch + 1]
                    S.op(S.act, [p_], [u_], lambda p_=p_, u_=u_, o=o: nc.scalar.activation(out=u_[:, o + 2:o + TT + 2], in_=p_[:], func=AF.Copy))
                    S.op(S.act, [p_, pp], [c_], lambda p_=p_, c_=c_, w2=w2, bb=bb: nc.scalar.activation(out=c_[:], in_=p_[:], func=AF.Identity, scale=w2, bias=bb))
                    S.op(S.dve, [u_, pp, c_], [c_], lambda u_=u_, c_=c_, w1=w1, o=o: nc.vector.scalar_tensor_tensor(out=c_[:], in0=u_[:, o + 1:o + TT + 1], scalar=w1, in1=c_[:], op0=ALU.mult, op1=ALU.add))
                    S.op(S.dve, [u_, pp, c_], [c_], lambda u_=u_, c_=c_, w0=w0, o=o: nc.vector.scalar_tensor_tensor(out=c_[:], in0=u_[:, o:o + TT], scalar=w0, in1=c_[:], op0=ALU.mult, op1=ALU.add))
                S.op(S.act, [cg], [sgl], lambda cg=cg, sgl=sgl: nc.scalar.activation(out=sgl[:], in_=cg[:], func=AF.Silu))
                S.op(S.dve, [sgl, cv], [st], lambda st=st, tt=tt, sgl=sgl, cv=cv: nc.vector.tensor_tensor(out=st[:, tt_sl(tt)], in0=sgl[:], in1=cv[:], op=ALU.mult))
            S.dma(S.q_sp, [st], [], lambda e, st=st, fg=fg: e.dma_start(out=K.gatedT[fg * 128:(fg + 1) * 128, :], in_=st[:]))
    S.barrier()


def phase_ffn_down(K, l):
    nc, S = K.nc, K.S
    NG = FFN // 128
    HALF = SEQ // 2
    with ExitStack() as es:
        sb, psum = mk_alloc(nc, es)
        gt = sb("d_gt", [128, NG, HALF], BF16)
        wd = [sb(f"d_wd{i}", [128, NG, 128], BF16) for i in range(2)]
        xc = [sb(f"d_xc{i}", [128, HALF], F32) for i in range(2)]
        ps = [psum(f"d_ps{i}") for i in range(4)]
        gv = K.gatedT.rearrange("(c p) t -> p c t", p=128)
        wdv = K.w_down[l].rearrange("(c p) m -> p c m", p=128)
        ip = 0
        iw = 0
        for half in range(2):
            hs = slice(half * HALF, (half + 1) * HALF)
            for q4 in range(2):
                S.dma(S.q_sp, [], [gt], lambda e, q4=q4, hs=hs: e.dma_start(out=gt[:, q4 * 11:(q4 + 1) * 11, :], in_=gv[:, q4 * 11:(q4 + 1) * 11, hs]))
            for mo in range(KC):
                w = wd[iw % 2]
                x = xc[iw % 2]
                iw += 1
                S.dma(S.q_pool, [], [w], lambda e, w=w, mo=mo: e.dma_start(out=w[:], in_=wdv[:, :, mo * 128:(mo + 1) * 128]))
                S.dma(S.q_sp, [], [x], lambda e, x=x, mo=mo, hs=hs: e.dma_start(out=x[:], in_=K.xres[mo * 128:(mo + 1) * 128, hs]))
                for t4 in range(HALF // TT):
                    p = ps[ip % 4]
                    ip += 1
                    ts_ = slice(t4 * TT, (t4 + 1) * TT)
                    S.mm([w, gt], [p], [(lambda p=p, c=c, w=w, ts_=ts_: nc.tensor.matmul(p[:], w[:, c, :], gt[:, c, ts_], start=(c == 0), stop=(c == NG - 1))) for c in range(NG)])
                    S.op(S.dve, [p, x], [x], lambda p=p, x=x, ts_=ts_: nc.vector.tensor_tensor(out=x[:, ts_], in0=p[:], in1=x[:, ts_], op=ALU.add))
                S.dma(S.q_act, [x], [], lambda e, x=x, mo=mo, hs=hs: e.dma_start(out=K.xres[mo * 128:(mo + 1) * 128, hs], in_=x[:]))
    S.barrier()


def phase_final(K):
    nc, S = K.nc, K.S
    with ExitStack() as es:
        sb, psum = mk_alloc(nc, es)
        xt = [sb(f"n_xt{i}", [128, KC, TT], F32) for i in range(4)]
        ot = [sb(f"n_ot{i}", [128, KC, TT], F32) for i in range(2)]
        sqs = [sb(f"n_sq{i}", [128, KC, TT], BF16) for i in range(2)]
        rstd = [sb(f"n_rstd{i}", [128, TT], F32) for i in range(2)]
        ps_n = [psum(f"n_psn{i}") for i in range(2)]
        xv = K.xres.rearrange("(c p) t -> p c t", p=128)
        ov = K.out.rearrange("(c p) t -> p c t", p=128)
        def load_x(tt):
            xb = xt[tt % 4]
            S.dma(S.q_sp, [], [xb], lambda e: e.dma_start(out=xb[:], in_=xv[:, :, tt_sl(tt)]))
        for tt in range(4):
            load_x(tt)
        for tt in range(NTT):
            xb, ob = xt[tt % 4], ot[tt % 2]
            rs = rstd[tt % 2]
            emit_rmsnorm_tile(K, (sqs[tt % 2], ps_n[tt % 2], rs), xb, None, None)
            for c in range(KC):
                S.op(S.dve, [xb, rs, K.nf], [ob],
                     lambda c=c, xb=xb, rs=rs, ob=ob: nc.vector.scalar_tensor_tensor(
                         out=ob[:, c, :], in0=xb[:, c, :], scalar=K.nf[:, c:c + 1], in1=rs[:], op0=ALU.mult, op1=ALU.mult))
            S.dma(S.q_act, [ob], [], lambda e, ob=ob, tt=tt: e.dma_start(out=ov[:, :, tt_sl(tt)], in_=ob[:]))
            if tt + 4 < NTT:
                load_x(tt + 4)
    S.barrier()


PHASES = ("p1", "attn_a", "attn_b", "s5", "merge", "ffn_up", "ffn_down")


def build_program():
    nc = bass.Bass("TRN2", target_bir_lowering=False)
    K = Ctx()
    K.nc = nc
    K.S = Sched(nc)
    S = K.S

    def din(name, shape, dt=F32):
        return nc.dram_tensor(name, shape, dt, kind="ExternalInput").ap()

    def dscr(name, shape, dt):
        kind = "ExternalOutput" if name in DEBUG else "Internal"
        return nc.dram_tensor(name, shape, dt, kind=kind).ap()

    K.xT = din("xT", [D, SEQ])
    K.w_in = din("w_in", [DEPTH, D, INW])
    K.pp_d = din("pp", [DEPTH, 128, PPW])
    K.nf_d = din("nf", [128, 8])
    K.consts_d = din("consts", [128, 896])
    K.bmat = din("bmat", [DEPTH, 2, 128, 16, 128])
    K.cmat = din("cmat", [DEPTH, 2, 128, 16, 128])
    K.w_glu = din("w_glu", [DEPTH, 512, 512])
    K.w_branch_a = din("w_branch_a", [DEPTH, 512, D])
    K.w_branch_b = din("w_branch_b", [DEPTH, 256, D])
    K.w_branch_c = din("w_branch_c", [DEPTH, 512, D])
    K.w_out = din("w_out", [DEPTH, D, D])
    K.w_up = din("w_up", [DEPTH, D, 2 * FFN])
    K.w_down = din("w_down", [DEPTH, FFN, D])
    K.out = nc.dram_tensor("outT", [D, SEQ], F32, kind="ExternalOutput").ap()
    K.projT = dscr("projT", [INW, SEQ], BF16)
    K.va_tok = dscr("va_tok", [SEQ, 128], BF16)
    K.vb_tok = dscr("vb_tok", [SEQ, 256], BF16)
    K.ymix = dscr("ymix", [1280, SEQ], BF16)
    K.xres = dscr("xres", [D, SEQ], F32)
    K.gatedT = dscr("gatedT", [FFN, SEQ], BF16)
    K.dbg = {}
    if "dbg" in DEBUG:
        K.dbgbuf = nc.dram_tensor("dbgbuf", [20, 128, 512], F32, kind="ExternalOutput").ap()
    if "h" in DEBUG:
        K.dbg["h"] = nc.dram_tensor("dbg_h", [D, SEQ], BF16, kind="ExternalOutput").ap()

    K.PP_NMIX = PP_NMIX
    K.ones_f = Buf(nc.alloc_sbuf_tensor("ones_f", [128, 128], F32))
    K.ones_b = Buf(nc.alloc_sbuf_tensor("ones_b", [128, 128], BF16))
    K.eps_col = Buf(nc.alloc_sbuf_tensor("eps_col", [128, 1], F32))
    K.cb16 = Buf(nc.alloc_sbuf_tensor("cb16", [128, 512], BF16))
    K.maskc4 = Buf(nc.alloc_sbuf_tensor("maskc4", [128, 512], BF16))
    K.maskpA4 = Buf(nc.alloc_sbuf_tensor("maskpA4", [128, 512], BF16))
    K.nf = Buf(nc.alloc_sbuf_tensor("nf_sb", [128, 8], F32))
    K.pp = [Buf(nc.alloc_sbuf_tensor(f"pp_sb{l}", [128, PPW], F32)) for l in range(DEPTH)]
    S.op(S.dve, [], [K.ones_f], lambda: nc.vector.memset(K.ones_f[:], 1.0))
    S.op(S.dve, [], [K.ones_b], lambda: nc.vector.memset(K.ones_b[:], 1.0))
    S.op(S.dve, [], [K.eps_col], lambda: nc.vector.memset(K.eps_col[:], EPS))
    S.dma(S.q_pool, [], [K.cb16], lambda e: e.dma_start(out=K.cb16[:], in_=K.consts_d[:, 0:512]))
    K.m01 = Buf(nc.alloc_sbuf_tensor("m01", [128, 256], BF16))
    S.dma(S.q_pool, [], [K.m01], lambda e: e.dma_start(out=K.m01[:], in_=K.consts_d[:, 512:768]))
    S.dma(S.q_sp, [], [K.nf], lambda e: e.dma_start(out=K.nf[:], in_=K.nf_d[:, :]))
    for l in range(DEPTH):
        S.dma(S.q_sp, [], [K.pp[l]], lambda e, l=l: e.dma_start(out=K.pp[l][:], in_=K.pp_d[l]))
    K.m01pA = Buf(nc.alloc_sbuf_tensor("m01pA", [128, 128], BF16))
    S.dma(S.q_pool, [], [K.m01pA], lambda e: e.dma_start(out=K.m01pA[:], in_=K.consts_d[:, 768:896]))
    K.m01c4 = Buf(nc.alloc_sbuf_tensor("m01c4", [128, 512], BF16))
    K.m01pA4 = Buf(nc.alloc_sbuf_tensor("m01pA4", [128, 512], BF16))
    for h in range(4):
        S.op(S.dve, [K.m01], [K.m01c4], lambda h=h: nc.vector.tensor_copy(out=K.m01c4[:, h * 128:(h + 1) * 128], in_=K.m01[:, 0:128]))
        S.op(S.dve, [K.m01pA], [K.m01pA4], lambda h=h: nc.vector.tensor_copy(out=K.m01pA4[:, h * 128:(h + 1) * 128], in_=K.m01pA[:]))
    for h in range(4):
        S.op(S.dve, [K.cb16], [K.maskc4], lambda h=h: nc.vector.tensor_copy(out=K.maskc4[:, h * 128:(h + 1) * 128], in_=K.cb16[:, 128:256]))
        S.op(S.dve, [K.cb16], [K.maskpA4], lambda h=h: nc.vector.tensor_copy(out=K.maskpA4[:, h * 128:(h + 1) * 128], in_=K.cb16[:, 256:384]))

    fns = {"p1": phase_p1, "attn_a": phase_attn_a, "attn_b": phase_attn_b, "s5": phase_s5, "merge": phase_merge,
           "ffn_up": phase_ffn_up, "ffn_down": phase_ffn_down}
    done = False
    for l in range(DEPTH):
        x_src = K.xT if l == 0 else K.xres
        for ph in PHASES:
            if ONLY is not None and (l, ph) not in ONLY:
                continue
            if ph in ("p1", "merge"):
                fns[ph](K, l, x_src)
            else:
                fns[ph](K, l)
            if STOP_AFTER is not None and (l, ph) == tuple(STOP_AFTER):
                done = True
                break
        if done:
            break
    if not done and ONLY is None:
        phase_final(K)
    S.barrier()
    return nc, K


ONLY = None


def prep_inputs(inputs):
    f = lambda k: np.asarray(inputs[k], dtype=np.float32)
    x = f("x")
    common = {}
    for k in ("w_in", "w_glu", "w_branch_a", "w_branch_b", "w_branch_c", "w_out", "w_up", "w_down"):
        common[k] = np.ascontiguousarray(f(k))
    pp = np.zeros((DEPTH, 128, PPW), np.float32)
    bmat = np.zeros((DEPTH, 2, 128, 16, 128), np.float32)
    cmat = np.zeros((DEPTH, 2, 128, 16, 128), np.float32)
    for l in range(DEPTH):
        pp[l, :, PP_NMIX:PP_NMIX + 8] = f("norm_mix")[l].reshape(8, 128).T
        pp[l, :, PP_NFFN:PP_NFFN + 8] = f("norm_ffn")[l].reshape(8, 128).T
        cw = f("conv_w")[l]
        pp[l, :, PP_CW:PP_CW + 132] = cw.reshape(3, 44, 128).transpose(2, 1, 0).reshape(128, 132)
        pp[l, :, PP_CB:PP_CB + 44] = f("conv_b")[l].reshape(44, 128).T
        pp[l, :, PP_BGLU:PP_BGLU + 4] = f("b_glu")[l].reshape(4, 128).T
        pp[l, :, PP_SSMD:PP_SSMD + 4] = f("ssm_d")[l].reshape(4, 128).T
        pp[l, :, PP_SINK:PP_SINK + 8] = np.broadcast_to(f("attn_sinks")[l][None, :], (128, 8))
        pp[l, :, PP_LR:PP_LR + 16] = f("ssm_lambda_re")[l].reshape(16, 128).T
        pp[l, :, PP_LI:PP_LI + 16] = f("ssm_lambda_im")[l].reshape(16, 128).T
        pp[l, :, PP_LDT:PP_LDT + 16] = np.repeat(f("ssm_log_dt")[l], 64).reshape(16, 128).T
        bre, bim = f("ssm_b_re")[l], f("ssm_b_im")[l]
        cre, cim = f("ssm_c_re")[l], f("ssm_c_im")[l]
        for g in range(32):
            j = g // 2
            ks = slice((g % 8) * 16, (g % 8) * 16 + 16)
            ms = slice((g % 2) * 64, (g % 2) * 64 + 64)
            bmat[l, 0, ks, j, ms] = bre[g].T
            bmat[l, 1, ks, j, ms] = bim[g].T
            cmat[l, 0, ms, j, ks] = cre[g].T
            cmat[l, 1, ms, j, ks] = cim[g].T
    common["pp"] = pp
    common["bmat"] = bmat
    common["cmat"] = cmat
    common["nf"] = np.ascontiguousarray(f("norm_final").reshape(8, 128).T)
    k_idx = np.arange(128)[:, None]
    q_idx = np.arange(128)[None, :]
    consts = np.zeros((128, 896), np.float32)
    consts[:, 768:896] = np.where(k_idx >= q_idx + 1, 1.0, 0.0)
    consts[:, 512:640] = np.where(k_idx <= q_idx, 1.0, 0.0)
    consts[:, 640:768] = np.where(k_idx >= q_idx, 1.0, 0.0)
    consts[:, 0:128] = np.eye(128, dtype=np.float32)
    consts[:, 128:256] = np.where(k_idx <= q_idx, 0.0, NEG)
    consts[:, 256:384] = np.where(k_idx >= q_idx + 1, 0.0, NEG)
    consts[:, 384:512] = np.where(k_idx >= q_idx, 0.0, NEG)
    common["consts"] = consts
    in_maps = []
    for b in range(NCORES):
        m = dict(common)
        m["xT"] = np.ascontiguousarray(x[b].T)
        in_maps.append(m)
    return in_maps


def kernel(**inputs):
    nc, K = build_program()
    in_maps = prep_inputs(inputs)
    res = run_bass_kernel_spmd(nc, in_maps, core_ids=list(range(NCORES)))
    outs = [np.asarray(r["outT"]).T for r in res.results]
    return np.ascontiguousarray(np.stack(outs, axis=0).astype(np.float32))
```

```python
import numpy as np
from contextlib import ExitStack
import concourse.bass as bass
import concourse.mybir as mybir
from concourse.bass_utils import run_bass_kernel_spmd

F32 = mybir.dt.float32
BF16 = mybir.dt.bfloat16
AF = mybir.ActivationFunctionType
ALU = mybir.AluOpType

D = 1024
SEQ = 4096
DEPTH = 2
NCORES = 8
INW = 5632
FFN = 2816
EPS = 1e-6
TT = 512
NTT = SEQ // TT
KC = D // 128

DEBUG = {}
STOP_AFTER = None


class Buf:
    __slots__ = ("t", "w", "r", "name")

    def __init__(self, t, name=""):
        self.t = t
        self.w = None
        self.r = {}
        self.name = name

    def __getitem__(self, k):
        return self.t[k]


class Eng:
    def __init__(self, name, eng, sem):
        self.name = name
        self.eng = eng
        self.sem = sem
        self.known = {}


class DmaQ:
    def __init__(self, S, E, k, name):
        self.S = S
        self.E = E
        self.sems = [S.new_sem(f"dq_{name}{i}") for i in range(k)]
        self.n = 0


class Sched:
    def __init__(self, nc):
        self.nc = nc
        self.sems = []
        self.issued = []
        self.pe = Eng("pe", nc.tensor, self.new_sem("c_pe"))
        self.act = Eng("act", nc.scalar, self.new_sem("c_act"))
        self.dve = Eng("dve", nc.vector, self.new_sem("c_dve"))
        self.pool = Eng("pool", nc.gpsimd, self.new_sem("c_pool"))
        self.sp = Eng("sp", nc.sync, None)
        self.engs = [self.pe, self.act, self.dve, self.pool, self.sp]
        self.q_sp = DmaQ(self, self.sp, 8, "sp")
        self.q_pool = DmaQ(self, self.pool, 8, "pool")
        self.q_act = DmaQ(self, self.act, 6, "act")
        self.nops = 0

    def new_sem(self, name):
        self.sems.append(self.nc.alloc_semaphore(name))
        self.issued.append(0)
        return len(self.sems) - 1

    def wait(self, E, s, v):
        if v > 0 and E.known.get(s, 0) < v:
            E.eng.wait_ge(self.sems[s], v)
            E.known[s] = v

    def _sync(self, E, reads, writes, is_dma):
        need = {}

        def add(ev):
            if ev is not None and ev[1] > need.get(ev[0], 0):
                need[ev[0]] = ev[1]

        for b in reads:
            add(b.w)
        for b in writes:
            if b.w is not None and (is_dma or b.w[0] != E.sem):
                add(b.w)
            for s, v in b.r.items():
                if is_dma or s != E.sem:
                    add((s, v))
        for s, v in need.items():
            self.wait(E, s, v)

    def _mark(self, ev, reads, writes):
        s, v = ev
        for b in reads:
            if b.r.get(s, 0) < v:
                b.r[s] = v
        for b in writes:
            b.w = ev
            b.r = {}

    def op(self, E, reads, writes, emit):
        self._sync(E, reads, writes, False)
        ins = emit()
        self.issued[E.sem] += 1
        ins.then_inc(self.sems[E.sem], 1)
        self._mark((E.sem, self.issued[E.sem]), reads, writes)
        self.nops += 1

    def mm(self, reads, writes, emits):
        E = self.pe
        self._sync(E, reads, writes, False)
        ins = None
        for e in emits:
            ins = e()
            self.nops += 1
        self.issued[E.sem] += 1
        ins.then_inc(self.sems[E.sem], 1)
        self._mark((E.sem, self.issued[E.sem]), reads, writes)

    def dma(self, Q, reads, writes, emit):
        E = Q.E
        s = Q.sems[Q.n % len(Q.sems)]
        Q.n += 1
        self.wait(E, s, self.issued[s])
        self._sync(E, reads, writes, True)
        ins = emit(E.eng)
        self.issued[s] += 16
        ins.then_inc(self.sems[s], 16)
        self._mark((s, self.issued[s]), reads, writes)
        self.nops += 1

    def barrier(self):
        for E in self.engs:
            for s in range(len(self.sems)):
                self.wait(E, s, self.issued[s])


class Ctx:
    pass


def tt_sl(tt):
    return slice(tt * TT, (tt + 1) * TT)


def emit_rmsnorm_tile(K, es_bufs, xt, gain_ap_fn, out_fn):
    S, nc = K.S, K.nc
    sq, ps, rstd = es_bufs
    S.op(S.act, [xt], [sq], lambda: nc.scalar.activation(out=sq[:], in_=xt[:], func=AF.Square))
    S.mm([sq, K.ones_b], [ps],
         [(lambda c=c: nc.tensor.matmul(ps[:], K.ones_b[:], sq[:, c, :], start=(c == 0), stop=(c == KC - 1)))
          for c in range(KC)])
    S.op(S.act, [ps], [rstd], lambda: nc.scalar.activation(out=rstd[:], in_=ps[:], func=AF.Ln,
                                                             scale=1.0 / D, bias=K.eps_col[:, 0:1]))
    S.op(S.act, [rstd], [rstd], lambda: nc.scalar.activation(out=rstd[:], in_=rstd[:], func=AF.Exp, scale=-0.5))


def phase_p1(K, l, x_src):
    nc, S = K.nc, K.S
    with ExitStack() as es:
        sb, psum = mk_alloc(nc, es)

        hT = [sb(f"p1_hT{tt}", [128, KC, TT], BF16) for tt in range(NTT)]
        g = K.pp[l]
        xv = x_src.rearrange("(c p) t -> p c t", p=128)
        with ExitStack() as es2:
            sb2, psum2 = mk_alloc(nc, es2)
            xt = [sb2(f"p1_xt{i}", [128, KC, TT], F32) for i in range(4)]
            sqs = [sb2(f"p1_sq{i}", [128, KC, TT], BF16) for i in range(2)]
            rstd = [sb2(f"p1_rstd{i}", [128, TT], F32) for i in range(2)]
            ps_n = [psum2(f"p1_psn{i}") for i in range(2)]
            def load_x(tt):
                xb = xt[tt % 4]
                S.dma(S.q_sp, [], [xb], lambda e: e.dma_start(out=xb[:], in_=xv[:, :, tt_sl(tt)]))
            for tt in range(4):
                load_x(tt)
            for tt in range(NTT):
                xb = xt[tt % 4]
                rs = rstd[tt % 2]
                emit_rmsnorm_tile(K, (sqs[tt % 2], ps_n[tt % 2], rs), xb, None, None)
                for c in range(KC):
                    S.op(S.dve, [xb, rs, g], [hT[tt]],
                         lambda c=c, xb=xb, rs=rs, tt=tt: nc.vector.scalar_tensor_tensor(
                             out=hT[tt][:, c, :], in0=xb[:, c, :], scalar=g[:, PP_NMIX + c:PP_NMIX + c + 1],
                             in1=rs[:], op0=ALU.mult, op1=ALU.mult))
                if tt + 4 < NTT:
                    load_x(tt + 4)
            S.barrier()
        ps_m = [psum(f"p1_psm{i}") for i in range(4)]
        wbuf = [sb(f"p1_w{i}", [128, KC, 512], BF16) for i in range(2)]
        stage = [sb(f"p1_st{i}", [128, SEQ], BF16) for i in range(2)]
        vst = sb("p1_vst", [128, 32, 384], BF16)

        w_in = K.w_in[l]
        wv = w_in.rearrange("(c p) n -> p c n", p=128)
        wvb = wbuf[0]
        S.dma(S.q_pool, [], [wvb], lambda e: e.dma_start(out=wvb[:, :, 0:128], in_=wv[:, :, 640:768]))
        S.dma(S.q_pool, [], [wvb], lambda e: e.dma_start(out=wvb[:, :, 128:384], in_=wv[:, :, 1792:2048]))
        pi = 0
        for blk in range(32):
            tt, o = blk // 4, (blk % 4) * 128
            ps = ps_m[pi % 4]
            pi += 1
            S.mm([hT[tt], wvb], [ps],
                 [(lambda c=c, tt=tt, o=o, ps=ps: nc.tensor.matmul(ps[:, 0:384], hT[tt][:, c, o:o + 128], wvb[:, c, 0:384],
                                                                   start=(c == 0), stop=(c == KC - 1)))
                  for c in range(KC)])
            S.op(S.act, [ps], [vst], lambda blk=blk, ps=ps: nc.scalar.activation(out=vst[:, blk, :], in_=ps[:, 0:384], func=AF.Copy))
        S.dma(S.q_act, [vst], [], lambda e: e.dma_start(out=K.va_tok.rearrange("(b p) f -> p b f", p=128), in_=vst[:, :, 0:128]))
        S.dma(S.q_act, [vst], [], lambda e: e.dma_start(out=K.vb_tok.rearrange("(b p) f -> p b f", p=128), in_=vst[:, :, 128:384]))

        ci = 0
        for grp in range(INW // 512):
            wb = wbuf[(grp + 1) % 2]
            S.dma(S.q_pool, [], [wb], lambda e, wb=wb, grp=grp: e.dma_start(out=wb[:], in_=wv[:, :, grp * 512:(grp + 1) * 512]))
            for j in range(4):
                ch = grp * 4 + j
                if ch in (5, 14, 15):
                    continue
                st = stage[ci % 2]
                ci += 1
                func = AF.Sigmoid if ch >= 20 else AF.Copy
                for tt in range(NTT):
                    ps = ps_m[pi % 4]
                    pi += 1
                    S.mm([hT[tt], wb], [ps],
                         [(lambda c=c, tt=tt, j=j, ps=ps, wb=wb: nc.tensor.matmul(
                             ps[:], wb[:, c, j * 128:(j + 1) * 128], hT[tt][:, c, :], start=(c == 0), stop=(c == KC - 1)))
                          for c in range(KC)])
                    S.op(S.act, [ps], [st], lambda tt=tt, ps=ps, st=st, func=func: nc.scalar.activation(
                        out=st[:, tt_sl(tt)], in_=ps[:], func=func))
                S.dma(S.q_sp, [st], [], lambda e, st=st, ch=ch: e.dma_start(out=K.projT[ch * 128:(ch + 1) * 128, :], in_=st[:]))
    S.barrier()


NEG = -30000.0
PP_NMIX, PP_NFFN, PP_CW, PP_CB, PP_BGLU, PP_SSMD, PP_SINK, PP_LR, PP_LI, PP_LDT = 0, 8, 16, 148, 192, 196, 200, 208, 224, 240
PPW = 256


_UID = [0]


def mk_alloc(nc, es):
    _UID[0] += 1
    uid = _UID[0]

    def sb(name, shape, dt):
        return Buf(es.enter_context(nc.sbuf_tensor(f"{name}_u{uid}", shape, dt)), name)

    def psum(name):
        return Buf(es.enter_context(nc.psum_tensor(f"{name}_u{uid}", [128, 512], F32)), name)
    return sb, psum


def dump(K, slot, buf, ap, cols=512):
    if "dbg" not in DEBUG:
        return
    K.S.dma(K.S.q_pool, [buf], [], lambda e: e.dma_start(out=K.dbgbuf[slot, :, 0:cols], in_=ap))


def phase_attn_a(K, l):
    nc, S = K.nc, K.S
    pp = K.pp[l]
    with ExitStack() as es:
        sb, psum = mk_alloc(nc, es)
        KTs = [sb(f"a_KT{i}", [64, SEQ], BF16) for i in range(2)]
        QTs = [sb(f"a_QT{i}", [64, 4, SEQ], BF16) for i in range(2)]
        Vs = [sb(f"a_V{i}", [128, 32, 64], BF16) for i in range(2)]
        yst = sb("a_yst", [64, 4, SEQ], BF16)
        esink = sb("a_es", [128, 8], F32)
        sinkt = sb("a_sinkt", [64, 8, 128], F32)
        zeros = sb("a_zeros", [64, 128], F32)
        PTc = [sb(f"a_PTc{i}", [128, 512], BF16) for i in range(2)]
        PTp = [sb(f"a_PTp{i}", [128, 512], BF16) for i in range(2)]
        tmp = [sb(f"a_tmp{i}", [64, 512], F32) for i in range(2)]
        ps_c = [psum(f"a_psc{i}") for i in range(2)]
        ps_p = [psum(f"a_psp{i}") for i in range(2)]
        ps_o = [psum(f"a_pso{i}") for i in range(2)]
        ps_l = [psum(f"a_psl{i}") for i in range(2)]

        S.op(S.act, [pp], [esink], lambda: nc.scalar.activation(out=esink[:], in_=pp[:, PP_SINK:PP_SINK + 8], func=AF.Exp))
        S.op(S.dve, [], [zeros], lambda: nc.vector.memset(zeros[:], 0.0))
        for h8 in range(8):
            S.op(S.dve, [zeros, esink], [sinkt], lambda h8=h8: nc.vector.tensor_scalar(
                out=sinkt[:, h8, :], in0=zeros[:], scalar1=esink[0:64, h8:h8 + 1], scalar2=None, op0=ALU.add))

        it = 0
        pend = None
        for g in range(2):
            KT, QT, V = KTs[g], QTs[g], Vs[g]
            S.dma(S.q_sp, [], [KT], lambda e, g=g, KT=KT: e.dma_start(out=KT[:], in_=K.projT[512 + g * 64:512 + (g + 1) * 64, :]))
            S.dma(S.q_sp, [], [QT], lambda e, g=g, QT=QT: e.dma_start(
                out=QT[:], in_=K.projT[g * 256:(g + 1) * 256, :].rearrange("(h d) t -> d h t", d=64)))
            S.dma(S.q_sp, [], [V], lambda e, g=g, V=V: e.dma_start(
                out=V[:], in_=K.va_tok[:, g * 64:(g + 1) * 64].rearrange("(b p) f -> p b f", p=128)))
        for g in range(2):
            KT, QT, V = KTs[g], QTs[g], Vs[g]
            for b in range(32):
                i2 = it % 2
                it += 1
                q_ap = QT[:, :, b * 128:(b + 1) * 128]
                pc, pp_, po, pl = ps_c[i2], ps_p[i2], ps_o[i2], ps_l[i2]
                ptc, ptp, tm = PTc[i2], PTp[i2], tmp[i2]

                def s1(b=b, q_ap=q_ap, pc=pc, pp_=pp_, ptc=ptc, ptp=ptp, KT=KT, QT=QT):
                    S.mm([KT, QT], [pc], [
                        lambda: nc.tensor.matmul(pc[:].rearrange("p (h q) -> p h q", h=4), KT[:, b * 128:(b + 1) * 128], q_ap, start=True, stop=True)])
                    S.op(S.act, [pc], [ptc], lambda: nc.scalar.activation(out=ptc[:], in_=pc[:], func=AF.Exp, scale=0.125))
                    S.op(S.dve, [ptc, K.m01c4], [ptc], lambda: nc.vector.tensor_tensor(out=ptc[:], in0=ptc[:], in1=K.m01c4[:], op=ALU.mult))
                    if b > 0:
                        S.mm([KT, QT], [pp_], [
                            lambda: nc.tensor.matmul(pp_[:].rearrange("p (h q) -> p h q", h=4), KT[:, (b - 1) * 128:b * 128], q_ap, start=True, stop=True)])
                        S.op(S.act, [pp_], [ptp], lambda: nc.scalar.activation(out=ptp[:], in_=pp_[:], func=AF.Exp, scale=0.125))
                        S.op(S.dve, [ptp, K.m01pA4], [ptp], lambda: nc.vector.tensor_tensor(out=ptp[:], in0=ptp[:], in1=K.m01pA4[:], op=ALU.mult))

                def s2(b=b, g=g, po=po, pl=pl, ptc=ptc, ptp=ptp, tm=tm, V=V):
                    if b > 0:
                        S.mm([V, ptc, ptp], [po], [
                            lambda: nc.tensor.matmul(po[0:64, :], V[:, b - 1, :], ptp[:], start=True, stop=False),
                            lambda: nc.tensor.matmul(po[0:64, :], V[:, b, :], ptc[:], start=False, stop=True)])
                        S.mm([K.ones_b, ptc, ptp], [pl], [
                            lambda: nc.tensor.matmul(pl[0:64, :], K.ones_b[:, 0:64], ptp[:], start=True, stop=False),
                            lambda: nc.tensor.matmul(pl[0:64, :], K.ones_b[:, 0:64], ptc[:], start=False, stop=True)])
                    else:
                        S.mm([V, ptc], [po], [lambda: nc.tensor.matmul(po[0:64, :], V[:, b, :], ptc[:], start=True, stop=True)])
                        S.mm([K.ones_b, ptc], [pl], [lambda: nc.tensor.matmul(pl[0:64, :], K.ones_b[:, 0:64], ptc[:], start=True, stop=True)])
                    S.op(S.dve, [pl, sinkt], [tm], lambda: nc.vector.tensor_tensor(
                        out=tm[:], in0=pl[0:64, :], in1=sinkt[:, g * 4:(g + 1) * 4, :].rearrange("p h q -> p (h q)"), op=ALU.add))
                    S.op(S.act, [tm], [tm], lambda: nc.scalar.activation(out=tm[:], in_=tm[:], func=AF.Ln))
                    S.op(S.act, [tm], [tm], lambda: nc.scalar.activation(out=tm[:], in_=tm[:], func=AF.Exp, scale=-1.0))
                    S.op(S.dve, [po, tm], [yst], lambda: nc.vector.tensor_tensor(
                        out=yst[:, :, b * 128:(b + 1) * 128], in0=po[0:64, :].rearrange("p (h q) -> p h q", h=4),
                        in1=tm[:].rearrange("p (h q) -> p h q", h=4), op=ALU.mult))
                s1()
                if pend is not None:
                    pend()
                pend = s2
            pend()
            pend = None
            S.dma(S.q_sp, [yst], [], lambda e, g=g: e.dma_start(
                out=K.ymix[g * 256:(g + 1) * 256, :].rearrange("(h d) t -> d h t", d=64), in_=yst[:]))
    S.barrier()


def phase_attn_b(K, l):
    nc, S = K.nc, K.S
    dils = (1, 4, 16)
    with ExitStack() as es:
        sb, psum = mk_alloc(nc, es)
        KTs = [sb(f"b_KT{i}", [64, SEQ], BF16) for i in range(2)]
        QTs = [sb(f"b_QT{i}", [64, 3, SEQ], BF16) for i in range(2)]

        def load_kq(h):
            KT, QT = KTs[h % 2], QTs[h % 2]
            S.dma(S.q_sp, [], [KT], lambda e: e.dma_start(out=KT[:], in_=K.projT[1536 + h * 64:1536 + (h + 1) * 64, :]))
            S.dma(S.q_sp, [], [QT], lambda e: e.dma_start(
                out=QT[:], in_=K.projT[768:1536, :].rearrange("(gi hh d) t -> d gi hh t", gi=3, hh=4, d=64)[:, :, h, :]))
        Vd = [sb(f"b_V{gi}", [128, 32, 256], BF16) for gi in range(3)]
        Va = [sb(f"b_Va{gi}", [128, 32, 128], BF16) for gi in range(3)]
        acc = sb("b_acc", [128, SEQ], F32)
        rec = [sb(f"b_rec{i}", [64, 512], F32) for i in range(2)]
        yst = sb("b_yst", [64, SEQ], BF16)
        ident_f = sb("b_identf", [128, 128], F32)
        NB = 3
        PT = [sb(f"b_PT{i}", [128, 256], BF16) for i in range(NB)]
        ps_s = [psum(f"b_pss{i}") for i in range(NB)]
        ps_o = [psum(f"b_pso{i}") for i in range(NB)]
        ps_d = [psum(f"b_psd{i}") for i in range(2)]
        S.dma(S.q_sp, [], [ident_f], lambda e: e.dma_start(out=ident_f[:], in_=K.consts_d[:, 0:128]))
        for gi, dil in enumerate(dils):
            S.dma(S.q_sp, [], [Vd[gi]], lambda e, gi=gi, dil=dil: e.dma_start(
                out=Vd[gi][:].rearrange("j (b r) f -> j b r f", r=dil),
                in_=K.vb_tok.rearrange("(b j r) f -> j b r f", j=128, r=dil)))
            S.op(S.dve, [], [Va[gi]], lambda gi=gi: nc.vector.memset(Va[gi][:, :, 64:128], 1.0))
        it = 0
        pendq = []
        load_kq(0)
        for h in range(4):
            KT, QT = KTs[h % 2], QTs[h % 2]
            if h + 1 < 4:
                load_kq(h + 1)
            for gi in range(3):
                S.op(S.act, [Vd[gi]], [Va[gi]], lambda gi=gi, h=h: nc.scalar.activation(
                    out=Va[gi][:, :, 0:64], in_=Vd[gi][:, :, h * 64:(h + 1) * 64], func=AF.Copy))
            for gi, dil in enumerate(dils):
                nb = 32 // dil
                for r in range(dil):
                    for b in range(nb):
                        i3 = it % NB
                        it += 1
                        pss, pso, pt = ps_s[i3], ps_o[i3], PT[i3]

                        def tok(bb, r=r, dil=dil):
                            s0 = r + dil * 128 * bb
                            return slice(s0, s0 + dil * 127 + 1, dil)

                        def s1(b=b, gi=gi, tok=tok, pss=pss, pt=pt, KT=KT, QT=QT):
                            mms = [lambda: nc.tensor.matmul(pss[:, 0:128], KT[:, tok(b)], QT[:, gi, tok(b)], start=True, stop=True)]
                            if b > 0:
                                mms += [lambda: nc.tensor.matmul(pss[:, 128:256], KT[:, tok(b - 1)], QT[:, gi, tok(b)], start=True, stop=True)]
                            S.mm([KT, QT], [pss], mms)
                            w = 256 if b > 0 else 128
                            S.op(S.act, [pss], [pt], lambda: nc.scalar.activation(out=pt[:, 0:w], in_=pss[:, 0:w], func=AF.Exp, scale=0.125))
                            S.op(S.dve, [pt, K.m01], [pt], lambda: nc.vector.tensor_tensor(out=pt[:, 0:w], in0=pt[:, 0:w], in1=K.m01[:, 0:w], op=ALU.mult))

                        def s2(b=b, gi=gi, dil=dil, r=r, tok=tok, pso=pso, pt=pt):
                            vcur = Va[gi][:, b * dil + r, :]
                            if b > 0:
                                vprev = Va[gi][:, (b - 1) * dil + r, :]
                                mms = [
                                    lambda: nc.tensor.matmul(pso[:, 0:128], vprev, pt[:, 128:256], start=True, stop=False),
                                    lambda: nc.tensor.matmul(pso[:, 0:128], vcur, pt[:, 0:128], start=False, stop=True)]
                            else:
                                mms = [lambda: nc.tensor.matmul(pso[:, 0:128], vcur, pt[:, 0:128], start=True, stop=True)]
                            S.mm([Va[gi], pt], [pso], mms)
                            dst = acc[:, tok(b)]
                            if gi == 0:
                                S.op(S.dve, [pso], [acc], lambda: nc.vector.tensor_copy(out=dst, in_=pso[:, 0:128]))
                            else:
                                S.op(S.dve, [pso, acc], [acc], lambda: nc.vector.tensor_tensor(out=dst, in0=dst, in1=pso[:, 0:128], op=ALU.add))
                        s1()
                        pendq.append(s2)
                        if len(pendq) > 2:
                            pendq.pop(0)()
            while pendq:
                pendq.pop(0)()
            for c in range(SEQ // 512):
                cs_ = slice(c * 512, (c + 1) * 512)
                pd, rc = ps_d[c % 2], rec[c % 2]
                S.mm([ident_f, acc], [pd], [lambda pd=pd, cs_=cs_: nc.tensor.matmul(pd[0:64, :], ident_f[:, 64:128], acc[:, cs_], start=True, stop=True)])
                S.op(S.act, [pd], [rc], lambda pd=pd, rc=rc: nc.scalar.activation(out=rc[:], in_=pd[0:64, :], func=AF.Ln))
                S.op(S.act, [rc], [rc], lambda rc=rc: nc.scalar.activation(out=rc[:], in_=rc[:], func=AF.Exp, scale=-1.0))
                S.op(S.dve, [acc, rc], [yst], lambda rc=rc, cs_=cs_: nc.vector.tensor_tensor(out=yst[:, cs_], in0=acc[0:64, cs_], in1=rc[:], op=ALU.mult))
            S.dma(S.q_sp, [yst], [], lambda e, h=h: e.dma_start(out=K.ymix[512 + h * 64:512 + (h + 1) * 64, :], in_=yst[:]))
    S.barrier()


def phase_s5(K, l):
    nc, S = K.nc, K.S
    pp = K.pp[l]
    TWO_PI = 6.283185307179586
    I32 = mybir.dt.int32
    with ExitStack() as es:
        sb, psum = mk_alloc(nc, es)
        cs = sb("s_cs", [128, 16, TT], F32)
        sn = sb("s_sn", [128, 16, TT], F32)
        Bre = sb("s_Bre", [128, 16, 128], BF16)
        Bim = sb("s_Bim", [128, 16, 128], BF16)
        Cfr = sb("s_Cfr", [128, 16, 128], BF16)
        nCfi = sb("s_nCfi", [128, 16, 128], BF16)
        nCfr = sb("s_nCfr", [128, 16, 128], BF16)
        diagD = sb("s_diagD", [128, 4, 128], BF16)
        wglu = sb("s_wglu", [128, 4, 512], BF16)
        rho = sb("s_rho", [128, 16], F32)
        c9 = [sb(f"s_ck{k}", [128, 16], F32) for k in range(10)]
        s9 = [sb(f"s_sk{k}", [128, 16], F32) for k in range(10)]
        ns9 = sb("s_ns9", [128, 16], F32)
        fr = sb("s_fr", [128, 16], F32)
        fi = sb("s_fi", [128, 16], F32)
        nfi = sb("s_nfi", [128, 16], F32)
        nfr = sb("s_nfr", [128, 16], F32)
        init_r = [sb(f"s_ir{i}", [128, 16], F32) for i in range(2)]
        init_i = [sb(f"s_ii{i}", [128, 16], F32) for i in range(2)]

        def dv(reads, writes, f):
            S.op(S.dve, reads, writes, f)

        S.dma(S.q_pool, [], [Bre], lambda e: e.dma_start(out=Bre[:], in_=K.bmat[l, 0]))
        S.dma(S.q_pool, [], [Bim], lambda e: e.dma_start(out=Bim[:], in_=K.bmat[l, 1]))
        S.dma(S.q_pool, [], [wglu], lambda e: e.dma_start(out=wglu[:], in_=K.w_glu[l].rearrange("(kc p) m -> p kc m", p=128)))
        for kt in range(4):
            dv([K.cb16, pp], [diagD], lambda kt=kt: nc.vector.tensor_scalar(
                out=diagD[:, kt, :], in0=K.cb16[:, 0:128], scalar1=pp[:, PP_SSMD + kt:PP_SSMD + kt + 1], scalar2=None, op0=ALU.mult))

        with ExitStack() as es2:
            sb2, _ = mk_alloc(nc, es2)
            def c16(n):
                return sb2("s_p_" + n, [128, 16], F32)
            dt, lrdt, th, q, kf, r, abr, abi, t1, t2, den, nr = [c16(n) for n in
                                                                 ("dt", "lrdt", "th", "q", "kf", "r", "abr", "abi", "t1", "t2", "den", "nr")]
            ki = sb2("s_p_ki", [128, 16], I32)
            Cre_f = sb2("s_Cre_f", [128, 16, 128], F32)
            Cim_f = sb2("s_Cim_f", [128, 16, 128], F32)
            ctmp = sb2("s_ctmp", [128, 128], F32)
            S.dma(S.q_sp, [], [Cre_f], lambda e: e.dma_start(out=Cre_f[:], in_=K.cmat[l, 0]))
            S.dma(S.q_sp, [], [Cim_f], lambda e: e.dma_start(out=Cim_f[:], in_=K.cmat[l, 1]))
            LR = pp[:, PP_LR:PP_LR + 16]
            LI = pp[:, PP_LI:PP_LI + 16]
            S.op(S.act, [pp], [dt], lambda: nc.scalar.activation(out=dt[:], in_=pp[:, PP_LDT:PP_LDT + 16], func=AF.Exp))
            dv([pp, dt], [lrdt], lambda: nc.vector.tensor_tensor(out=lrdt[:], in0=LR, in1=dt[:], op=ALU.mult))
            dv([pp, dt], [th], lambda: nc.vector.tensor_tensor(out=th[:], in0=LI, in1=dt[:], op=ALU.mult))
            S.op(S.act, [lrdt], [rho], lambda: nc.scalar.activation(out=rho[:], in_=lrdt[:], func=AF.Exp))

            C1, C2 = 6.28125, 0.0019353071795864769
            y_, y2, sA, cA, sB, cB = [c16(n) for n in ("y", "y2", "sA", "cA", "sB", "cB")]
            dv([th], [q], lambda: nc.vector.tensor_scalar(out=q[:], in0=th[:], scalar1=1.0 / TWO_PI, scalar2=None, op0=ALU.mult))
            dv([q], [ki], lambda: nc.vector.tensor_copy(out=ki[:], in_=q[:]))
            dv([ki], [kf], lambda: nc.vector.tensor_copy(out=kf[:], in_=ki[:]))
            dv([kf, th], [r], lambda: nc.vector.scalar_tensor_tensor(out=r[:], in0=kf[:], scalar=-C1, in1=th[:], op0=ALU.mult, op1=ALU.add))
            dv([kf, r], [r], lambda: nc.vector.scalar_tensor_tensor(out=r[:], in0=kf[:], scalar=-C2, in1=r[:], op0=ALU.mult, op1=ALU.add))
            dv([r], [y_], lambda: nc.vector.tensor_scalar(out=y_[:], in0=r[:], scalar1=0.125, scalar2=None, op0=ALU.mult))
            dv([y_], [y2], lambda: nc.vector.tensor_tensor(out=y2[:], in0=y_[:], in1=y_[:], op=ALU.mult))

            def horner(dst, coefs, last_mul, last_add):
                dv([y2], [dst], lambda: nc.vector.tensor_scalar(out=dst[:], in0=y2[:], scalar1=coefs[0], scalar2=None, op0=ALU.mult))
                for cf in coefs[1:]:
                    dv([dst, y2], [dst], lambda cf=cf: nc.vector.scalar_tensor_tensor(out=dst[:], in0=dst[:], scalar=cf, in1=y2[:], op0=ALU.add, op1=ALU.mult))
                if last_mul is not None:
                    dv([dst, last_mul], [dst], lambda: nc.vector.scalar_tensor_tensor(out=dst[:], in0=dst[:], scalar=last_add, in1=last_mul[:], op0=ALU.add, op1=ALU.mult))
                else:
                    dv([dst], [dst], lambda: nc.vector.tensor_scalar(out=dst[:], in0=dst[:], scalar1=last_add, scalar2=None, op0=ALU.add))
            horner(sA, [1.0 / 362880, -1.0 / 5040, 1.0 / 120, -1.0 / 6], y_, 1.0)
            horner(cA, [-1.0 / 3628800, 1.0 / 40320, -1.0 / 720, 1.0 / 24, -0.5], None, 1.0)
            cur_s, cur_c, nxt_s, nxt_c = sA, cA, sB, cB
            for dbl in range(3):
                fin = (dbl == 2)
                ds_ = s9[0] if fin else nxt_s
                dc_ = c9[0] if fin else nxt_c
                dv([cur_s], [t1], lambda cur_s=cur_s: nc.vector.tensor_tensor(out=t1[:], in0=cur_s[:], in1=cur_s[:], op=ALU.mult))
                dv([cur_s, cur_c], [ds_], lambda cur_s=cur_s, cur_c=cur_c, ds_=ds_: nc.vector.scalar_tensor_tensor(out=ds_[:], in0=cur_s[:], scalar=2.0, in1=cur_c[:], op0=ALU.mult, op1=ALU.mult))
                dv([t1], [dc_], lambda dc_=dc_: nc.vector.tensor_scalar(out=dc_[:], in0=t1[:], scalar1=-2.0, scalar2=1.0, op0=ALU.mult, op1=ALU.add))
                cur_s, cur_c, nxt_s, nxt_c = ds_, dc_, cur_s, cur_c
            dv([rho, c9[0]], [abr], lambda: nc.vector.tensor_tensor(out=abr[:], in0=rho[:], in1=c9[0][:], op=ALU.mult))
            dv([rho, s9[0]], [abi], lambda: nc.vector.tensor_tensor(out=abi[:], in0=rho[:], in1=s9[0][:], op=ALU.mult))
            dv([abr], [nr], lambda: nc.vector.tensor_scalar(out=nr[:], in0=abr[:], scalar1=-1.0, scalar2=None, op0=ALU.add))
            dv([pp], [t1], lambda: nc.vector.tensor_tensor(out=t1[:], in0=LR, in1=LR, op=ALU.mult))
            dv([pp], [t2], lambda: nc.vector.tensor_tensor(out=t2[:], in0=LI, in1=LI, op=ALU.mult))
            dv([t1, t2], [den], lambda: nc.vector.tensor_tensor(out=den[:], in0=t1[:], in1=t2[:], op=ALU.add))
            dv([den], [den], lambda: nc.vector.reciprocal(out=den[:], in_=den[:]))
            dv([nr, pp], [t1], lambda: nc.vector.tensor_tensor(out=t1[:], in0=nr[:], in1=LR, op=ALU.mult))
            dv([abi, pp], [t2], lambda: nc.vector.tensor_tensor(out=t2[:], in0=abi[:], in1=LI, op=ALU.mult))
            dv([t1, t2], [t1], lambda: nc.vector.tensor_tensor(out=t1[:], in0=t1[:], in1=t2[:], op=ALU.add))
            dv([t1, den], [fr], lambda: nc.vector.tensor_tensor(out=fr[:], in0=t1[:], in1=den[:], op=ALU.mult))
            dv([abi, pp], [t1], lambda: nc.vector.tensor_tensor(out=t1[:], in0=abi[:], in1=LR, op=ALU.mult))
            dv([nr, pp], [t2], lambda: nc.vector.tensor_tensor(out=t2[:], in0=nr[:], in1=LI, op=ALU.mult))
            dv([t1, t2], [t1], lambda: nc.vector.tensor_tensor(out=t1[:], in0=t1[:], in1=t2[:], op=ALU.subtract))
            dv([t1, den], [fi], lambda: nc.vector.tensor_tensor(out=fi[:], in0=t1[:], in1=den[:], op=ALU.mult))
            dv([fi], [nfi], lambda: nc.vector.tensor_scalar(out=nfi[:], in0=fi[:], scalar1=-1.0, scalar2=None, op0=ALU.mult))
            dv([fr], [nfr], lambda: nc.vector.tensor_scalar(out=nfr[:], in0=fr[:], scalar1=-1.0, scalar2=None, op0=ALU.mult))
            for j in range(16):
                dv([Cre_f, fr], [ctmp], lambda j=j: nc.vector.tensor_scalar(out=ctmp[:], in0=Cre_f[:, j, :], scalar1=fr[:, j:j + 1], scalar2=None, op0=ALU.mult))
                dv([Cim_f, nfi, ctmp], [Cfr], lambda j=j: nc.vector.scalar_tensor_tensor(out=Cfr[:, j, :], in0=Cim_f[:, j, :], scalar=nfi[:, j:j + 1], in1=ctmp[:], op0=ALU.mult, op1=ALU.add))
                dv([Cre_f, nfi], [ctmp], lambda j=j: nc.vector.tensor_scalar(out=ctmp[:], in0=Cre_f[:, j, :], scalar1=nfi[:, j:j + 1], scalar2=None, op0=ALU.mult))
                dv([Cim_f, nfr, ctmp], [nCfi], lambda j=j: nc.vector.scalar_tensor_tensor(out=nCfi[:, j, :], in0=Cim_f[:, j, :], scalar=nfr[:, j:j + 1], in1=ctmp[:], op0=ALU.mult, op1=ALU.add))
                dv([Cre_f, nfr], [ctmp], lambda j=j: nc.vector.tensor_scalar(out=ctmp[:], in0=Cre_f[:, j, :], scalar1=nfr[:, j:j + 1], scalar2=None, op0=ALU.mult))
                dv([Cim_f, fi, ctmp], [nCfr], lambda j=j: nc.vector.scalar_tensor_tensor(out=nCfr[:, j, :], in0=Cim_f[:, j, :], scalar=fi[:, j:j + 1], in1=ctmp[:], op0=ALU.mult, op1=ALU.add))
            tmpb = sb2("s_tmpb", [128, 16, TT // 2], F32)
            dv([], [cs], lambda: nc.vector.memset(cs[:, :, 0:1], 1.0))
            dv([], [sn], lambda: nc.vector.memset(sn[:, :, 0:1], 0.0))
            for k in range(9):
                n = 1 << k
                ck, sk = c9[k], s9[k]
                ckb = ck[:, :].unsqueeze(2).to_broadcast([128, 16, n])
                skb = sk[:, :].unsqueeze(2).to_broadcast([128, 16, n])
                lo, hi = slice(0, n), slice(n, 2 * n)
                dv([cs, ck], [cs], lambda ckb=ckb, lo=lo, hi=hi: nc.vector.tensor_tensor(out=cs[:, :, hi], in0=cs[:, :, lo], in1=ckb, op=ALU.mult))
                dv([sn, sk], [tmpb], lambda skb=skb, lo=lo, n=n: nc.vector.tensor_tensor(out=tmpb[:, :, 0:n], in0=sn[:, :, lo], in1=skb, op=ALU.mult))
                dv([cs, tmpb], [cs], lambda hi=hi, n=n: nc.vector.tensor_tensor(out=cs[:, :, hi], in0=cs[:, :, hi], in1=tmpb[:, :, 0:n], op=ALU.subtract))
                dv([cs, sk], [tmpb], lambda skb=skb, lo=lo, n=n: nc.vector.tensor_tensor(out=tmpb[:, :, 0:n], in0=cs[:, :, lo], in1=skb, op=ALU.mult))
                dv([sn, ck], [sn], lambda ckb=ckb, lo=lo, hi=hi: nc.vector.tensor_tensor(out=sn[:, :, hi], in0=sn[:, :, lo], in1=ckb, op=ALU.mult))
                dv([sn, tmpb], [sn], lambda hi=hi, n=n: nc.vector.tensor_tensor(out=sn[:, :, hi], in0=sn[:, :, hi], in1=tmpb[:, :, 0:n], op=ALU.add))
                last = 2 * n - 1
                csl, snl = cs[:, :, last], sn[:, :, last]
                dv([cs, c9[0]], [t1], lambda csl=csl: nc.vector.tensor_tensor(out=t1[:], in0=csl, in1=c9[0][:], op=ALU.mult))
                dv([sn, s9[0]], [t2], lambda snl=snl: nc.vector.tensor_tensor(out=t2[:], in0=snl, in1=s9[0][:], op=ALU.mult))
                dv([t1, t2], [c9[k + 1]], lambda k=k: nc.vector.tensor_tensor(out=c9[k + 1][:], in0=t1[:], in1=t2[:], op=ALU.subtract))
                dv([sn, c9[0]], [t1], lambda snl=snl: nc.vector.tensor_tensor(out=t1[:], in0=snl, in1=c9[0][:], op=ALU.mult))
                dv([cs, s9[0]], [t2], lambda csl=csl: nc.vector.tensor_tensor(out=t2[:], in0=csl, in1=s9[0][:], op=ALU.mult))
                dv([t1, t2], [s9[k + 1]], lambda k=k: nc.vector.tensor_tensor(out=s9[k + 1][:], in0=t1[:], in1=t2[:], op=ALU.add))
            dv([s9[9]], [ns9], lambda: nc.vector.tensor_scalar(out=ns9[:], in0=s9[9][:], scalar1=-1.0, scalar2=None, op0=ALU.mult))
        cL, sL = c9[9], s9[9]
        S.barrier()
        dump(K, 0, cs, cs[:, 0, :])
        dump(K, 1, sn, sn[:, 0, :])
        for i_, b_ in enumerate((rho, fr, fi, c9[0], s9[0], c9[9], s9[9])):
            dump(K, 9 + i_, b_, b_[:], 16)

        wts = [[sb(f"s_wt{i}_{k}", [128, TT], mybir.dt.float32r) for k in range(4)] for i in range(2)]
        xq = [[sb(f"s_xq{i}_{k}", [128, TT], BF16) for k in range(4)] for i in range(3)]
        ident_f0 = sb("s_identf0", [128, 128], F32)
        S.dma(S.q_sp, [], [ident_f0], lambda e: e.dma_start(out=ident_f0[:], in_=K.consts_d[:, 0:128]))
        ident_f = sb("s_identr", [128, 128], mybir.dt.float32r)
        S.op(S.dve, [ident_f0], [ident_f], lambda: nc.vector.tensor_copy(out=ident_f[:], in_=ident_f0[:]))
        zr = sb("s_zr", [128, TT], F32)
        zi = sb("s_zi", [128, TT], F32)
        zT = [sb(f"s_zT{i}", [128, 4, TT], BF16) for i in range(2)]
        sig = [sb(f"s_sig{i}", [128, TT], BF16) for i in range(4)]
        ycst = [sb(f"s_ycst{i}", [128, 4, TT], BF16) for i in range(2)]
        ps_br = [psum(f"s_psbr{i}") for i in range(1)]
        ps_bi = [psum(f"s_psbi{i}") for i in range(1)]
        btrs = [psum(f"s_psbtr{i}") for i in range(2)]
        btis = [psum(f"s_psbti{i}") for i in range(2)]
        ps_y = [psum(f"s_psy{i}") for i in range(1)]
        ps_g = [psum(f"s_psg{i}") for i in range(1)]
        dv([], [init_r[0]], lambda: nc.vector.memset(init_r[0][:], 0.0))
        dv([], [init_i[0]], lambda: nc.vector.memset(init_i[0][:], 0.0))
        uview = K.projT[2048:2560, :].rearrange("(kt p) t -> p kt t", p=128)
        uT = [sb(f"s_uT{i}", [128, 4, TT], BF16) for i in range(2)]
        NIT = NTT * 16
        ctmpA = [sb(f"s_ctA{i}", [128, 1], F32) for i in range(2)]
        ctmpB = [sb(f"s_ctB{i}", [128, 1], F32) for i in range(2)]
        st = {"iy": 0, "ig": 0}

        def load_u(tt):
            u = uT[tt % 2]
            S.dma(S.q_sp, [], [u], lambda e: e.dma_start(out=u[:], in_=uview[:, :, tt_sl(tt)]))

        def stageA(n):
            tt, j = divmod(n, 16)
            kt = j // 4
            u = uT[tt % 2]
            if j == 3 and tt + 1 < NTT:
                load_u(tt + 1)
            pbr, pbi = ps_br[0], ps_bi[0]
            w1, w2, w3, w4 = wts[n % 2]
            btr, bti = btrs[n % 2], btis[n % 2]
            S.mm([Bre, u], [pbr], [lambda: nc.tensor.matmul(pbr[:], Bre[:, j, :], u[:, kt, :], start=True, stop=True)])
            S.mm([Bim, u], [pbi], [lambda: nc.tensor.matmul(pbi[:], Bim[:, j, :], u[:, kt, :], start=True, stop=True)])

        def stageB(n):
            tt, j = divmod(n, 16)
            pbr, pbi = ps_br[0], ps_bi[0]
            w1, w2, w3, w4 = wts[n % 2]
            btr, bti = btrs[n % 2], btis[n % 2]
            csj, snj = cs[:, j, :], sn[:, j, :]
            dv([cs, pbr], [w1], lambda: nc.vector.tensor_tensor(out=w1[:], in0=csj, in1=pbr[:], op=ALU.mult))
            dv([sn, pbi], [w2], lambda: nc.vector.tensor_tensor(out=w2[:], in0=snj, in1=pbi[:], op=ALU.mult))
            dv([cs, pbi], [w3], lambda: nc.vector.tensor_tensor(out=w3[:], in0=csj, in1=pbi[:], op=ALU.mult))
            dv([sn, pbr], [w4], lambda: nc.vector.scalar_tensor_tensor(out=w4[:], in0=pbr[:], scalar=-1.0, in1=snj, op0=ALU.mult, op1=ALU.mult))
            S.mm([ident_f, w1, w2], [btr], [
                lambda: nc.tensor.matmul(btr[:], ident_f[:], w1[:], start=True, stop=False),
                lambda: nc.tensor.matmul(btr[:], ident_f[:], w2[:], start=False, stop=True)])
            S.mm([ident_f, w3, w4], [bti], [
                lambda: nc.tensor.matmul(bti[:], ident_f[:], w3[:], start=True, stop=False),
                lambda: nc.tensor.matmul(bti[:], ident_f[:], w4[:], start=False, stop=True)])

        def stageCd(n):
            tt, j = divmod(n, 16)
            kt = j // 4
            u = uT[tt % 2]
            btr, bti = btrs[n % 2], btis[n % 2]
            ir, ii = init_r[tt % 2], init_i[tt % 2]
            nir, nii = init_r[(tt + 1) % 2], init_i[(tt + 1) % 2]
            zt = zT[tt % 2]
            csj, snj = cs[:, j, :], sn[:, j, :]
            rb = rho[:, j:j + 1].to_broadcast([128, TT])
            dv([rho, btr, ir], [zr], lambda: nc.vector.tensor_tensor_scan(
                out=zr[:], data0=rb, data1=btr[:], initial=ir[:, j:j + 1], op0=ALU.mult, op1=ALU.add))
            dv([rho, bti, ii], [zi], lambda: nc.vector.tensor_tensor_scan(
                out=zi[:], data0=rb, data1=bti[:], initial=ii[:, j:j + 1], op0=ALU.mult, op1=ALU.add))
            if tt < NTT - 1:
                ca, cb_ = ctmpA[n % 2], ctmpB[n % 2]
                S.op(S.act, [zr, cL], [ca], lambda: nc.scalar.activation(out=ca[:], in_=zr[:, TT - 1:TT], func=AF.Identity, scale=cL[:, j:j + 1]))
                S.op(S.act, [zi, ns9, ca], [nir], lambda: nc.scalar.activation(out=nir[:, j:j + 1], in_=zi[:, TT - 1:TT], func=AF.Identity, scale=ns9[:, j:j + 1], bias=ca[:, 0:1]))
                S.op(S.act, [zr, sL], [cb_], lambda: nc.scalar.activation(out=cb_[:], in_=zr[:, TT - 1:TT], func=AF.Identity, scale=sL[:, j:j + 1]))
                S.op(S.act, [zi, cL, cb_], [nii], lambda: nc.scalar.activation(out=nii[:, j:j + 1], in_=zi[:, TT - 1:TT], func=AF.Identity, scale=cL[:, j:j + 1], bias=cb_[:, 0:1]))
            x1, x2, x3, x4 = xq[n % 3]
            dv([cs, zr], [x1], lambda: nc.vector.tensor_tensor(out=x1[:], in0=csj, in1=zr[:], op=ALU.mult))
            dv([sn, zi], [x2], lambda: nc.vector.tensor_tensor(out=x2[:], in0=snj, in1=zi[:], op=ALU.mult))
            dv([sn, zr], [x3], lambda: nc.vector.tensor_tensor(out=x3[:], in0=snj, in1=zr[:], op=ALU.mult))
            dv([cs, zi], [x4], lambda: nc.vector.tensor_tensor(out=x4[:], in0=csj, in1=zi[:], op=ALU.mult))

        def stageCp(n):
            tt, j = divmod(n, 16)
            kt = j // 4
            u = uT[tt % 2]
            zt = zT[tt % 2]
            x1, x2, x3, x4 = xq[n % 3]
            py = ps_y[0]
            mms = []
            if j % 4 == 0:
                mms.append(lambda: nc.tensor.matmul(py[:], diagD[:, kt, :], u[:, kt, :], start=True, stop=False))
            mms.append(lambda: nc.tensor.matmul(py[:], Cfr[:, j, :], x1[:], start=False, stop=False))
            mms.append(lambda: nc.tensor.matmul(py[:], nCfr[:, j, :], x2[:], start=False, stop=False))
            mms.append(lambda: nc.tensor.matmul(py[:], nCfi[:, j, :], x3[:], start=False, stop=False))
            mms.append(lambda: nc.tensor.matmul(py[:], nCfi[:, j, :], x4[:], start=False, stop=(j % 4 == 3)))
            S.mm([diagD, u, Cfr, nCfr, nCfi, x1, x2, x3, x4], [py], mms)
            if j % 4 == 3:
                S.op(S.act, [py], [zt], lambda: nc.scalar.activation(out=zt[:, kt, :], in_=py[:], func=AF.Gelu_apprx_tanh))
                st["iy"] += 1
            if j == 15:
                deferred.append(lambda tt=tt, zt=zt: emit_glu(tt, zt))

        def emit_glu(tt, zt):
            for mo in range(4):
                deferred3.append(lambda mo=mo: emit_glu_mo(tt, zt, mo))

        def emit_glu_mo(tt, zt, mo):
            pg = ps_g[0]
            sg = sig[mo]
            S.mm([wglu, zt], [pg], [(lambda kc=kc: nc.tensor.matmul(
                pg[:], wglu[:, kc, mo * 128:(mo + 1) * 128], zt[:, kc, :], start=(kc == 0), stop=(kc == 3))) for kc in range(4)])
            S.op(S.act, [pg, pp], [sg], lambda: nc.scalar.activation(
                out=sg[:], in_=pg[:], func=AF.Sigmoid, bias=pp[:, PP_BGLU + mo:PP_BGLU + mo + 1]))
            if mo == 3:
                deferred2.append(lambda: emit_glu_b(tt, zt))

        deferred3 = []

        def emit_glu_b(tt, zt):
            yc = ycst[tt % 2]
            for mo in range(4):
                sg = sig[mo]
                S.op(S.dve, [zt, sg], [yc], lambda mo=mo, sg=sg: nc.vector.tensor_tensor(
                    out=yc[:, mo, :], in0=zt[:, mo, :], in1=sg[:], op=ALU.mult))
            S.dma(S.q_sp, [yc], [], lambda e: e.dma_start(
                out=K.ymix[768:1280, :].rearrange("(mo p) t -> p mo t", p=128)[:, :, tt_sl(tt)], in_=yc[:]))

        deferred2 = []
        deferred = []
        load_u(0)
        stageA(0)
        stageB(0)
        for n in range(NIT):
            if n + 1 < NIT:
                stageA(n + 1)
                stageB(n + 1)
            if n >= 1:
                stageCp(n - 1)
            stageCd(n)
            if n % 16 == 2 and deferred:
                deferred.pop(0)()
            if n % 16 in (3, 4, 5, 6) and deferred3:
                deferred3.pop(0)()
            if n % 16 == 9 and deferred2:
                deferred2.pop(0)()
        stageCp(NIT - 1)
        while deferred:
            deferred.pop(0)()
        while deferred3:
            deferred3.pop(0)()
        while deferred2:
            deferred2.pop(0)()
    S.barrier()


def phase_merge(K, l, x_src):
    nc, S = K.nc, K.S
    with ExitStack() as es:
        sb, psum = mk_alloc(nc, es)
        wbr = sb("m_wbr", [128, 10, D], BF16)
        wout = sb("m_wout", [128, KC, D], BF16)
        yt = [sb(f"m_yt{i}", [128, 10, TT], BF16) for i in range(2)]
        gt = [sb(f"m_gt{i}", [128, 24, TT], BF16) for i in range(2)]
        xt = [sb(f"m_xt{i}", [128, KC, TT], F32) for i in range(2)]
        mg = [sb(f"m_mg{i}", [128, KC, TT], BF16) for i in range(2)]
        t1s = [sb(f"m_t1{i}", [128, TT], F32) for i in range(2)]
        t2s = [sb(f"m_t2{i}", [128, TT], F32) for i in range(2)]
        t3s = [sb(f"m_t3{i}", [128, TT], F32) for i in range(2)]
        pA = [psum(f"m_pA{i}") for i in range(2)]
        pB = [psum(f"m_pB{i}") for i in range(2)]
        pC = [psum(f"m_pC{i}") for i in range(2)]
        pO = [psum(f"m_pO{i}") for i in range(2)]
        S.dma(S.q_pool, [], [wbr], lambda e: e.dma_start(out=wbr[:, 0:4, :], in_=K.w_branch_a[l].rearrange("(kc p) m -> p kc m", p=128)))
        S.dma(S.q_pool, [], [wbr], lambda e: e.dma_start(out=wbr[:, 4:6, :], in_=K.w_branch_b[l].rearrange("(kc p) m -> p kc m", p=128)))
        S.dma(S.q_pool, [], [wbr], lambda e: e.dma_start(out=wbr[:, 6:10, :], in_=K.w_branch_c[l].rearrange("(kc p) m -> p kc m", p=128)))
        S.dma(S.q_pool, [], [wout], lambda e: e.dma_start(out=wout[:], in_=K.w_out[l].rearrange("(kc p) m -> p kc m", p=128)))
        yv = K.ymix.rearrange("(c p) t -> p c t", p=128)
        gv = K.projT[2560:5632, :].rearrange("(c p) t -> p c t", p=128)
        xv = x_src.rearrange("(c p) t -> p c t", p=128)
        xo = K.xres.rearrange("(c p) t -> p c t", p=128)
        st = {"im": 0, "io": 0}

        def load_yg(tt):
            y, g = yt[tt % 2], gt[tt % 2]
            S.dma(S.q_sp, [], [y], lambda e: e.dma_start(out=y[:], in_=yv[:, :, tt_sl(tt)]))
            S.dma(S.q_sp, [], [g], lambda e: e.dma_start(out=g[:], in_=gv[:, :, tt_sl(tt)]))

        def load_x(tt):
            x = xt[tt % 2]
            S.dma(S.q_sp, [], [x], lambda e: e.dma_start(out=x[:], in_=xv[:, :, tt_sl(tt)]))

        def branch(tt):
            y, g, m = yt[tt % 2], gt[tt % 2], mg[tt % 2]
            for mo in range(KC):
                im = st["im"]
                st["im"] += 1
                a, b, c = pA[im % 2], pB[im % 2], pC[im % 2]
                t1, t2, t3 = t1s[im % 2], t2s[im % 2], t3s[im % 2]
                ms = slice(mo * 128, (mo + 1) * 128)
                S.mm([wbr, y], [a], [(lambda kc=kc: nc.tensor.matmul(a[:], wbr[:, kc, ms], y[:, kc, :], start=(kc == 0), stop=(kc == 3))) for kc in range(0, 4)])
                S.mm([wbr, y], [b], [(lambda kc=kc: nc.tensor.matmul(b[:], wbr[:, kc, ms], y[:, kc, :], start=(kc == 4), stop=(kc == 5))) for kc in range(4, 6)])
                S.mm([wbr, y], [c], [(lambda kc=kc: nc.tensor.matmul(c[:], wbr[:, kc, ms], y[:, kc, :], start=(kc == 6), stop=(kc == 9))) for kc in range(6, 10)])
                S.op(S.dve, [a, g], [t1], lambda mo=mo: nc.vector.tensor_tensor(out=t1[:], in0=a[:], in1=g[:, mo, :], op=ALU.mult))
                S.op(S.dve, [b, g], [t2], lambda mo=mo: nc.vector.tensor_tensor(out=t2[:], in0=b[:], in1=g[:, 8 + mo, :], op=ALU.mult))
                S.op(S.dve, [t1, t2], [t1], lambda: nc.vector.tensor_tensor(out=t1[:], in0=t1[:], in1=t2[:], op=ALU.add))
                S.op(S.dve, [c, g], [t3], lambda mo=mo: nc.vector.tensor_tensor(out=t3[:], in0=c[:], in1=g[:, 16 + mo, :], op=ALU.mult))
                S.op(S.dve, [t1, t3], [m], lambda mo=mo: nc.vector.tensor_tensor(out=m[:, mo, :], in0=t1[:], in1=t3[:], op=ALU.add))

        def outproj(tt):
            x, m = xt[tt % 2], mg[tt % 2]
            for mo in range(KC):
                io = st["io"]
                st["io"] += 1
                o = pO[io % 2]
                ms = slice(mo * 128, (mo + 1) * 128)
                S.mm([wout, m], [o], [(lambda kc=kc: nc.tensor.matmul(o[:], wout[:, kc, ms], m[:, kc, :], start=(kc == 0), stop=(kc == KC - 1))) for kc in range(KC)])
                S.op(S.dve, [o, x], [x], lambda mo=mo: nc.vector.tensor_tensor(out=x[:, mo, :], in0=o[:], in1=x[:, mo, :], op=ALU.add))
            S.dma(S.q_act, [x], [], lambda e: e.dma_start(out=xo[:, :, tt_sl(tt)], in_=x[:]))

        load_yg(0)
        load_x(0)
        load_yg(1)
        load_x(1)
        branch(0)
        for tt in range(NTT):
            if tt + 1 < NTT:
                branch(tt + 1)
            if tt + 2 < NTT:
                load_yg(tt + 2)
            outproj(tt)
            if tt + 2 < NTT:
                load_x(tt + 2)
    S.barrier()


def phase_ffn_up(K, l):
    nc, S = K.nc, K.S
    pp = K.pp[l]
    NG = FFN // 128
    with ExitStack() as es:
        sb, psum = mk_alloc(nc, es)
        hT = [sb(f"f_hT{tt}", [128, KC, TT], BF16) for tt in range(NTT)]
        xv = K.xres.rearrange("(c p) t -> p c t", p=128)
        with ExitStack() as es2:
            sb2, psum2 = mk_alloc(nc, es2)
            xt = [sb2(f"f_xt{i}", [128, KC, TT], F32) for i in range(4)]
            sqs = [sb2(f"f_sq{i}", [128, KC, TT], BF16) for i in range(2)]
            rstd = [sb2(f"f_rstd{i}", [128, TT], F32) for i in range(2)]
            ps_n = [psum2(f"f_psn{i}") for i in range(2)]
            def load_x(tt):
                xb = xt[tt % 4]
                S.dma(S.q_sp, [], [xb], lambda e: e.dma_start(out=xb[:], in_=xv[:, :, tt_sl(tt)]))
            for tt in range(4):
                load_x(tt)
            for tt in range(NTT):
                xb = xt[tt % 4]
                rs = rstd[tt % 2]
                emit_rmsnorm_tile(K, (sqs[tt % 2], ps_n[tt % 2], rs), xb, None, None)
                for c in range(KC):
                    S.op(S.dve, [xb, rs, pp], [hT[tt]],
                         lambda c=c, xb=xb, rs=rs, tt=tt: nc.vector.scalar_tensor_tensor(
                             out=hT[tt][:, c, :], in0=xb[:, c, :], scalar=pp[:, PP_NFFN + c:PP_NFFN + c + 1],
                             in1=rs[:], op0=ALU.mult, op1=ALU.mult))
                if tt + 4 < NTT:
                    load_x(tt + 4)
            S.barrier()
        ps_g = [psum(f"f_psg{i}") for i in range(4)]
        ps_v = [psum(f"f_psv{i}") for i in range(4)]
        wg = [sb(f"f_wg{i}", [128, KC, 128], BF16) for i in range(2)]
        wv_ = [sb(f"f_wv{i}", [128, KC, 128], BF16) for i in range(2)]
        Ug = [sb(f"f_Ug{i}", [128, SEQ + 2], F32) for i in range(2)]
        Uv = [sb(f"f_Uv{i}", [128, SEQ + 2], F32) for i in range(2)]
        cgs = [sb(f"f_cg{i}", [128, TT], F32) for i in range(2)]
        cvs = [sb(f"f_cv{i}", [128, TT], F32) for i in range(2)]
        sgls = [sb(f"f_sgl{i}", [128, TT], F32) for i in range(2)]
        stage = [sb(f"f_st{i}", [128, SEQ], BF16) for i in range(2)]
        for ub in Ug + Uv:
            S.op(S.dve, [], [ub], lambda ub=ub: nc.vector.memset(ub[:, 0:2], 0.0))
        wup = K.w_up[l].rearrange("(c p) n -> p c n", p=128)
        ip = 0
        for fg in range(NG):
            fv = fg + NG
            wgb, wvb = wg[fg % 2], wv_[fg % 2]
            S.dma(S.q_pool, [], [wgb], lambda e, wgb=wgb, fg=fg: e.dma_start(out=wgb[:], in_=wup[:, :, fg * 128:(fg + 1) * 128]))
            S.dma(S.q_pool, [], [wvb], lambda e, wvb=wvb, fv=fv: e.dma_start(out=wvb[:], in_=wup[:, :, fv * 128:(fv + 1) * 128]))
            st = stage[fg % 2]
            ug, uv = Ug[fg % 2], Uv[fg % 2]
            for tt in range(NTT):
                pg, pv = ps_g[ip % 4], ps_v[ip % 4]
                cg, cv, sgl = cgs[ip % 2], cvs[ip % 2], sgls[ip % 2]
                ip += 1
                S.mm([hT[tt], wgb], [pg], [(lambda pg=pg, c=c, tt=tt, wgb=wgb: nc.tensor.matmul(pg[:], wgb[:, c, :], hT[tt][:, c, :], start=(c == 0), stop=(c == KC - 1))) for c in range(KC)])
                S.mm([hT[tt], wvb], [pv], [(lambda pv=pv, c=c, tt=tt, wvb=wvb: nc.tensor.matmul(pv[:], wvb[:, c, :], hT[tt][:, c, :], start=(c == 0), stop=(c == KC - 1))) for c in range(KC)])
                o = tt * TT
                for (p_, u_, c_, ch) in ((pg, ug, cg, fg), (pv, uv, cv, fv)):
                    w0 = pp[:, PP_CW + ch * 3 + 0:PP_CW + ch * 3 + 1]
                    w1 = pp[:, PP_CW + ch * 3 + 1:PP_CW + ch * 3 + 2]
                    w2 = pp[:, PP_CW + ch * 3 + 2:PP_CW + ch * 3 + 3]
                    bb = pp[:, PP_CB + ch:PP_CB + ch + 1]
                    S.op(S.act, [p_], [u_], lambda p_=p_, u_=u_, o=o: nc.scalar.activation(out=u_[:, o + 2:o + TT + 2], in_=p_[:], func=AF.Copy))
                    S.op(S.act, [p_, pp], [c_], lambda p_=p_, c_=c_, w2=w2, bb=bb: nc.scalar.activation(out=c_[:], in_=p_[:], func=AF.Identity, scale=w2, bias=bb))
                    S.op(S.dve, [u_, pp, c_], [c_], lambda u_=u_, c_=c_, w1=w1, o=o: nc.vector.scalar_tensor_tensor(out=c_[:], in0=u_[:, o + 1:o + TT + 1], scalar=w1, in1=c_[:], op0=ALU.mult, op1=ALU.add))
                    S.op(S.dve, [u_, pp, c_], [c_], lambda u_=u_, c_=c_, w0=w0, o=o: nc.vector.scalar_tensor_tensor(out=c_[:], in0=u_[:, o:o + TT], scalar=w0, in1=c_[:], op0=ALU.mult, op1=ALU.add))
                S.op(S.act, [cg], [sgl], lambda cg=cg, sgl=sgl: nc.scalar.activation(out=sgl[:], in_=cg[:], func=AF.Silu))
                S.op(S.dve, [sgl, cv], [st], lambda st=st, tt=tt, sgl=sgl, cv=cv: nc.vector.tensor_tensor(out=st[:, tt_sl(tt)], in0=sgl[:], in1=cv[:], op=ALU.mult))
            S.dma(S.q_sp, [st], [], lambda e, st=st, fg=fg: e.dma_start(out=K.gatedT[fg * 128:(fg + 1) * 128, :], in_=st[:]))
    S.barrier()


def phase_ffn_down(K, l):
    nc, S = K.nc, K.S
    NG = FFN // 128
    QT_ = SEQ // 4
    with ExitStack() as es:
        sb, psum = mk_alloc(nc, es)
        gts = [[sb(f"d_gt{i}_{c}", [128, 11, QT_], BF16) for c in range(2)] for i in range(2)]
        wd = [sb(f"d_wd{i}", [128, NG, 128], BF16) for i in range(3)]
        xc = [sb(f"d_xc{i}", [128, QT_], F32) for i in range(3)]
        ps = [psum(f"d_ps{i}") for i in range(4)]
        gv = K.gatedT.rearrange("(c p) t -> p c t", p=128)
        wdv = K.w_down[l].rearrange("(c p) m -> p c m", p=128)
        st = {"ip": 0, "iw": 0}

        def load_g(q):
            qs = slice(q * QT_, (q + 1) * QT_)
            for c2 in range(2):
                gb = gts[q % 2][c2]
                S.dma(S.q_sp, [], [gb], lambda e, gb=gb, c2=c2: e.dma_start(out=gb[:], in_=gv[:, c2 * 11:(c2 + 1) * 11, qs]))

        load_g(0)
        for q in range(4):
            qs = slice(q * QT_, (q + 1) * QT_)
            if q + 1 < 4:
                load_g(q + 1)
            ga, gb_ = gts[q % 2]
            for mo in range(KC):
                iw = st["iw"]
                st["iw"] += 1
                w = wd[iw % 3]
                x = xc[iw % 3]
                S.dma(S.q_pool, [], [w], lambda e, w=w, mo=mo: e.dma_start(out=w[:], in_=wdv[:, :, mo * 128:(mo + 1) * 128]))
                S.dma(S.q_sp, [], [x], lambda e, x=x, mo=mo, qs=qs: e.dma_start(out=x[:], in_=K.xres[mo * 128:(mo + 1) * 128, qs]))
                for t4 in range(QT_ // TT):
                    p = ps[st["ip"] % 4]
                    st["ip"] += 1
                    ts_ = slice(t4 * TT, (t4 + 1) * TT)
                    S.mm([w, ga, gb_], [p], [(lambda p=p, c=c, w=w, ts_=ts_, ga=ga, gb_=gb_: nc.tensor.matmul(
                        p[:], w[:, c, :], (ga if c < 11 else gb_)[:, c % 11, ts_], start=(c == 0), stop=(c == NG - 1))) for c in range(NG)])
                    S.op(S.dve, [p, x], [x], lambda p=p, x=x, ts_=ts_: nc.vector.tensor_tensor(out=x[:, ts_], in0=p[:], in1=x[:, ts_], op=ALU.add))
                S.dma(S.q_act, [x], [], lambda e, x=x, mo=mo, qs=qs: e.dma_start(out=K.xres[mo * 128:(mo + 1) * 128, qs], in_=x[:]))
    S.barrier()


def phase_final(K):
    nc, S = K.nc, K.S
    with ExitStack() as es:
        sb, psum = mk_alloc(nc, es)
        xt = [sb(f"n_xt{i}", [128, KC, TT], F32) for i in range(4)]
        ot = [sb(f"n_ot{i}", [128, KC, TT], F32) for i in range(2)]
        sqs = [sb(f"n_sq{i}", [128, KC, TT], BF16) for i in range(2)]
        rstd = [sb(f"n_rstd{i}", [128, TT], F32) for i in range(2)]
        ps_n = [psum(f"n_psn{i}") for i in range(2)]
        xv = K.xres.rearrange("(c p) t -> p c t", p=128)
        ov = K.out.rearrange("(c p) t -> p c t", p=128)
        def load_x(tt):
            xb = xt[tt % 4]
            S.dma(S.q_sp, [], [xb], lambda e: e.dma_start(out=xb[:], in_=xv[:, :, tt_sl(tt)]))
        for tt in range(4):
            load_x(tt)
        for tt in range(NTT):
            xb, ob = xt[tt % 4], ot[tt % 2]
            rs = rstd[tt % 2]
            emit_rmsnorm_tile(K, (sqs[tt % 2], ps_n[tt % 2], rs), xb, None, None)
            for c in range(KC):
                S.op(S.dve, [xb, rs, K.nf], [ob],
                     lambda c=c, xb=xb, rs=rs, ob=ob: nc.vector.scalar_tensor_tensor(
                         out=ob[:, c, :], in0=xb[:, c, :], scalar=K.nf[:, c:c + 1], in1=rs[:], op0=ALU.mult, op1=ALU.mult))
            S.dma(S.q_act, [ob], [], lambda e, ob=ob, tt=tt: e.dma_start(out=ov[:, :, tt_sl(tt)], in_=ob[:]))
            if tt + 4 < NTT:
                load_x(tt + 4)
    S.barrier()


PHASES = ("p1", "attn_a", "attn_b", "s5", "merge", "ffn_up", "ffn_down")


def build_program():
    nc = bass.Bass("TRN2", target_bir_lowering=False)
    K = Ctx()
    K.nc = nc
    K.S = Sched(nc)
    S = K.S

    def din(name, shape, dt=F32):
        return nc.dram_tensor(name, shape, dt, kind="ExternalInput").ap()

    def dscr(name, shape, dt):
        kind = "ExternalOutput" if name in DEBUG else "Internal"
        return nc.dram_tensor(name, shape, dt, kind=kind).ap()

    K.xT = din("xT", [D, SEQ])
    K.w_in = din("w_in", [DEPTH, D, INW])
    K.pp_d = din("pp", [DEPTH, 128, PPW])
    K.nf_d = din("nf", [128, 8])
    K.consts_d = din("consts", [128, 896])
    K.bmat = din("bmat", [DEPTH, 2, 128, 16, 128])
    K.cmat = din("cmat", [DEPTH, 2, 128, 16, 128])
    K.w_glu = din("w_glu", [DEPTH, 512, 512])
    K.w_branch_a = din("w_branch_a", [DEPTH, 512, D])
    K.w_branch_b = din("w_branch_b", [DEPTH, 256, D])
    K.w_branch_c = din("w_branch_c", [DEPTH, 512, D])
    K.w_out = din("w_out", [DEPTH, D, D])
    K.w_up = din("w_up", [DEPTH, D, 2 * FFN])
    K.w_down = din("w_down", [DEPTH, FFN, D])
    K.out = nc.dram_tensor("outT", [D, SEQ], F32, kind="ExternalOutput").ap()
    K.projT = dscr("projT", [INW, SEQ], BF16)
    K.va_tok = dscr("va_tok", [SEQ, 128], BF16)
    K.vb_tok = dscr("vb_tok", [SEQ, 256], BF16)
    K.ymix = dscr("ymix", [1280, SEQ], BF16)
    K.xres = dscr("xres", [D, SEQ], F32)
    K.gatedT = dscr("gatedT", [FFN, SEQ], BF16)
    K.dbg = {}
    if "dbg" in DEBUG:
        K.dbgbuf = nc.dram_tensor("dbgbuf", [20, 128, 512], F32, kind="ExternalOutput").ap()
    if "h" in DEBUG:
        K.dbg["h"] = nc.dram_tensor("dbg_h", [D, SEQ], BF16, kind="ExternalOutput").ap()

    K.PP_NMIX = PP_NMIX
    K.ones_f = Buf(nc.alloc_sbuf_tensor("ones_f", [128, 128], F32))
    K.ones_b = Buf(nc.alloc_sbuf_tensor("ones_b", [128, 128], BF16))
    K.eps_col = Buf(nc.alloc_sbuf_tensor("eps_col", [128, 1], F32))
    K.cb16 = Buf(nc.alloc_sbuf_tensor("cb16", [128, 512], BF16))
    K.maskc4 = Buf(nc.alloc_sbuf_tensor("maskc4", [128, 512], BF16))
    K.maskpA4 = Buf(nc.alloc_sbuf_tensor("maskpA4", [128, 512], BF16))
    K.nf = Buf(nc.alloc_sbuf_tensor("nf_sb", [128, 8], F32))
    K.pp = [Buf(nc.alloc_sbuf_tensor(f"pp_sb{l}", [128, PPW], F32)) for l in range(DEPTH)]
    S.op(S.dve, [], [K.ones_f], lambda: nc.vector.memset(K.ones_f[:], 1.0))
    S.op(S.dve, [], [K.ones_b], lambda: nc.vector.memset(K.ones_b[:], 1.0))
    S.op(S.dve, [], [K.eps_col], lambda: nc.vector.memset(K.eps_col[:], EPS))
    S.dma(S.q_pool, [], [K.cb16], lambda e: e.dma_start(out=K.cb16[:], in_=K.consts_d[:, 0:512]))
    K.m01 = Buf(nc.alloc_sbuf_tensor("m01", [128, 256], BF16))
    S.dma(S.q_pool, [], [K.m01], lambda e: e.dma_start(out=K.m01[:], in_=K.consts_d[:, 512:768]))
    S.dma(S.q_sp, [], [K.nf], lambda e: e.dma_start(out=K.nf[:], in_=K.nf_d[:, :]))
    for l in range(DEPTH):
        S.dma(S.q_sp, [], [K.pp[l]], lambda e, l=l: e.dma_start(out=K.pp[l][:], in_=K.pp_d[l]))
    K.m01pA = Buf(nc.alloc_sbuf_tensor("m01pA", [128, 128], BF16))
    S.dma(S.q_pool, [], [K.m01pA], lambda e: e.dma_start(out=K.m01pA[:], in_=K.consts_d[:, 768:896]))
    K.m01c4 = Buf(nc.alloc_sbuf_tensor("m01c4", [128, 512], BF16))
    K.m01pA4 = Buf(nc.alloc_sbuf_tensor("m01pA4", [128, 512], BF16))
    for h in range(4):
        S.op(S.dve, [K.m01], [K.m01c4], lambda h=h: nc.vector.tensor_copy(out=K.m01c4[:, h * 128:(h + 1) * 128], in_=K.m01[:, 0:128]))
        S.op(S.dve, [K.m01pA], [K.m01pA4], lambda h=h: nc.vector.tensor_copy(out=K.m01pA4[:, h * 128:(h + 1) * 128], in_=K.m01pA[:]))
    for h in range(4):
        S.op(S.dve, [K.cb16], [K.maskc4], lambda h=h: nc.vector.tensor_copy(out=K.maskc4[:, h * 128:(h + 1) * 128], in_=K.cb16[:, 128:256]))
        S.op(S.dve, [K.cb16], [K.maskpA4], lambda h=h: nc.vector.tensor_copy(out=K.maskpA4[:, h * 128:(h + 1) * 128], in_=K.cb16[:, 256:384]))

    fns = {"p1": phase_p1, "attn_a": phase_attn_a, "attn_b": phase_attn_b, "s5": phase_s5, "merge": phase_merge,
           "ffn_up": phase_ffn_up, "ffn_down": phase_ffn_down}
    done = False
    for l in range(DEPTH):
        x_src = K.xT if l == 0 else K.xres
        for ph in PHASES:
            if ONLY is not None and (l, ph) not in ONLY:
                continue
            if ph in ("p1", "merge"):
                fns[ph](K, l, x_src)
            else:
                fns[ph](K, l)
            if STOP_AFTER is not None and (l, ph) == tuple(STOP_AFTER):
                done = True
                break
        if done:
            break
    if not done and ONLY is None:
        phase_final(K)
    S.barrier()
    return nc, K


ONLY = None


def prep_inputs(inputs):
    f = lambda k: np.asarray(inputs[k], dtype=np.float32)
    x = f("x")
    common = {}
    for k in ("w_in", "w_glu", "w_branch_a", "w_branch_b", "w_branch_c", "w_out", "w_up", "w_down"):
        common[k] = np.ascontiguousarray(f(k))
    pp = np.zeros((DEPTH, 128, PPW), np.float32)
    bmat = np.zeros((DEPTH, 2, 128, 16, 128), np.float32)
    cmat = np.zeros((DEPTH, 2, 128, 16, 128), np.float32)
    for l in range(DEPTH):
        pp[l, :, PP_NMIX:PP_NMIX + 8] = f("norm_mix")[l].reshape(8, 128).T
        pp[l, :, PP_NFFN:PP_NFFN + 8] = f("norm_ffn")[l].reshape(8, 128).T
        cw = f("conv_w")[l]
        pp[l, :, PP_CW:PP_CW + 132] = cw.reshape(3, 44, 128).transpose(2, 1, 0).reshape(128, 132)
        pp[l, :, PP_CB:PP_CB + 44] = f("conv_b")[l].reshape(44, 128).T
        pp[l, :, PP_BGLU:PP_BGLU + 4] = f("b_glu")[l].reshape(4, 128).T
        pp[l, :, PP_SSMD:PP_SSMD + 4] = f("ssm_d")[l].reshape(4, 128).T
        pp[l, :, PP_SINK:PP_SINK + 8] = np.broadcast_to(f("attn_sinks")[l][None, :], (128, 8))
        pp[l, :, PP_LR:PP_LR + 16] = f("ssm_lambda_re")[l].reshape(16, 128).T
        pp[l, :, PP_LI:PP_LI + 16] = f("ssm_lambda_im")[l].reshape(16, 128).T
        pp[l, :, PP_LDT:PP_LDT + 16] = np.repeat(f("ssm_log_dt")[l], 64).reshape(16, 128).T
        bre, bim = f("ssm_b_re")[l], f("ssm_b_im")[l]
        cre, cim = f("ssm_c_re")[l], f("ssm_c_im")[l]
        for g in range(32):
            j = g // 2
            ks = slice((g % 8) * 16, (g % 8) * 16 + 16)
            ms = slice((g % 2) * 64, (g % 2) * 64 + 64)
            bmat[l, 0, ks, j, ms] = bre[g].T
            bmat[l, 1, ks, j, ms] = bim[g].T
            cmat[l, 0, ms, j, ks] = cre[g].T
            cmat[l, 1, ms, j, ks] = cim[g].T
    common["pp"] = pp
    common["bmat"] = bmat
    common["cmat"] = cmat
    common["nf"] = np.ascontiguousarray(f("norm_final").reshape(8, 128).T)
    k_idx = np.arange(128)[:, None]
    q_idx = np.arange(128)[None, :]
    consts = np.zeros((128, 896), np.float32)
    consts[:, 768:896] = np.where(k_idx >= q_idx + 1, 1.0, 0.0)
    consts[:, 512:640] = np.where(k_idx <= q_idx, 1.0, 0.0)
    consts[:, 640:768] = np.where(k_idx >= q_idx, 1.0, 0.0)
    consts[:, 0:128] = np.eye(128, dtype=np.float32)
    consts[:, 128:256] = np.where(k_idx <= q_idx, 0.0, NEG)
    consts[:, 256:384] = np.where(k_idx >= q_idx + 1, 0.0, NEG)
    consts[:, 384:512] = np.where(k_idx >= q_idx, 0.0, NEG)
    common["consts"] = consts
    in_maps = []
    for b in range(NCORES):
        m = dict(common)
        m["xT"] = np.ascontiguousarray(x[b].T)
        in_maps.append(m)
    return in_maps


def kernel(**inputs):
    nc, K = build_program()
    in_maps = prep_inputs(inputs)
    res = run_bass_kernel_spmd(nc, in_maps, core_ids=list(range(NCORES)))
    outs = [np.asarray(r["outT"]).T for r in res.results]
    return np.ascontiguousarray(np.stack(outs, axis=0).astype(np.float32))
```
